# Optimizing a Trainium2 kernel written in Bass

```python
import math
import jax, jax.numpy as jnp
from jax import lax
import numpy as np

D_MODEL = 1024
BATCH = 16
SEQ = 256
DEPTH = 4
DEC_BATCH = 4
DEC_SEQ = 1024
PAST_LEN = 256

GRID_W = 64
HEAD_DIM = 64
BRANCH_W = 512
N_BRANCH = 3
SSD_HEADS = 8
SSD_HEAD_DIM = 64
SSD_INNER = SSD_HEADS * SSD_HEAD_DIM
SSD_GROUPS = 2
SSD_STATE = 64
SSD_BC = SSD_GROUPS * SSD_STATE
SSD_CHUNK = 128
CONV_W = 5
CONV_CH = SSD_INNER + 2 * SSD_BC
NA_HEADS = 8
NA_ROWS = 8
NA_COLS = 16
NA_SLAB = 2 * NA_COLS
GQA_HEADS = 8
GQA_KV_HEADS = 2
ROPE_THETA = 10000.0
Q_BLOCK = 128
PEER_HEADS = 8
PEER_KEYS = 128
PEER_EXPERTS = PEER_KEYS * PEER_KEYS
PEER_QDIM = 256
PEER_TOPK = 16
PEER_TOK_BLOCK = 128
DN_ALPHA = (2 * DEPTH) ** 0.25
DN_BETA = (8 * DEPTH) ** -0.25
EPS = 1e-6
IN_SIZES = (SSD_INNER, SSD_INNER, SSD_BC, SSD_BC, 2 * SSD_HEADS,
            NA_HEADS * HEAD_DIM, NA_HEADS * HEAD_DIM, NA_HEADS * HEAD_DIM,
            GQA_HEADS * HEAD_DIM, GQA_KV_HEADS * HEAD_DIM, GQA_KV_HEADS * HEAD_DIM,
            N_BRANCH * D_MODEL)
IN_COLS = sum(IN_SIZES)

kernel_name = 'hybrid_dit_ssd_na_gqa_peer_step'


def layer_norm(x, g=None, b=None):
    xf = x.astype(jnp.float32)
    mu = jnp.mean(xf, -1, keepdims=True)
    var = jnp.mean(jnp.square(xf - mu), -1, keepdims=True)
    y = ((xf - mu) * lax.rsqrt(var + EPS)).astype(x.dtype)
    if g is not None:
        y = y * g + b
    return y


def rms_norm(x, g):
    xf = x.astype(jnp.float32)
    y = xf * lax.rsqrt(jnp.mean(xf * xf, -1, keepdims=True) + EPS)
    return y.astype(x.dtype) * g


def adaln(cond, w, b):
    m = jax.nn.silu(cond) @ w + b
    return jnp.split(m[..., None, :], 6, axis=-1)


def modulate(x, shift, scale):
    return layer_norm(x) * (1 + scale) + shift


def dwconv(x, w, b):
    y = lax.conv_general_dilated(x, w[:, None, :].astype(x.dtype), window_strides=(1,),
                                 padding=[(CONV_W // 2, CONV_W // 2)],
                                 dimension_numbers=('NWC', 'WIO', 'NWC'),
                                 feature_group_count=x.shape[-1])
    return jax.nn.silu(y + b)


def segsum(a):
    cs = jnp.cumsum(a, axis=-1)
    diff = cs[..., :, None] - cs[..., None, :]
    n = a.shape[-1]
    return jnp.where(jnp.tril(jnp.ones((n, n), bool)), diff, -jnp.inf)


def ssd_scan(x, dt, A, B, C, h0):
    b, L, H, P = x.shape
    N = B.shape[-1]
    nc = L // SSD_CHUNK
    Q = SSD_CHUNK
    xd = (x * dt[..., None]).reshape(b, nc, Q, H, P)
    a = (dt * A).reshape(b, nc, Q, H).transpose(0, 3, 1, 2)
    Bc = B.reshape(b, nc, Q, H, N)
    Cc = C.reshape(b, nc, Q, H, N)
    a_cs = jnp.cumsum(a, axis=-1)
    Lmat = jnp.exp(segsum(a))
    cb = jnp.einsum('bclhn,bcshn->bhcls', Cc, Bc)
    y_diag = jnp.einsum('bhcls,bcshp->bclhp', cb * Lmat, xd)
    decay_states = jnp.exp(a_cs[..., -1:] - a_cs)
    states = jnp.einsum('bclhn,bhcl,bclhp->bchpn', Bc, decay_states, xd)
    states = jnp.concatenate([h0[:, None], states], axis=1)
    chunk_a = jnp.pad(a_cs[..., -1], ((0, 0), (0, 0), (1, 0)))
    decay_chunk = jnp.exp(segsum(chunk_a))
    new_states = jnp.einsum('bhzc,bchpn->bzhpn', decay_chunk, states)
    states, final = new_states[:, :-1], new_states[:, -1]
    y_off = jnp.einsum('bclhn,bchpn,bhcl->bclhp', Cc, states, jnp.exp(a_cs))
    return (y_diag + y_off).reshape(b, L, H, P), final


def ssd_mixer(z, xbc, dt_raw, a_log, dt_bias, d_skip, norm_g, h0):
    b, L, _ = z.shape
    xs, Bm, Cm = jnp.split(xbc.astype(jnp.float32), [SSD_INNER, SSD_INNER + SSD_BC], axis=-1)
    xh = xs.reshape(b, L, SSD_HEADS, SSD_HEAD_DIM)
    rep = SSD_HEADS // SSD_GROUPS
    Bh = jnp.repeat(Bm.reshape(b, L, SSD_GROUPS, SSD_STATE), rep, axis=2)
    Ch = jnp.repeat(Cm.reshape(b, L, SSD_GROUPS, SSD_STATE), rep, axis=2)
    dt = jax.nn.softplus(dt_raw.astype(jnp.float32).reshape(b, L, 2, SSD_HEADS) + dt_bias.astype(jnp.float32))
    A = -jnp.exp(a_log.astype(jnp.float32))
    h0 = h0.astype(jnp.float32)
    y_f, s_f = ssd_scan(xh, dt[:, :, 0], A[0], Bh, Ch, h0[:, 0])
    fl = lambda t: jnp.flip(t, axis=1)
    y_b, s_b = ssd_scan(fl(xh), fl(dt[:, :, 1]), A[1], fl(Bh), fl(Ch), h0[:, 1])
    y = y_f + fl(y_b) + d_skip.astype(jnp.float32)[:, None] * xh
    y = y.reshape(b, L, SSD_INNER) * jax.nn.silu(z.astype(jnp.float32))
    y = rms_norm(y, norm_g)
    return y.astype(z.dtype), jnp.stack([s_f, s_b], axis=1).astype(z.dtype)


def blocked_attention(q, k, v):
    b, Lq, H, d = q.shape
    kvh = k.shape[2]
    g = H // kvh
    nb = Lq // Q_BLOCK
    qb = q.reshape(b, nb, Q_BLOCK, kvh, g, d).transpose(1, 0, 2, 3, 4, 5)
    scale = d ** -0.5

    def one_block(qi):
        s = jnp.einsum('bqkgd,bskd->bkgqs', qi, k).astype(jnp.float32) * scale
        p = jax.nn.softmax(s, axis=-1).astype(v.dtype)
        return jnp.einsum('bkgqs,bskd->bqkgd', p, v)

    o = lax.map(one_block, qb)
    return o.transpose(1, 0, 2, 3, 4, 5).reshape(b, Lq, H * d)


def axial_rope(x):
    L = x.shape[1]
    t = jnp.arange(L)
    row = (t // GRID_W).astype(jnp.float32)
    col = (t % GRID_W).astype(jnp.float32)
    half = x.shape[-1] // 2
    inv = 1.0 / (ROPE_THETA ** (jnp.arange(0, half, 2, dtype=jnp.float32) / half))

    def rot(xh, pos):
        ang = pos[:, None] * inv
        cos = jnp.cos(ang)[None, :, None, :]
        sin = jnp.sin(ang)[None, :, None, :]
        x1, x2 = jnp.split(xh, 2, axis=-1)
        return jnp.concatenate([x1 * cos - x2 * sin, x2 * cos + x1 * sin], axis=-1)

    xr, xc = jnp.split(x.astype(jnp.float32), 2, axis=-1)
    return jnp.concatenate([rot(xr, row), rot(xc, col)], axis=-1).astype(x.dtype)


def neighbourhood_attention(q, k, v, ctx_k, ctx_v, rpb):
    b, L, H, d = q.shape
    rows = L // GRID_W
    kr = min(NA_ROWS, rows)
    ncb = GRID_W // NA_COLS
    r = jnp.arange(rows)
    key_rows = jnp.clip(r - kr // 2, 0, rows - kr)[:, None] + jnp.arange(kr)
    slab0 = jnp.clip(jnp.arange(ncb) * NA_COLS - NA_COLS // 2, 0, GRID_W - NA_SLAB)
    key_cols = slab0[:, None] + jnp.arange(NA_SLAB)
    qcol = jnp.arange(ncb)[:, None] * NA_COLS + jnp.arange(NA_COLS)
    qstart = jnp.clip(qcol - NA_COLS // 2, 0, GRID_W - NA_COLS)
    kc = key_cols[:, None, :]
    valid = (kc >= qstart[..., None]) & (kc < qstart[..., None] + NA_COLS)
    dr_idx = key_rows - r[:, None] + NA_ROWS - 1
    dc_idx = jnp.clip(kc - qcol[..., None] + NA_COLS - 1, 0, 2 * NA_COLS - 2)
    bias = rpb[:, dr_idx[:, None, None, :, None], dc_idx[None, :, :, None, :]].astype(jnp.float32)
    kg = k.reshape(b, rows, GRID_W, H, d)
    vg = v.reshape(b, rows, GRID_W, H, d)
    ri = key_rows[:, None, :, None]
    ci = key_cols[None, :, None, :]
    kb = kg[:, ri, ci]
    vb = vg[:, ri, ci].reshape(b, rows, ncb, kr * NA_SLAB, H, d)
    qb = q.reshape(b, rows, ncb, NA_COLS, H, d)
    scale = d ** -0.5
    s_loc = jnp.einsum('brcqhd,brcijhd->bhrcqij', qb, kb).astype(jnp.float32) * scale + bias[None]
    s_loc = jnp.where(valid[:, :, None, :], s_loc, -jnp.inf)
    n_loc = kr * NA_SLAB
    s_loc = s_loc.reshape(b, H, rows, ncb, NA_COLS, n_loc)
    s_ctx = jnp.einsum('brcqhd,bshd->bhrcqs', qb, ctx_k).astype(jnp.float32) * scale
    p = jax.nn.softmax(jnp.concatenate([s_loc, s_ctx], axis=-1), axis=-1).astype(v.dtype)
    o = (jnp.einsum('bhrcqn,brcnhd->brcqhd', p[..., :n_loc], vb)
         + jnp.einsum('bhrcqs,bshd->brcqhd', p[..., n_loc:], ctx_v))
    return o.reshape(b, L, H * d)


def token_mixer(h, w_in, conv_w, conv_b, a_log, dt_bias, d_skip, ssd_g, rpb, qn_g, kn_g, w_branch, w_out, ctx):
    b, L, _ = h.shape
    p = h @ w_in
    offs = np.cumsum(IN_SIZES)[:-1].tolist()
    z, xs, Bm, Cm, dt_raw, na_q, na_k, na_v, g_q, g_k, g_v, gate_raw = jnp.split(p, offs, axis=-1)
    xbc = dwconv(jnp.concatenate([xs, Bm, Cm], axis=-1), conv_w, conv_b)
    na_q = na_q.reshape(b, L, NA_HEADS, HEAD_DIM)
    na_k = na_k.reshape(b, L, NA_HEADS, HEAD_DIM)
    na_v = na_v.reshape(b, L, NA_HEADS, HEAD_DIM)
    g_q = rms_norm(g_q.reshape(b, L, GQA_HEADS, HEAD_DIM), qn_g)
    g_k = rms_norm(g_k.reshape(b, L, GQA_KV_HEADS, HEAD_DIM), kn_g)
    g_v = g_v.reshape(b, L, GQA_KV_HEADS, HEAD_DIM)
    if ctx is None:
        h0 = jnp.zeros((b, 2, SSD_HEADS, SSD_HEAD_DIM, SSD_STATE), h.dtype)
        y_a, ssd_state = ssd_mixer(z, xbc, dt_raw, a_log, dt_bias, d_skip, ssd_g, h0)
        y_b = blocked_attention(na_q, na_k, na_v)
        y_c = blocked_attention(g_q, g_k, g_v)
        new_ctx = (na_k, na_v, g_k, g_v, ssd_state)
    else:
        c_na_k, c_na_v, c_g_k, c_g_v, c_ssd = ctx
        y_a, _ = ssd_mixer(z, xbc, dt_raw, a_log, dt_bias, d_skip, ssd_g, c_ssd)
        y_b = neighbourhood_attention(na_q, na_k, na_v, c_na_k, c_na_v, rpb)
        k_all = jnp.concatenate([c_g_k, axial_rope(g_k)], axis=1)
        v_all = jnp.concatenate([c_g_v, g_v], axis=1)
        y_c = blocked_attention(axial_rope(g_q), k_all, v_all)
        new_ctx = None
    ys = jnp.stack([y_a, y_b, y_c], axis=2)
    gates = jax.nn.sigmoid(gate_raw.reshape(b, L, N_BRANCH, D_MODEL))
    merged = jnp.sum(gates * jnp.einsum('blie,ied->blid', ys, w_branch), axis=2)
    return merged @ w_out, new_ctx


def peer(h, wq, keys, u_tab, v_tab):
    b, L, D = h.shape
    T = b * L
    t = h.reshape(T, D)
    q = (t @ wq).reshape(T, PEER_HEADS, 2, PEER_QDIM // 2)
    s = jnp.einsum('thpd,hpkd->thpk', q, keys).astype(jnp.float32)
    sv, si = lax.top_k(s, PEER_TOPK)
    cand = (sv[:, :, 0, :, None] + sv[:, :, 1, None, :]).reshape(T, PEER_HEADS, PEER_TOPK * PEER_TOPK)
    cv, ci = lax.top_k(cand, PEER_TOPK)
    e = (jnp.take_along_axis(si[:, :, 0], ci // PEER_TOPK, axis=-1) * PEER_KEYS
         + jnp.take_along_axis(si[:, :, 1], ci % PEER_TOPK, axis=-1))
    g = jax.nn.softmax(cv, axis=-1).astype(h.dtype)
    nb = T // PEER_TOK_BLOCK
    E = PEER_HEADS * PEER_TOPK

    def expert_block(args):
        tb, eb, gb = args
        act = jax.nn.gelu(jnp.einsum('td,ted->te', tb, u_tab[eb])) * gb
        return jnp.einsum('te,ted->td', act, v_tab[eb])

    out = lax.map(expert_block, (t.reshape(nb, PEER_TOK_BLOCK, D),
                                 e.reshape(nb, PEER_TOK_BLOCK, E),
                                 g.reshape(nb, PEER_TOK_BLOCK, E)))
    return out.reshape(b, L, D)


def setup_inputs(seed: int = 0) -> dict:
    key = jax.random.key(seed)
    ks = jax.random.split(key, 32)
    D = D_MODEL

    def nrm(k, shape, s):
        return jax.random.normal(k, shape, jnp.float32) * s

    a_log = jnp.log(jax.random.uniform(ks[14], (DEPTH, 2, SSD_HEADS), jnp.float32, 1.0, 16.0))
    dt0 = jnp.exp(jax.random.uniform(ks[15], (DEPTH, 2, SSD_HEADS), jnp.float32, math.log(1e-3), math.log(1e-1)))
    dt_bias = dt0 + jnp.log(-jnp.expm1(-dt0))
    return {
        'x_prompt': nrm(ks[0], (BATCH, SEQ, D), 1.0),
        'x_sample': nrm(ks[1], (DEC_BATCH, DEC_SEQ, D), 1.0),
        'cache_na_k': nrm(ks[2], (DEC_BATCH, DEPTH, PAST_LEN, NA_HEADS, HEAD_DIM), 1.0),
        'cache_na_v': nrm(ks[3], (DEC_BATCH, DEPTH, PAST_LEN, NA_HEADS, HEAD_DIM), 1.0),
        'cache_gqa_k': nrm(ks[4], (DEC_BATCH, DEPTH, PAST_LEN, GQA_KV_HEADS, HEAD_DIM), 1.0),
        'cache_gqa_v': nrm(ks[5], (DEC_BATCH, DEPTH, PAST_LEN, GQA_KV_HEADS, HEAD_DIM), 1.0),
        'state_ssd': nrm(ks[6], (DEC_BATCH, DEPTH, 2, SSD_HEADS, SSD_HEAD_DIM, SSD_STATE), 0.1),
        'c': nrm(ks[7], (DEC_BATCH, D), 1.0),
        'c_ctx': nrm(ks[8], (D,), 1.0),
        'w_mod': nrm(ks[9], (DEPTH, D, 6 * D), 0.5 * D ** -0.5),
        'b_mod': nrm(ks[10], (DEPTH, 6 * D), 0.01),
        'w_in': nrm(ks[11], (DEPTH, D, IN_COLS), D ** -0.5),
        'conv_w': nrm(ks[12], (DEPTH, CONV_W, CONV_CH), CONV_W ** -0.5),
        'conv_b': nrm(ks[13], (DEPTH, CONV_CH), 0.01),
        'ssd_a_log': a_log,
        'ssd_dt_bias': dt_bias,
        'ssd_d': 1.0 + nrm(ks[16], (DEPTH, SSD_HEADS), 0.1),
        'ssd_norm_g': 1.0 + nrm(ks[17], (DEPTH, SSD_INNER), 0.02),
        'na_rpb': nrm(ks[18], (DEPTH, NA_HEADS, 2 * NA_ROWS - 1, 2 * NA_COLS - 1), 0.02),
        'gqa_q_norm': 1.0 + nrm(ks[19], (DEPTH, HEAD_DIM), 0.02),
        'gqa_k_norm': 1.0 + nrm(ks[20], (DEPTH, HEAD_DIM), 0.02),
        'w_branch': nrm(ks[21], (DEPTH, N_BRANCH, BRANCH_W, D), BRANCH_W ** -0.5),
        'w_out': nrm(ks[22], (DEPTH, D, D), DN_BETA * D ** -0.5),
        'ln1_g': 1.0 + nrm(ks[23], (DEPTH, D), 0.02),
        'ln1_b': nrm(ks[24], (DEPTH, D), 0.01),
        'ln2_g': 1.0 + nrm(ks[25], (DEPTH, D), 0.02),
        'ln2_b': nrm(ks[26], (DEPTH, D), 0.01),
        'peer_wq': nrm(ks[27], (DEPTH, D, PEER_HEADS * PEER_QDIM), D ** -0.5),
        'peer_keys': nrm(ks[28], (DEPTH, PEER_HEADS, 2, PEER_KEYS, PEER_QDIM // 2), (PEER_QDIM // 2) ** -0.5),
        'peer_u': nrm(ks[29], (DEPTH, PEER_EXPERTS, D), D ** -0.5),
        'peer_v': nrm(ks[30], (DEPTH, PEER_EXPERTS, D), DN_BETA * PEER_HEADS ** -0.5),
    }


def reference(x_prompt, x_sample, cache_na_k, cache_na_v, cache_gqa_k, cache_gqa_v, state_ssd, c, c_ctx,
              w_mod, b_mod, w_in, conv_w, conv_b, ssd_a_log, ssd_dt_bias, ssd_d, ssd_norm_g, na_rpb,
              gqa_q_norm, gqa_k_norm, w_branch, w_out, ln1_g, ln1_b, ln2_g, ln2_b,
              peer_wq, peer_keys, peer_u, peer_v):
    def layer(x, l, cond, ctx):
        sh1, sc1, g1, sh2, sc2, g2 = adaln(cond, w_mod[l], b_mod[l])
        y, new_ctx = token_mixer(modulate(x, sh1, sc1), w_in[l], conv_w[l], conv_b[l], ssd_a_log[l],
                                 ssd_dt_bias[l], ssd_d[l], ssd_norm_g[l], na_rpb[l], gqa_q_norm[l],
                                 gqa_k_norm[l], w_branch[l], w_out[l], ctx)
        x = layer_norm(DN_ALPHA * x + g1 * y, ln1_g[l], ln1_b[l])
        f = peer(modulate(x, sh2, sc2), peer_wq[l], peer_keys[l], peer_u[l], peer_v[l])
        x = layer_norm(DN_ALPHA * x + g2 * f, ln2_g[l], ln2_b[l])
        return x, new_ctx

    xp = x_prompt
    per_layer = []
    for l in range(DEPTH):
        xp, ctx_l = layer(xp, l, c_ctx, None)
        per_layer.append(ctx_l)
    new_cache_na_k = jnp.stack([t[0] for t in per_layer], axis=1)
    new_cache_na_v = jnp.stack([t[1] for t in per_layer], axis=1)
    new_cache_gqa_k = jnp.stack([t[2] for t in per_layer], axis=1)
    new_cache_gqa_v = jnp.stack([t[3] for t in per_layer], axis=1)
    new_state_ssd = jnp.stack([t[4] for t in per_layer], axis=1)

    xs = x_sample
    for l in range(DEPTH):
        ctx_l = (cache_na_k[:, l], cache_na_v[:, l], cache_gqa_k[:, l], cache_gqa_v[:, l], state_ssd[:, l])
        xs, _ = layer(xs, l, c, ctx_l)

    return (xp, xs, new_cache_na_k, new_cache_na_v, new_cache_gqa_k, new_cache_gqa_v, new_state_ssd)
```

```python
import contextlib
import sys
import numpy as np
import concourse.bass as bass
import concourse.mybir as mybir
from concourse.alu_op_type import AluOpType as ALU
from concourse.bass_utils import run_bass_kernel_spmd

F32 = mybir.dt.float32
BF16 = mybir.dt.bfloat16
U32 = mybir.dt.uint32
I32 = mybir.dt.int32
AF = mybir.ActivationFunctionType
AX = mybir.AxisListType

D = 1024
NT = 8
DEPTH = 4
EPS = 1e-6
DN_ALPHA = (2 * DEPTH) ** 0.25
NEG = -30000.0
IN_COLS = 6672
OFF = dict(z=0, xs=512, B=1024, C=1152, dt=1280, naq=1296, nak=1808, nav=2320, gq=2832, gk=3344, gv=3472, gate=3600)


class Res:
    __slots__ = ("name", "last_w", "reads", "ch")

    def __init__(self, name):
        self.name = name
        self.last_w = None
        self.reads = {}
        self.ch = None


class Prog:
    ENG = ("pe", "act", "dve", "pool", "sp")

    def __init__(self, nc):
        self.nc = nc
        self.stack = contextlib.ExitStack()
        self.ops = {e: [] for e in self.ENG}
        self.cnt = {}
        self.sems = {}
        self.seen = {e: {} for e in self.ENG}
        self.pend = {e: {} for e in self.ENG}
        self.out_events = []
        self.nres = 0
        for e in self.ENG:
            self._sem("E_" + e)

    def _sem(self, key):
        if key not in self.sems:
            self.sems[key] = self.stack.enter_context(self.nc.semaphore("s" + key))
            self.cnt[key] = 0
        return self.sems[key]

    def sb(self, name, shape, dtype):
        return self.stack.enter_context(self.nc.sbuf_tensor("sb_" + name, list(shape), dtype))

    def ps(self, name, shape, dtype=F32):
        return self.stack.enter_context(self.nc.psum_tensor("ps_" + name, list(shape), dtype))

    def res(self, name=None):
        self.nres += 1
        return Res("%s_%d" % (name or "r", self.nres))

    def _deps(self, e, reads, writes):
        deps = dict(self.pend[e])
        self.pend[e] = {}

        def add(ev):
            if ev is None:
                return
            k, v = ev
            if deps.get(k, 0) < v:
                deps[k] = v

        for r in reads:
            add(r.last_w)
        for w in writes:
            add(w.last_w)
            for k, v in w.reads.items():
                add((k, v))
        waits = []
        seen = self.seen[e]
        own = "E_" + e
        for k, v in deps.items():
            if e == "pe" and k == own:
                continue
            if seen.get(k, 0) >= v:
                continue
            seen[k] = v
            waits.append((k, v))
        return waits

    def _commit(self, ev, reads, writes):
        k, v = ev
        for r in reads:
            if r.reads.get(k, 0) < v:
                r.reads[k] = v
        for w in writes:
            w.last_w = ev
            w.reads = {}

    def op(self, e, fn, reads=(), writes=()):
        waits = self._deps(e, reads, writes)
        key = "E_" + e
        self.cnt[key] += 1
        ev = (key, self.cnt[key])
        self.ops[e].append((waits, fn, key, 1, self._where()))
        self._commit(ev, reads, writes)
        return ev

    @staticmethod
    def _where():
        f = sys._getframe(2)
        out = []
        while f is not None and len(out) < 5:
            out.append(f.f_lineno)
            f = f.f_back
        return out

    def dma(self, q, fn, reads=(), writes=(), is_output=False):
        tgt = writes[0] if writes else reads[0]
        if tgt.ch is None:
            tgt.ch = "D_" + tgt.name.rsplit("_", 1)[0]
            self._sem(tgt.ch)
        key = tgt.ch
        waits = self._deps(q, reads, writes)
        self.cnt[key] += 16
        ev = (key, self.cnt[key])
        self.ops[q].append((waits, fn, key, 16, self._where()))
        self._commit(ev, reads, writes)
        if is_output:
            self.out_events.append(ev)
        return ev

    def barrier(self):
        snap = dict(self.cnt)
        for e in self.ENG:
            for k, v in snap.items():
                if v > 0 and self.pend[e].get(k, 0) < v:
                    self.pend[e][k] = v

    def finish(self):
        finals = {}
        for k, v in self.out_events:
            finals[k] = max(finals.get(k, 0), v)
        for e in self.ENG:
            if e != "sp" and self.cnt["E_" + e] > 0:
                finals["E_" + e] = self.cnt["E_" + e]
        final_waits = list(finals.items())
        nc, sems, ops = self.nc, self.sems, self.ops

        def replay(eng, lst):
            for waits, fn, key, inc, where in lst:
                for k, v in waits:
                    eng.wait_ge(sems[k], v)
                try:
                    fn(eng).then_inc(sems[key], inc)
                except Exception:
                    print("FAILED OP created at lines", where)
                    raise

        with nc.Block() as block:
            @block.tensor
            def _(eng):
                replay(eng, ops["pe"])

            @block.scalar
            def _(eng):
                replay(eng, ops["act"])

            @block.vector
            def _(eng):
                replay(eng, ops["dve"])

            @block.gpsimd
            def _(eng):
                replay(eng, ops["pool"])

            @block.sync
            def _(eng):
                replay(eng, ops["sp"])
                for k, v in final_waits:
                    eng.wait_ge(sems[k], v)
        self.stack.close()


class Buf:
    __slots__ = ("t", "r")

    def __init__(self, t, r):
        self.t = t
        self.r = r

    def __getitem__(self, idx):
        return self.t[idx]


class _Stop(Exception):
    pass


def build(nc, NL=DEPTH, taps=None, stop=None):
    P = Prog(nc)
    taps = taps or []
    tap_out = {}

    def din(name, shape, dt=F32):
        return nc.dram_tensor(name, list(shape), dt, kind="ExternalInput").ap()

    def dout(name, shape, dt=F32):
        return nc.dram_tensor(name, list(shape), dt, kind="ExternalOutput").ap()

    I = dict(
        x0=din("x0", [1024, D]), cond_rep=din("cond_rep", [128, 8, 128]),
        w_mod=din("w_mod", [NL, D, 6144]), b_mod=din("b_mod", [NL, 1, 6144]),
        w_in=din("w_in", [NL, D, IN_COLS]), cw=din("cw", [NL, 128, 6, 5]), cb=din("cb", [NL, 128, 6]),
        alog=din("alog", [NL, 128, 16]), dtb=din("dtb", [NL, 128, 16]),
        dsk=din("dsk", [NL, 128, 4]), ng=din("ng", [NL, 128, 4]),
        tt=din("tt", [NL, 8, 128, 30 * 64]), qn=din("qn", [NL, 128, 64]), kn=din("kn", [NL, 128, 64]),
        lnp=din("lnp", [NL, 4, 128, D]),
        w_branch=din("w_branch", [NL, 3, 512, D]), w_out=din("w_out", [NL, D, D]),
        wq=din("wq", [NL, D, 2048]), keysT=din("keysT", [NL, 16, 128, 128]),
        uv=[din("uv%d" % i, [16384, 2 * D]) for i in range(NL)],
        c_nakT=din("c_nakT", [NL, 8, 64, 256]), c_nav=din("c_nav", [NL, 256, 512]),
        c_gkT=din("c_gkT", [NL, 2, 64, 256]), c_gv=din("c_gv", [NL, 256, 128]),
        h0=din("h0", [NL, 2, 128, 256]),
        qaug=din("qaug", [16, 1024]), kaug_na=din("kaug_na", [16, 1280]), kaug_g=din("kaug_g", [16, 1280]),
        cosT=din("cosT", [128, 8, 64]), sinT=din("sinT", [128, 8, 64]),
        keep=din("keep", [128, 16]), cmask=din("cmask", [128, 1]),
    )
    O = dict(
        y=dout("y", [1024, D]), o_nak=dout("o_nak", [NL, 1024, 512]), o_nav=dout("o_nav", [NL, 1024, 512]),
        o_gk=dout("o_gk", [NL, 1024, 128]), o_gv=dout("o_gv", [NL, 1024, 128]),
        o_st=dout("o_st", [NL, 4, 2, 128, 256]),
    )

    def B(name, shape, dt):
        return Buf(P.sb(name, shape, dt), P.res(name))

    def OP(e, fn, reads=(), writes=()):
        ws = [b.r for b in writes] + [b.r for b in reads if b.r.name.startswith("bank")]
        P.op(e, fn, [b.r for b in reads], ws)

    def DMA(q, out, in_, reads=(), writes=(), is_output=False):
        P.dma(q, lambda e: e.dma_start(out=out, in_=in_), [b.r for b in reads], [b.r for b in writes], is_output)

    def mm(out, lhsT, rhs, start, stop, reads, writes):
        OP("pe", lambda e: e.matmul(out, lhsT=lhsT, rhs=rhs, start=start, stop=stop), reads, writes)

    def tp(out, in_, ident, reads, writes):
        OP("pe", lambda e: e.transpose(out=out, in_=in_, identity=ident), reads, writes)

    def act(out, in_, func, reads, writes, bias=None, scale=None, accum_out=None):
        kw = {}
        if bias is not None:
            kw["bias"] = bias
        if scale is not None:
            kw["scale"] = scale
        if accum_out is not None:
            kw["accum_out"] = accum_out
        OP("act", lambda e: e.activation(out=out, in_=in_, func=func, **kw), reads, writes)

    def tt_(eng, out, in0, in1, op, reads, writes):
        OP(eng, lambda e: e.tensor_tensor(out=out, in0=in0, in1=in1, op=op), reads, writes)

    def ts_(eng, out, in0, s1, s2, op0, op1, reads, writes):
        if op1 is None:
            OP(eng, lambda e: e.tensor_scalar(out=out, in0=in0, scalar1=s1, scalar2=None, op0=op0), reads, writes)
        else:
            OP(eng, lambda e: e.tensor_scalar(out=out, in0=in0, scalar1=s1, scalar2=s2, op0=op0, op1=op1), reads, writes)

    def stt(out, in0, scalar, in1, op0, op1, reads, writes, accum_out=None):
        if accum_out is None:
            OP("dve", lambda e: e.scalar_tensor_tensor(out=out, in0=in0, scalar=scalar, in1=in1, op0=op0, op1=op1),
               reads, writes)
        else:
            OP("dve", lambda e: e.scalar_tensor_tensor(out=out, in0=in0, scalar=scalar, in1=in1, op0=op0, op1=op1,
                                                       accum_out=accum_out), reads, writes)

    def cp(eng, out, in_, reads, writes):
        if eng == "act":
            OP("act", lambda e: e.copy(out=out, in_=in_), reads, writes)
        else:
            OP(eng, lambda e: e.tensor_copy(out=out, in_=in_), reads, writes)

    def tap(name, buf, ap, shape, dt=F32):
        if name in taps:
            d = dout("tap_" + name, shape, dt)
            tap_out[name] = d
            DMA("sp", d, ap, reads=[buf], is_output=True)

    banks = [Buf(P.ps("bank%d" % i, [128, 512], F32), P.res("bank%d" % i)) for i in range(8)]
    rot = [0, 0]

    def sbank():
        b = banks[rot[0] % 4]
        rot[0] += 1
        return b

    def hbank():
        b = banks[4 + rot[1] % 4]
        rot[1] += 1
        return b

    x = B("x", [128, NT, D], F32)
    mrep = B("mrep", [128, 6144], F32)
    hT = B("hT", [128, 8, 1024], BF16)
    lnp = B("lnp", [128, 2, D], F32)
    ident = B("ident", [128, 128], BF16)
    identf = B("identf", [128, 128], F32)
    TRIf = B("TRIf", [128, 128], F32)
    TRIb = B("TRIb", [128, 128], F32)
    maskf = B("maskf", [128, 128], F32)
    maskb = B("maskb", [128, 128], F32)
    onesb = B("onesb", [128, 128], BF16)
    onesf = B("onesf", [128, 128], F32)
    iota16 = B("iota16", [128, 16], F32)
    condS = B("condS", [128, 8, 128], BF16)
    small = B("small", [128, 64], F32)
    keep = B("keep", [128, 16], F32)
    cmask = B("cmask", [128, 1], F32)
    cosT = B("cosT", [128, 8, 64], F32)
    sinT = B("sinT", [128, 8, 64], F32)
    NW = 3
    wbuf = [B("wbuf%d" % i, [128, 8, 512], BF16) for i in range(NW)]
    wrot = [0]

    uvb = [nc.dram_tensor("uvb%d" % i, [16384, 2 * D], BF16, kind="Internal").ap() for i in range(NL)]
    uvbuf = [Buf(None, P.res("uvb")) for i in range(NL)]
    NSTG = 3
    stgc = [B("stgc%d" % i, [128, 2 * D], BF16) for i in range(NSTG)]
    conv = {"l": 0, "i": 128}

    def start_conv(l_):
        conv["l"] = l_
        conv["i"] = 0

    def pump(n):
        while n > 0 and conv["i"] < 128:
            i_, l_ = conv["i"], conv["l"]
            st = stgc[i_ % NSTG]
            DMA("pool", st[:], I["uv"][l_][i_ * 128:(i_ + 1) * 128, :], writes=[st])
            DMA("sp", uvb[l_][i_ * 128:(i_ + 1) * 128, :], st[:], reads=[st], writes=[uvbuf[l_]])
            conv["i"] += 1
            n -= 1

    ARENA_W = 20480
    arena = P.sb("arena", [128, ARENA_W], F32)
    ar = {"off": 0, "n": 0}

    def stage_begin():
        P.barrier()
        ar["off"] = 0

    def A(name, shape, dt, parts=128):
        free = int(np.prod(shape[1:]))
        words = (free * (2 if dt == BF16 else 4) + 3) // 4
        words = (words + 7) // 8 * 8
        o = ar["off"]
        assert o + words <= ARENA_W, ("arena overflow", name, o, words)
        ar["off"] = o + words
        ar["n"] += 1
        v = arena[0:shape[0], o:o + words]
        if dt != F32:
            v = v.bitcast(dt)
        v = v[:, 0:free]
        if len(shape) == 3:
            v = v.rearrange("p (a b) -> p a b", b=shape[2])
        elif len(shape) == 4:
            v = v.rearrange("p (a b c) -> p a b c", b=shape[2], c=shape[3])
        elif len(shape) == 5:
            v = v.rearrange("p (a b c d) -> p a b c d", b=shape[2], c=shape[3], d=shape[4])
        return Buf(v, P.res(name))

    def load_w(src, ncols):
        wb = wbuf[wrot[0] % NW]
        wrot[0] += 1
        DMA("pool", wb[:, :, 0:ncols], src.rearrange("(k p) c -> p k c", p=128), writes=[wb])
        pump(2)
        return wb

    OP("pool", lambda e: e.memset(onesf[:], 1.0), writes=[onesf])
    OP("pool", lambda e: e.memset(onesb[:], 1.0), writes=[onesb])
    OP("pool", lambda e: e.memset(identf[:], 1.0), writes=[identf])
    OP("pool", lambda e: e.affine_select(out=identf[:], in_=identf[:], pattern=[[-1, 128]], compare_op=ALU.is_equal,
                                         fill=0.0, base=0, channel_multiplier=1), reads=[identf], writes=[identf])
    cp("pool", ident[:], identf[:], [identf], [ident])
    OP("pool", lambda e: e.affine_select(out=TRIf[:], in_=onesf[:], pattern=[[1, 128]], compare_op=ALU.is_ge,
                                         fill=0.0, base=0, channel_multiplier=-1), reads=[onesf], writes=[TRIf])
    OP("pool", lambda e: e.affine_select(out=TRIb[:], in_=onesf[:], pattern=[[-1, 128]], compare_op=ALU.is_ge,
                                         fill=0.0, base=0, channel_multiplier=1), reads=[onesf], writes=[TRIb])
    OP("pool", lambda e: e.memset(maskf[:], 0.0), writes=[maskf])
    OP("pool", lambda e: e.memset(maskb[:], 0.0), writes=[maskb])
    OP("pool", lambda e: e.affine_select(out=maskf[:], in_=maskf[:], pattern=[[1, 128]], compare_op=ALU.is_ge,
                                         fill=NEG, base=0, channel_multiplier=-1), reads=[maskf], writes=[maskf])
    OP("pool", lambda e: e.affine_select(out=maskb[:], in_=maskb[:], pattern=[[-1, 128]], compare_op=ALU.is_ge,
                                         fill=NEG, base=0, channel_multiplier=1), reads=[maskb], writes=[maskb])
    OP("pool", lambda e: e.iota(iota16[:], pattern=[[1, 16]], base=0, channel_multiplier=0,
                                allow_small_or_imprecise_dtypes=True), writes=[iota16])
    DMA("sp", keep[:], I["keep"], writes=[keep])
    DMA("sp", cmask[:], I["cmask"], writes=[cmask])
    DMA("sp", cosT[:], I["cosT"], writes=[cosT])
    DMA("sp", sinT[:], I["sinT"], writes=[sinT])
    condf = A("condf", [128, 8, 128], F32)
    DMA("sp", condf[:], I["cond_rep"], writes=[condf])
    act(condS[:], condf[:], AF.Silu, [condf], [condS])
    for t in range(NT):
        DMA("sp", x[:, t, :], I["x0"][t * 128:(t + 1) * 128, :], writes=[x])

    def ln_stats(src_ap, srcbuf):
        OP("dve", lambda e: e.bn_stats(out=small[:, 0:6], in_=src_ap[:, 0:512]), [srcbuf], [small])
        OP("dve", lambda e: e.bn_stats(out=small[:, 6:12], in_=src_ap[:, 512:1024]), [srcbuf, small], [small])
        OP("dve", lambda e: e.bn_aggr(out=small[:, 12:14], in_=small[:, 0:12]), [small], [small])
        act(small[:, 14:15], small[:, 13:14], AF.Ln, [small], [small], bias=EPS)
        act(small[:, 15:16], small[:, 14:15], AF.Exp, [small], [small], scale=-0.5)
        return small[:, 12:13], small[:, 15:16]

    def ln_modulate(shift_off, scale_off, tmpA, hb, h2tok=None):
        for t in range(NT):
            mean, rstd = ln_stats(x[:, t, :], x)
            ts_("dve", tmpA[:], x[:, t, :], mean, rstd, ALU.subtract, ALU.mult, [x, small], [tmpA])
            tt_("dve", tmpA[:], tmpA[:], mrep[:, scale_off:scale_off + D], ALU.mult, [tmpA, mrep], [tmpA])
            dst = hb if h2tok is None else h2tok
            dst_ap = hb[:] if h2tok is None else h2tok[:, t, :]
            tt_("dve", dst_ap, tmpA[:], mrep[:, shift_off:shift_off + D], ALU.add, [tmpA, mrep], [dst])
            bk = sbank()
            bkb = bk[:].bitcast(BF16)
            for k in range(8):
                tp(bkb[:, k * 128:(k + 1) * 128], dst_ap[:, k * 128:(k + 1) * 128], ident[:], [dst, ident], [bk])
            cp("act", hT[:, :, t * 128:(t + 1) * 128], bkb.rearrange("p (k t) -> p k t", t=128), [bk], [hT])

    def ln_affine(pre, gi, tmpbuf):
        pass

    def chk(k):
        if stop == k:
            raise _Stop()

    def layer_body(l):
        stage_begin()
        start_conv(l)
        bmod = A("bmod", [1, 6144], BF16)
        DMA("pool", bmod[:], I["b_mod"][l], writes=[bmod])
        for cc in range(12):
            wb = load_w(I["w_mod"][l][:, cc * 512:(cc + 1) * 512], 512)
            bk = sbank()
            for k in range(8):
                mm(bk[:], condS[:, k, :], wb[:, k, :], k == 0, False, [condS, wb], [bk])
            mm(bk[:], onesb[0:1, :], bmod[0:1, cc * 512:(cc + 1) * 512], False, True, [onesb, bmod], [bk])
            if cc in (2, 3, 8, 9):
                act(mrep[:, cc * 512:(cc + 1) * 512], bk[:], AF.Identity, [bk], [mrep], bias=1.0)
            else:
                cp("act", mrep[:, cc * 512:(cc + 1) * 512], bk[:], [bk], [mrep])
        if l == 0:
            tap("mrep", mrep, mrep[:], [128, 6144])
        chk(1)

        tmpA = A("tmpA", [128, D], F32)
        hb = A("hb", [128, D], BF16)
        ln_modulate(0, 1024, tmpA, hb)
        if l == 0:
            tap("hT", hT, hT[:], [128, 8, 1024], BF16)
        chk(2)

        stage_begin()
        yA = A("yA", [128, 4, 1024], BF16)
        yB = A("yB", [128, 4, 1024], BF16)
        yC = A("yC", [128, 4, 1024], BF16)
        mix_off = ar["off"]
        xc = A("xc", [128, 6, 1024], BF16)
        x_tok = A("x_tok", [128, 8, 512], BF16)
        B_tok = A("B_tok", [128, 8, 128], BF16)
        Sin = A("Sin", [128, 8, 2, 256], BF16)
        a_all = A("a_all", [128, 8, 16], F32)
        lndt = A("lndt", [128, 8, 16], F32)
        csT = A("csT", [128, 8, 32], F32)
        w_all = A("w_all", [128, 8, 16], F32)
        decG = A("decG", [128, 8, 2, 4], F32)
        Arep = A("Arep", [128, 16], F32)
        dtbr = A("dtbr", [128, 16], F32)
        cwb = A("cwb", [128, 6, 5], F32)
        cbb = A("cbb", [128, 6], F32)
        dskb = A("dskb", [128, 4], F32)
        ngb = A("ngb", [128, 4], F32)
        S = A("S", [128, 2, 256], F32)
        sc16 = A("sc16", [128, 4, 16], F32)
        ssd_off = ar["off"]
        xp = A("xp", [128, 6, 4, 260], BF16)
        acc = A("acc", [128, 4, 256], F32)

        DMA("sp", Arep[:], I["alog"][l], writes=[Arep])
        DMA("sp", dtbr[:], I["dtb"][l], writes=[dtbr])
        DMA("sp", cwb[:], I["cw"][l], writes=[cwb])
        DMA("sp", cbb[:], I["cb"][l], writes=[cbb])
        DMA("sp", dskb[:], I["dsk"][l], writes=[dskb])
        DMA("sp", ngb[:], I["ng"][l], writes=[ngb])
        DMA("sp", S[:], I["h0"][l].rearrange("d p f -> p d f"), writes=[S])
        act(Arep[:], Arep[:], AF.Exp, [Arep], [Arep])
        ts_("dve", Arep[:], Arep[:], -1.0, None, ALU.mult, None, [Arep], [Arep])

        OP("pool", lambda e: e.memset(xp[:], 0.0), writes=[xp])
        w1 = load_w(I["w_in"][l][:, OFF["xs"]:OFF["xs"] + 512], 512)
        w2 = load_w(I["w_in"][l][:, OFF["B"]:OFF["B"] + 272], 272)
        for k in range(6):
            wsrc, c0 = (w1, k * 128) if k < 4 else (w2, (k - 4) * 128)
            for half in range(2):
                bk = sbank()
                for kd in range(8):
                    mm(bk[:], wsrc[:, kd, c0:c0 + 128], hT[:, kd, half * 512:(half + 1) * 512], kd == 0, kd == 7,
                       [wsrc, hT], [bk])
                cp("act", xp[:, k, 2 * half:2 * half + 2, 2:258], bk[:].rearrange("p (s t) -> p s t", t=256), [bk], [xp])
        chk(21)
        for t in range(NT):
            bk = sbank()
            for kd in range(8):
                mm(bk[:, 0:16], hT[:, kd, t * 128:(t + 1) * 128], w2[:, kd, 256:272], kd == 0, kd == 7, [hT, w2], [bk])
            tt_("dve", sc16[:, 0, :], bk[:, 0:16], dtbr[:], ALU.add, [bk, dtbr], [sc16])
            act(sc16[:, 1, :], sc16[:, 0, :], AF.Exp, [sc16], [sc16])
            act(sc16[:, 2, :], sc16[:, 1, :], AF.Ln, [sc16], [sc16], bias=1.0)
            act(lndt[:, t, :], sc16[:, 2, :], AF.Ln, [sc16], [lndt])
            tt_("dve", a_all[:, t, :], sc16[:, 2, :], Arep[:], ALU.mult, [sc16, Arep], [a_all])
        chk(22)
        ts_("dve", xp[:, :, 1:4, 0:2], xp[:, :, 0:3, 256:258], cmask[:, 0:1], None, ALU.mult, None, [xp, cmask], [xp])
        ts_("dve", xp[:, :, 0:3, 258:260], xp[:, :, 1:4, 2:4], cmask[:, 0:1], None, ALU.mult, None, [xp, cmask], [xp])
        for k in range(6):
            ts_("dve", acc[:], xp[:, k, :, 0:256], cwb[:, k, 0:1], None, ALU.mult, None, [xp, cwb], [acc])
            for j in range(1, 5):
                stt(acc[:], xp[:, k, :, j:j + 256], cwb[:, k, j:j + 1], acc[:], ALU.mult, ALU.add, [xp, cwb, acc], [acc])
            act(xc[:, k, :].rearrange("p (s t) -> p s t", t=256), acc[:], AF.Silu, [acc, cbb], [xc], bias=cbb[:, k:k + 1])
        chk(23)
        for t in range(NT):
            bk = sbank()
            bkb = bk[:].bitcast(BF16)
            for k in range(5):
                tp(bkb[:, k * 128:(k + 1) * 128], xc[:, k, t * 128:(t + 1) * 128], ident[:], [xc, ident], [bk])
            cp("act", x_tok[:, t, :], bkb[:, 0:512], [bk], [x_tok])
            chk(24)
            cp("act", B_tok[:, t, :], bkb[:, 512:640], [bk], [B_tok])

        chk(3)
        P.barrier()
        ar["off"] = ssd_off
        for c in range(8):
            bk = sbank()
            mm(bk[:, 0:8], TRIf[:], a_all[:, c, 0:8], True, True, [TRIf, a_all], [bk])
            mm(bk[:, 8:16], TRIb[:], a_all[:, c, 8:16], True, True, [TRIb, a_all], [bk])
            mm(bk[:, 16:32], onesf[:], a_all[:, c, :], True, True, [onesf, a_all], [bk])
            cp("act", csT[:, c, :], bk[:, 0:32], [bk], [csT])
            act(sc16[:, 0, :], bk[:, 16:32], AF.Exp, [bk], [sc16])
            d4 = sc16[:, 0, :].rearrange("p (d g h) -> p d g h", d=2, g=2)
            cp("dve", decG[0:64, c, :, :], d4[0:64, :, 0, :], [sc16], [decG])
            cp("dve", decG[64:128, c, :, :], d4[64:128, :, 1, :], [sc16], [decG])
            tt_("dve", sc16[:, 1, :], csT[:, c, 16:32], csT[:, c, 0:16], ALU.subtract, [csT], [sc16])
            tt_("dve", sc16[:, 1, :], sc16[:, 1, :], lndt[:, c, :], ALU.add, [sc16, lndt], [sc16])
            act(w_all[:, c, :], sc16[:, 1, :], AF.Exp, [sc16], [w_all])

        xws = [A("xw%d" % i, [128, 512], BF16) for i in range(2)]
        Sd = [Buf(S.t[:, d_, :], P.res("Sdir")) for d_ in range(2)]
        P.barrier()
        for step in range(8):
            for d_ in range(2):
                c = step if d_ == 0 else 7 - step
                Sx, xw = Sd[d_], xws[d_]
                if step > 0:
                    ts_("dve", Sx[:], Sx[:], keep[:, d_ * 8 + c:d_ * 8 + c + 1], None, ALU.mult, None, [Sx, keep], [Sx])
                cp("act", Sin[:, c, d_, :], Sx[:], [Sx], [Sin])
                tt_("dve", xw[:].rearrange("p (h q) -> p h q", q=64),
                    x_tok[:, c, :].rearrange("p (h q) -> p h q", q=64),
                    w_all[:, c, d_ * 8:d_ * 8 + 8].unsqueeze(2).broadcast_to([128, 8, 64]), ALU.mult,
                    [x_tok, w_all], [xw])
                bk = sbank()
                for g in range(2):
                    mm(bk[g * 64:(g + 1) * 64, 0:256], B_tok[:, c, g * 64:(g + 1) * 64], xw[:, g * 256:(g + 1) * 256],
                       True, True, [B_tok, xw], [bk])
                tt_("dve", Sx[:].rearrange("p (h q) -> p h q", q=64),
                    Sx[:].rearrange("p (h q) -> p h q", q=64),
                    decG[:, c, d_, :].unsqueeze(2).broadcast_to([128, 4, 64]), ALU.mult, [Sx, decG], [Sx])
                tt_("dve", Sx[:], Sx[:], bk[:, 0:256], ALU.add, [Sx, bk], [Sx])
                if (d_ == 0 and c % 2 == 1) or (d_ == 1 and c % 2 == 0):
                    DMA("sp", O["o_st"][l, c // 2, d_], Sx[:], reads=[Sx], is_output=True)

        chk(4)
        yph_off = ar["off"]
        arep = A("arep", [128, 16, 128], F32)
        cbT = A("cbT", [128, 2, 128], F32)
        LwD = A("LwD", [128, 2, 4, 128], F32)
        MT = A("MT", [128, 2, 128], BF16)
        Eall = A("Eall", [128, 16, 128], BF16)
        Cp = A("Cp", [128, 4, 128], BF16)
        ygate_off = ar["off"]
        ygf = yA
        for c in range(8):
            pump(2)
            cs_ = slice(c * 128, (c + 1) * 128)
            cp("dve", arep[:], a_all[:, c, :].unsqueeze(2).broadcast_to([128, 16, 128]), [a_all], [arep])
            reps = []
            for q in range(4):
                bk = hbank()
                for jj in range(4):
                    j = q * 4 + jj
                    mm(bk[:, jj * 128:(jj + 1) * 128], arep[:, j, :], (TRIf if j < 8 else TRIb)[:], True, True,
                       [arep, TRIf, TRIb], [bk])
                reps.append(bk)
                act(Eall[:, q * 4:(q + 1) * 4, :], bk[:].rearrange("p (j l) -> p j l", l=128), AF.Exp, [bk], [Eall])
            for g in range(2):
                bk = sbank()
                mm(bk[:, 0:128], xc[g * 64:(g + 1) * 64, 4, cs_], xc[g * 64:(g + 1) * 64, 5, cs_], True, True, [xc], [bk])
                cp("act", cbT[:, g, :], bk[:, 0:128], [bk], [cbT])
            ybks = {}

            def S12(h):
                par = h % 2
                rf = reps[h // 4][:, (h % 4) * 128:(h % 4 + 1) * 128]
                rb = reps[2 + h // 4][:, (h % 4) * 128:(h % 4 + 1) * 128]
                stt(LwD[:, par, 0, :], rf, csT[:, c, h:h + 1], maskf[:], ALU.subtract, ALU.add,
                    [reps[h // 4], csT, maskf], [LwD])
                stt(LwD[:, par, 2, :], rb, csT[:, c, 8 + h:9 + h], maskb[:], ALU.subtract, ALU.add,
                    [reps[2 + h // 4], csT, maskb], [LwD])
                act(LwD[:, par, 1, :], LwD[:, par, 0, :], AF.Exp, [LwD, lndt], [LwD], bias=lndt[:, c, h:h + 1])
                act(LwD[:, par, 3, :], LwD[:, par, 2, :], AF.Exp, [LwD, lndt], [LwD], bias=lndt[:, c, 8 + h:9 + h])

            def S34(h):
                par = h % 2
                pr, hh = h // 2, h % 2
                g, h4 = h // 4, h % 4
                gs = slice(g * 64, (g + 1) * 64)
                tt_("dve", LwD[:, par, 1, :], LwD[:, par, 1, :], LwD[:, par, 3, :], ALU.add, [LwD], [LwD])
                tt_("dve", MT[:, par, :], LwD[:, par, 1, :], cbT[:, g, :], ALU.mult, [LwD, cbT], [MT])
                tt_("dve", Cp[gs, par * 2, :], xc[gs, 5, cs_], Eall[gs, h, :], ALU.mult, [xc, Eall], [Cp])
                tt_("dve", Cp[gs, par * 2 + 1, :], xc[gs, 5, cs_], Eall[gs, 8 + h, :], ALU.mult, [xc, Eall], [Cp])
                if hh == 0:
                    ybks[pr] = sbank()
                ybk = ybks[pr]
                yo = ybk[hh * 64:(hh + 1) * 64, 0:128]
                mm(yo, x_tok[:, c, h * 64:(h + 1) * 64], MT[:, par, :], True, False, [x_tok, MT], [ybk])
                mm(yo, Sin[gs, c, 0, h4 * 64:(h4 + 1) * 64], Cp[gs, par * 2, :], False, False, [Sin, Cp], [ybk])
                mm(yo, Sin[gs, c, 1, h4 * 64:(h4 + 1) * 64], Cp[gs, par * 2 + 1, :], False, True, [Sin, Cp], [ybk])
                if hh == 1:
                    stt(ygf[:, pr, cs_], xc[:, pr, cs_], dskb[:, pr:pr + 1], ybk[:, 0:128], ALU.mult, ALU.add,
                        [xc, dskb, ybk], [ygf])

            for k in range(9):
                if k < 8:
                    S12(k)
                if k >= 1:
                    S34(k - 1)
        P.barrier()
        ar["off"] = yph_off
        yg32 = A("yg32", [128, 512], F32)
        ysq = A("ysq", [128, 512], F32)
        chk(5)
        wz = load_w(I["w_in"][l][:, OFF["z"]:OFF["z"] + 512], 512)
        for half in range(2):
            hs = slice(half * 512, (half + 1) * 512)
            sbk = hbank()
            for pr in range(4):
                bk = sbank()
                for kd in range(8):
                    mm(bk[:], wz[:, kd, pr * 128:(pr + 1) * 128], hT[:, kd, hs], kd == 0, kd == 7, [wz, hT], [bk])
                act(yg32[:], bk[:], AF.Silu, [bk], [yg32])
                tt_("dve", ygf[:, pr, hs], ygf[:, pr, hs], yg32[:], ALU.mult, [ygf, yg32], [ygf])
                tt_("dve", ysq[:], ygf[:, pr, hs], ygf[:, pr, hs], ALU.mult, [ygf], [ysq])
                mm(sbk[:], onesf[:], ysq[:], pr == 0, pr == 3, [onesf, ysq], [sbk])
            act(yg32[:], sbk[:], AF.Ln, [sbk], [yg32], bias=EPS, scale=1.0 / 512)
            act(yg32[:], yg32[:], AF.Exp, [yg32], [yg32], scale=-0.5)
            for pr in range(4):
                stt(yA[:, pr, hs], ygf[:, pr, hs], ngb[:, pr:pr + 1], yg32[:], ALU.mult, ALU.mult, [ygf, ngb, yg32], [ygf])

        chk(6)
        for br in range(2):
            P.barrier()
            ar["off"] = mix_off
            is_na = br == 0
            if br == 1:
                chk(7)
            nq = 4 if is_na else 8
            nkb = 4 if is_na else 2
            QT = A("QT", [80, nq, 1024], BF16)
            KT = A("KT", [80, nkb, 1280], BF16)
            vw = 512 if is_na else 128
            V_tok = A("V_tok", [128, 8, vw], BF16)
            V_ctx = A("V_ctx", [128, 2, vw], BF16)
            stg = A("stg", [128, 2, 512], F32)
            PT = A("PT", [128, 3, 512], BF16)
            tmpS = A("tmpS", [128, 2, 512], F32)
            rs = A("rs", [128, 512], F32)
            yout = yB if is_na else yC
            ka = I["kaug_na"] if is_na else I["kaug_g"]
            ck = I["c_nakT"] if is_na else I["c_gkT"]
            cv = I["c_nav"] if is_na else I["c_gv"]
            DMA("pool", V_ctx[:], cv[l].rearrange("(j p) c -> p j c", p=128), writes=[V_ctx])

            def fill_aug(h0_, n_):
                DMA("pool", QT[64:80, :, :], I["qaug"].unsqueeze(1).broadcast_to([16, nq, 1024]), writes=[QT])
                DMA("pool", KT[64:80, :, :], ka.unsqueeze(1).broadcast_to([16, nkb, 1280]), writes=[KT])
                DMA("pool", KT[0:64, :, 1024:1280], ck[l, h0_:h0_ + n_].rearrange("h d k -> d h k"), writes=[KT])

            if is_na:
                ttb = [A("ttb%d" % i, [128, 30 * 64], BF16) for i in range(2)]
                wqn = load_w(I["w_in"][l][:, OFF["naq"]:OFF["naq"] + 512], 512)
                wkn = load_w(I["w_in"][l][:, OFF["nak"]:OFF["nak"] + 512], 512)
                wvn = load_w(I["w_in"][l][:, OFF["nav"]:OFF["nav"] + 512], 512)
                for t in range(NT):
                    for (wsrc, dram, keepbf) in ((wkn, O["o_nak"], False), (wvn, O["o_nav"], True)):
                        bk = sbank()
                        for kd in range(8):
                            mm(bk[:], hT[:, kd, t * 128:(t + 1) * 128], wsrc[:, kd, :], kd == 0, kd == 7, [hT, wsrc], [bk])
                        si_ = 0 if not keepbf else 1
                        cp("act", stg[:, si_, :], bk[:], [bk], [stg])
                        if keepbf:
                            cp("dve", V_tok[:, t, :], bk[:], [bk], [V_tok])
                        DMA("sp", dram[l, t * 128:(t + 1) * 128, :], stg[:, si_, :], reads=[stg], is_output=True)
                groups = [list(range(0, 4)), list(range(4, 8))]
            else:
                fill_aug(0, 2)
                wqg = load_w(I["w_in"][l][:, OFF["gq"]:OFF["gq"] + 512], 512)
                wkv = load_w(I["w_in"][l][:, OFF["gk"]:OFF["gk"] + 256], 256)
                gq = A("gq", [128, 512], F32)
                gsq = A("gsq", [128, 512], F32)
                gr = A("gr", [128, 512], F32)
                qr = A("qr", [128, 512], BF16)
                g8 = A("g8", [128, 4, 8], F32)
                qnb = A("qnb", [128, 64], F32)
                knb = A("knb", [128, 64], F32)
                DMA("sp", qnb[:], I["qn"][l], writes=[qnb])
                DMA("sp", knb[:], I["kn"][l], writes=[knb])

                def norm_rope(src_ap, src_bk, nh, gain, t, cache_dram):
                    n = nh * 64
                    v3 = lambda ap: ap.rearrange("p (h q) -> p h q", q=64)
                    cp("act", gq[:, 0:n], src_ap, [src_bk], [gq])
                    act(gsq[:, 0:n], src_ap, AF.Square, [src_bk], [gsq])
                    OP("dve", lambda e: e.tensor_reduce(out=g8[:, 0, 0:nh], in_=v3(gsq[:, 0:n]), axis=AX.X, op=ALU.add),
                       [gsq], [g8])
                    act(g8[:, 1, 0:nh], g8[:, 0, 0:nh], AF.Ln, [g8], [g8], bias=EPS, scale=1.0 / 64)
                    act(g8[:, 2, 0:nh], g8[:, 1, 0:nh], AF.Exp, [g8], [g8], scale=-0.5)
                    tt_("dve", v3(gq[:, 0:n]), v3(gq[:, 0:n]), g8[:, 2, 0:nh].unsqueeze(2).broadcast_to([128, nh, 64]),
                        ALU.mult, [gq, g8], [gq])
                    tt_("dve", v3(gq[:, 0:n]), v3(gq[:, 0:n]), gain[:].unsqueeze(1).broadcast_to([128, nh, 64]),
                        ALU.mult, [gq, gain], [gq])
                    if cache_dram is not None:
                        DMA("sp", cache_dram, gq[:, 0:n], reads=[gq], is_output=True)
                    v5 = lambda ap: ap.rearrange("p (h a b c) -> p h a b c", a=2, b=2, c=16)
                    sn = sinT[:, t, :].rearrange("p (a b c) -> p a b c", a=2, b=2)
                    for b_ in range(2):
                        tt_("dve", v5(gr[:, 0:n])[:, :, :, b_, :], v5(gq[:, 0:n])[:, :, :, 1 - b_, :],
                            sn[:, :, b_, :].unsqueeze(1).broadcast_to([128, nh, 2, 16]), ALU.mult, [gq, sinT], [gr])
                    tt_("dve", v3(gsq[:, 0:n]), v3(gq[:, 0:n]), cosT[:, t, :].unsqueeze(1).broadcast_to([128, nh, 64]),
                        ALU.mult, [gq, cosT], [gsq])
                    tt_("dve", qr[:, 0:n], gsq[:, 0:n], gr[:, 0:n], ALU.add, [gsq, gr], [qr])

                for t in range(NT):
                    ts = slice(t * 128, (t + 1) * 128)
                    bk = sbank()
                    for kd in range(8):
                        mm(bk[:], hT[:, kd, ts], wqg[:, kd, :], kd == 0, kd == 7, [hT, wqg], [bk])
                    norm_rope(bk[:, 0:512], bk, 8, qnb, t, None)
                    bk2 = sbank()
                    b2 = bk2[:].bitcast(BF16)
                    for h in range(8):
                        tp(b2[0:64, h * 128:(h + 1) * 128], qr[:, h * 64:(h + 1) * 64], ident[:], [qr, ident], [bk2])
                    cp("act", QT[0:64, :, ts], b2[0:64, :].rearrange("p (h t) -> p h t", t=128), [bk2], [QT])
                    bk = sbank()
                    for kd in range(8):
                        mm(bk[:, 0:256], hT[:, kd, ts], wkv[:, kd, 0:256], kd == 0, kd == 7, [hT, wkv], [bk])
                    cp("act", stg[:, 1, 0:128], bk[:, 128:256], [bk], [stg])
                    cp("dve", V_tok[:, t, :], bk[:, 128:256], [bk], [V_tok])
                    DMA("sp", O["o_gv"][l, ts, :], stg[:, 1, 0:128], reads=[stg], is_output=True)
                    norm_rope(bk[:, 0:128], bk, 2, knb, t, O["o_gk"][l, ts, :])
                    bk2 = sbank()
                    b2 = bk2[:].bitcast(BF16)
                    for h in range(2):
                        tp(b2[0:64, h * 128:(h + 1) * 128], qr[:, h * 64:(h + 1) * 64], ident[:], [qr, ident], [bk2])
                    cp("act", KT[0:64, 0:2, ts], b2[0:64, 0:256].rearrange("p (h t) -> p h t", t=128), [bk2], [KT])
                groups = [list(range(8))]

            if is_na:
                chk(61)
            else:
                chk(71)
            pcnt = 0
            for heads in groups:
                if is_na:
                    fill_aug(heads[0], 4)
                    for (wsrc, dst) in ((wqn, QT), (wkn, KT)):
                        for hi, h in enumerate(heads):
                            for half in range(2):
                                bk = sbank()
                                for kd in range(8):
                                    mm(bk[0:64, :], wsrc[:, kd, h * 64:(h + 1) * 64], hT[:, kd, half * 512:(half + 1) * 512],
                                       kd == 0, kd == 7, [wsrc, hT], [bk])
                                cp("act" if half == 0 else "dve", dst[0:64, hi, half * 512:(half + 1) * 512], bk[0:64, :],
                                   [bk], [dst])
                steps = [(hi, h, qc, kt) for hi, h in enumerate(heads) for qc in range(2) for kt in range(10)]
                sbk_of = {}
                acc_of = {}

                def issue_S(i):
                    hi, h, qc, kt = steps[i]
                    kvb = hi if is_na else h // 4
                    if is_na and qc == 0 and kt == 0:
                        tb = ttb[h % 2]
                        DMA("pool", tb[:], I["tt"][l, h], writes=[tb])
                    sbk = sbank()
                    mm(sbk[:], KT[0:80, kvb, kt * 128:(kt + 1) * 128], QT[0:80, hi, qc * 512:(qc + 1) * 512], True, True,
                       [KT, QT], [sbk])
                    sbk_of[i] = sbk

                for i in range(min(2, len(steps))):
                    issue_S(i)
                for i, (hi, h, qc, kt) in enumerate(steps):
                    if kt == 0:
                        pump(1)
                    if i + 2 < len(steps):
                        issue_S(i + 2)
                    kv = h if is_na else h // 4
                    ps_ = slice((h % 2) * 64, (h % 2) * 64 + 64)
                    qs = slice(qc * 512, (qc + 1) * 512)
                    if kt == 0:
                        acc_of[(h, qc)] = (hbank(), hbank())
                    ob, sb_ = acc_of[(h, qc)]
                    sbk = sbk_of.pop(i)
                    p2 = i % 3
                    if is_na and kt < 8:
                        tb = ttb[h % 2]
                        e0 = qc * 8 - 2 * kt + 14
                        stt(tmpS[:, i % 2, :], sbk[:], 0.125, tb[:, e0 * 64:(e0 + 8) * 64], ALU.mult, ALU.add,
                            [sbk, tb], [tmpS])
                        act(PT[:, p2, :], tmpS[:, i % 2, :], AF.Exp, [tmpS], [PT])
                    else:
                        act(PT[:, p2, :], sbk[:], AF.Exp, [sbk], [PT], scale=0.125)
                    if kt < 8:
                        vop = V_tok[:, kt, kv * 64:(kv + 1) * 64]
                        vb = V_tok
                    else:
                        vop = V_ctx[:, kt - 8, kv * 64:(kv + 1) * 64]
                        vb = V_ctx
                    mm(ob[ps_, :], vop, PT[:, p2, :], kt == 0, kt == 9, [vb, PT], [ob])
                    mm(sb_[ps_, :], onesb[:, 0:64], PT[:, p2, :], kt == 0, kt == 9, [onesb, PT], [sb_])
                    if kt == 9:
                        act(rs[ps_, :], sb_[ps_, :], AF.Ln, [sb_], [rs])
                        act(rs[ps_, :], rs[ps_, :], AF.Exp, [rs], [rs], scale=-1.0)
                        tt_("dve", yout[ps_, h // 2, qs], ob[ps_, :], rs[ps_, :], ALU.mult, [ob, rs], [yout])

        chk(8)
        P.barrier()
        ar["off"] = mix_off
        mT = A("mT", [128, 8, 1024], BF16)
        Wg = [[A("Wg%d_%d" % (i, j), [128, 8, 128], BF16) for j in range(3)] for i in range(2)]
        Wb = [[A("Wb%d_%d" % (i, j), [128, 4, 128], BF16) for j in range(3)] for i in range(2)]
        sg = A("sg", [128, 3, 512], F32)
        mtmp = A("mtmp", [128, 2, 512], F32)
        ys = (yA, yB, yC)
        for dc in range(8):
            pump(2)
            wg_, wb_ = Wg[dc % 2], Wb[dc % 2]
            for i in range(3):
                c0 = OFF["gate"] + i * 1024 + dc * 128
                DMA("pool", wg_[i][:], I["w_in"][l][:, c0:c0 + 128].rearrange("(k p) c -> p k c", p=128),
                    writes=[wg_[i]])
                DMA("pool", wb_[i][:], I["w_branch"][l, i][:, dc * 128:(dc + 1) * 128].rearrange("(k p) c -> p k c", p=128),
                    writes=[wb_[i]])
            for half in range(2):
                hs = slice(half * 512, (half + 1) * 512)
                for i in range(3):
                    gb = sbank()
                    for kd in range(8):
                        mm(gb[:], wg_[i][:, kd, :], hT[:, kd, hs], kd == 0, kd == 7, [wg_[i], hT], [gb])
                    act(sg[:, i, :], gb[:], AF.Sigmoid, [gb], [sg])
                    pb = sbank()
                    for e4 in range(4):
                        mm(pb[:], wb_[i][:, e4, :], ys[i][:, e4, hs], e4 == 0, e4 == 3, [wb_[i], ys[i]], [pb])
                    if i == 0:
                        tt_("dve", mtmp[:, 0, :], pb[:], sg[:, i, :], ALU.mult, [pb, sg], [mtmp])
                    else:
                        tt_("dve", mtmp[:, 1, :], pb[:], sg[:, i, :], ALU.mult, [pb, sg], [mtmp])
                        if i == 1:
                            tt_("dve", mtmp[:, 0, :], mtmp[:, 0, :], mtmp[:, 1, :], ALU.add, [mtmp], [mtmp])
                        else:
                            tt_("dve", mT[:, dc, hs], mtmp[:, 0, :], mtmp[:, 1, :], ALU.add, [mtmp], [mT])
        wo = [load_w(I["w_out"][l][:, hf * 512:(hf + 1) * 512], 512) for hf in range(2)]
        DMA("sp", lnp[:, 0, :], I["lnp"][l, 0], writes=[lnp])
        DMA("sp", lnp[:, 1, :], I["lnp"][l, 1], writes=[lnp])
        pre = A("pre", [128, D], F32)
        for t in range(NT):
            ts = slice(t * 128, (t + 1) * 128)
            for hf in range(2):
                bk = sbank()
                for dc in range(8):
                    mm(bk[:], mT[:, dc, ts], wo[hf][:, dc, :], dc == 0, dc == 7, [mT, wo[hf]], [bk])
                tt_("dve", pre[:, hf * 512:(hf + 1) * 512], bk[:], mrep[:, 2048 + hf * 512:2048 + (hf + 1) * 512], ALU.mult,
                    [bk, mrep], [pre])
            stt(pre[:], x[:, t, :], DN_ALPHA, pre[:], ALU.mult, ALU.add, [x, pre], [pre])
            mean, rstd = ln_stats(pre[:], pre)
            ts_("dve", pre[:], pre[:], mean, rstd, ALU.subtract, ALU.mult, [pre, small], [pre])
            tt_("dve", pre[:], pre[:], lnp[:, 0, :], ALU.mult, [pre, lnp], [pre])
            tt_("dve", x[:, t, :], pre[:], lnp[:, 1, :], ALU.add, [pre, lnp], [x])
        if l == 0:
            tap("x1", x, x[:], [128, NT, D])

        chk(9)
        stage_begin()
        h2tok = A("h2tok", [128, NT, D], BF16)
        e_idx = A("e_idx", [128, NT, 128], I32)
        g_all = A("g_all", [128, NT, 128], F32)
        asel = A("asel", [128, NT, 128], F32)
        peer_off = ar["off"]
        tmpA = A("tmpA2", [128, D], F32)
        ln_modulate(3072, 4096, tmpA, None, h2tok)
        P.barrier()
        ar["off"] = peer_off
        keysT = A("keysT", [128, 16, 128], BF16)
        qTh = A("qTh", [128, 16, 512], BF16)
        sv = A("sv", [128, 16, 16], F32)
        si = A("si", [128, 16, 16], U32)
        sif = A("sif", [128, 16, 16], F32)
        wk16 = A("wk16", [128, 16, 128], F32)
        wk8 = A("wk8", [128, 8, 256], F32)
        oh = Buf(wk16.t.rearrange("p a b -> p (a b)").rearrange("p (h i j) -> p h i j", h=8, i=16), wk16.r)
        cand = A("cand", [128, 8, 16, 16], F32)
        cvv = A("cvv", [128, 8, 16], F32)
        ci = A("ci", [128, 8, 16], U32)
        cij = A("cij", [128, 2, 8, 16], U32)
        cijf = A("cijf", [128, 2, 8, 16], F32)
        k01 = A("k01", [128, 2, 8, 16], F32)
        g8p = A("g8p", [128, 2, 8], F32)
        DMA("pool", keysT[:], I["keysT"][l].rearrange("h d k -> d h k"), writes=[keysT])
        for half in range(2):
            hs = slice(half * 512, (half + 1) * 512)
            for qb in range(4):
                wqb = load_w(I["wq"][l][:, qb * 512:(qb + 1) * 512], 512)
                for j in range(4):
                    bk = sbank()
                    for kd in range(8):
                        mm(bk[:], wqb[:, kd, j * 128:(j + 1) * 128], hT[:, kd, hs], kd == 0, kd == 7, [wqb, hT], [bk])
                    cp("act", qTh[:, qb * 4 + j, :], bk[:], [bk], [qTh])
            for tl in range(4):
                t = half * 4 + tl
                for grp in range(4):
                    bk = sbank()
                    for j in range(4):
                        hp = grp * 4 + j
                        mm(bk[:, j * 128:(j + 1) * 128], qTh[:, hp, tl * 128:(tl + 1) * 128], keysT[:, hp, :], True, True,
                           [qTh, keysT], [bk])
                    srcs = [(grp * 4 + j, bk[:, j * 128:(j + 1) * 128]) for j in range(4)]
                    for hp, src in srcs:
                        OP("dve", lambda e, src=src, hp=hp: e.max(out=sv[:, hp, 0:8], in_=src), [bk], [sv])
                    for hp, src in srcs:
                        OP("dve", lambda e, src=src, hp=hp: e.max_index(out=si[:, hp, 0:8], in_max=sv[:, hp, 0:8], in_values=src),
                           [bk, sv], [si])
                    for hp, src in srcs:
                        OP("dve", lambda e, src=src, hp=hp: e.match_replace(out=wk16[:, hp, :], in_to_replace=sv[:, hp, 0:8],
                                                                            in_values=src, imm_value=-1e30), [bk, sv], [wk16])
                    for hp, src in srcs:
                        OP("dve", lambda e, hp=hp: e.max(out=sv[:, hp, 8:16], in_=wk16[:, hp, :]), [wk16], [sv])
                    for hp, src in srcs:
                        OP("dve", lambda e, hp=hp: e.max_index(out=si[:, hp, 8:16], in_max=sv[:, hp, 8:16], in_values=wk16[:, hp, :]),
                           [wk16, sv], [si])
                cp("dve", sif[:], si[:], [si], [sif])
                sv4 = sv[:].rearrange("p (h q) k -> p h q k", q=2)
                sif4 = sif[:].rearrange("p (h q) k -> p h q k", q=2)
                tt_("dve", cand[:], sv4[:, :, 0, :].unsqueeze(3).broadcast_to([128, 8, 16, 16]),
                    sv4[:, :, 1, :].unsqueeze(2).broadcast_to([128, 8, 16, 16]), ALU.add, [sv], [cand])
                c2s = [cand[:, h, :, :].rearrange("p a b -> p (a b)") for h in range(8)]
                for h in range(8):
                    OP("dve", lambda e, c2=c2s[h], h=h: e.max(out=cvv[:, h, 0:8], in_=c2), [cand], [cvv])
                for h in range(8):
                    OP("dve", lambda e, c2=c2s[h], h=h: e.max_index(out=ci[:, h, 0:8], in_max=cvv[:, h, 0:8], in_values=c2),
                       [cand, cvv], [ci])
                for h in range(8):
                    OP("dve", lambda e, c2=c2s[h], h=h: e.match_replace(out=wk8[:, h, :], in_to_replace=cvv[:, h, 0:8], in_values=c2,
                                                                         imm_value=-1e30), [cand, cvv], [wk8])
                for h in range(8):
                    OP("dve", lambda e, h=h: e.max(out=cvv[:, h, 8:16], in_=wk8[:, h, :]), [wk8], [cvv])
                for h in range(8):
                    OP("dve", lambda e, h=h: e.max_index(out=ci[:, h, 8:16], in_max=cvv[:, h, 8:16], in_values=wk8[:, h, :]),
                       [wk8, cvv], [ci])
                OP("dve", lambda e: e.tensor_single_scalar(out=cij[:, 0, :, :], in_=ci[:], scalar=4, op=ALU.logical_shift_right),
                   [ci], [cij])
                OP("dve", lambda e: e.tensor_single_scalar(out=cij[:, 1, :, :], in_=ci[:], scalar=15, op=ALU.bitwise_and),
                   [ci], [cij])
                cp("dve", cijf[:], cij[:], [cij], [cijf])
                for q in range(2):
                    tt_("dve", oh[:], cijf[:, q, :, :].unsqueeze(3).broadcast_to([128, 8, 16, 16]),
                        iota16[:].unsqueeze(1).unsqueeze(1).broadcast_to([128, 8, 16, 16]), ALU.is_equal, [cijf, iota16], [oh])
                    tt_("dve", oh[:], oh[:], sif4[:, :, q, :].unsqueeze(2).broadcast_to([128, 8, 16, 16]), ALU.mult,
                        [oh, sif], [oh])
                    OP("dve", lambda e, q=q: e.tensor_reduce(out=k01[:, q, :, :], in_=oh[:], axis=AX.X, op=ALU.add), [oh], [k01])
                stt(k01[:, 0, :, :], k01[:, 0, :, :], 128.0, k01[:, 1, :, :], ALU.mult, ALU.add, [k01], [k01])
                cp("dve", e_idx[:, t, :].rearrange("p (h k) -> p h k", k=16), k01[:, 0, :, :], [k01], [e_idx])
                tt_("dve", cvv[:], cvv[:], cvv[:, :, 0:1].broadcast_to([128, 8, 16]), ALU.subtract, [cvv], [cvv])
                act(cvv[:], cvv[:], AF.Exp, [cvv], [cvv])
                OP("dve", lambda e: e.tensor_reduce(out=g8p[:, 0, :], in_=cvv[:], axis=AX.X, op=ALU.add), [cvv], [g8p])
                OP("dve", lambda e: e.reciprocal(out=g8p[:, 1, :], in_=g8p[:, 0, :]), [g8p], [g8p])
                tt_("dve", g_all[:, t, :].rearrange("p (h k) -> p h k", k=16), cvv[:],
                    g8p[:, 1, :].unsqueeze(2).broadcast_to([128, 8, 16]), ALU.mult, [cvv, g8p], [g_all])
        if l == 0:
            tap("e_idx", e_idx, e_idx[:], [128, NT, 128], I32)
            tap("g_all", g_all, g_all[:], [128, NT, 128])

        chk(11)
        P.barrier()
        ar["off"] = peer_off
        pump(128)
        NG = 10
        gb_ = [A("gbuf%d" % i, [128, 2 * D], BF16) for i in range(NG)]
        junks = [A("junk%d" % i, [128, D], BF16) for i in range(2)]
        dgs = [A("dg%d" % i, [128, 2, 128], BF16) for i in range(4)]
        asg = [A("asg%d" % i, [128, 2], F32) for i in range(8)]
        awg = [A("awg%d" % i, [128, 2], F32) for i in range(8)]
        pre = A("pre2", [128, D], F32)
        DMA("sp", lnp[:, 0, :], I["lnp"][l, 2], writes=[lnp])
        DMA("sp", lnp[:, 1, :], I["lnp"][l, 3], writes=[lnp])

        def finish_v(t, a0, a1):
            for hf, a_ in ((0, a0), (1, a1)):
                tt_("dve", pre[:, hf * 512:(hf + 1) * 512], a_[:], mrep[:, 5120 + hf * 512:5120 + (hf + 1) * 512], ALU.mult,
                    [a_, mrep], [pre])
            stt(pre[:], x[:, t, :], DN_ALPHA, pre[:], ALU.mult, ALU.add, [x, pre], [pre])
            mean, rstd = ln_stats(pre[:], pre)
            ts_("dve", pre[:], pre[:], mean, rstd, ALU.subtract, ALU.mult, [pre, small], [pre])
            tt_("dve", pre[:], pre[:], lnp[:, 0, :], ALU.mult, [pre, lnp], [pre])
            tt_("dve", x[:, t, :], pre[:], lnp[:, 1, :], ALU.add, [pre, lnp], [x])

        NSTEP = NT * 64
        accs = {}

        def st_A(i):
            t, q = divmod(i, 64)
            as_ = asg[i % 8]
            for j in range(2):
                s_ = q * 2 + j
                gb = gb_[(2 * i + j) % NG]
                P.dma("pool", lambda e, gb=gb, s_=s_, t=t: e.indirect_dma_start(
                    out=gb[:], out_offset=None, in_=uvb[l],
                    in_offset=bass.IndirectOffsetOnAxis(ap=e_idx[:, t, s_:s_ + 1], axis=0)),
                    [e_idx.r, uvbuf[l].r], [gb.r])
            for j in range(2):
                jk = junks[j]
                gb = gb_[(2 * i + j) % NG]
                stt(jk[:], gb[:, 0:D], 1.0, h2tok[:, t, :], ALU.mult, ALU.mult, [gb, h2tok], [jk, as_],
                    accum_out=as_[:, j:j + 1])

        def st_B(i):
            act(awg[i % 8][:], asg[i % 8][:], AF.Gelu, [asg[i % 8]], [awg[i % 8]])

        def st_CDE(i):
            t, q = divmod(i, 64)
            aw_, dg = awg[i % 8], dgs[i % 4]
            if q == 0:
                accs[t] = (hbank(), hbank())
            a0, a1 = accs[t]
            tt_("dve", aw_[:], aw_[:], g_all[:, t, q * 2:q * 2 + 2], ALU.mult, [aw_, g_all], [aw_])
            for j in range(2):
                act(dg[:, j, :], ident[:], AF.Identity, [ident, aw_], [dg], scale=aw_[:, j:j + 1])
            for j in range(2):
                s_ = q * 2 + j
                gb = gb_[(2 * i + j) % NG]
                mm(a0[:], dg[:, j, :], gb[:, D:D + 512], s_ == 0, s_ == 127, [dg, gb], [a0])
                mm(a1[:], dg[:, j, :], gb[:, D + 512:2 * D], s_ == 0, s_ == 127, [dg, gb], [a1])
            if q == 63:
                finish_v(t, a0, a1)

        for i in range(NSTEP + 2):
            if i < NSTEP:
                st_A(i)
            if 0 <= i - 1 < NSTEP:
                st_B(i - 1)
            if 0 <= i - 2 < NSTEP:
                st_CDE(i - 2)
        if l == 0:
            tap("x2", x, x[:], [128, NT, D])

    try:
        for l in range(NL):
            layer_body(l)
    except _Stop:
        pass

    P.barrier()
    for t in range(NT):
        DMA("sp", O["y"][t * 128:(t + 1) * 128, :], x[:, t, :], reads=[x], is_output=True)
    P.finish()
    return tap_out


def _na_tables():
    rows, kr = 16, 8
    r = np.arange(rows)
    start = np.clip(r - kr // 2, 0, rows - kr)
    rowvalid = np.zeros((rows, rows), bool)
    for i in range(rows):
        rowvalid[i, start[i]:start[i] + kr] = True
    qc = np.arange(64)
    qstart = np.clip(qc - 8, 0, 48)
    kc = np.arange(64)
    colvalid = (kc[None, :] >= qstart[:, None]) & (kc[None, :] < qstart[:, None] + 16)
    return rowvalid, colvalid


def host_inputs(inp, NL=DEPTH):
    f = np.float32
    rowvalid, colvalid = _na_tables()
    com = {}
    com["w_mod"] = np.ascontiguousarray(inp["w_mod"][:NL])
    com["b_mod"] = np.ascontiguousarray(inp["b_mod"][:NL].reshape(NL, 1, 6144))
    com["w_in"] = np.ascontiguousarray(inp["w_in"][:NL])
    cw = inp["conv_w"][:NL]
    com["cw"] = np.ascontiguousarray(cw.reshape(NL, 5, 6, 128).transpose(0, 3, 2, 1))
    com["cb"] = np.ascontiguousarray(inp["conv_b"][:NL].reshape(NL, 6, 128).transpose(0, 2, 1))
    com["alog"] = np.ascontiguousarray(np.broadcast_to(inp["ssd_a_log"][:NL].reshape(NL, 1, 16), (NL, 128, 16)))
    com["dtb"] = np.ascontiguousarray(np.broadcast_to(inp["ssd_dt_bias"][:NL].reshape(NL, 1, 16), (NL, 128, 16)))
    dsk = np.zeros((NL, 128, 4), f)
    for h in range(8):
        dsk[:, (h % 2) * 64:(h % 2) * 64 + 64, h // 2] = inp["ssd_d"][:NL, h][:, None]
    com["dsk"] = dsk
    com["ng"] = np.ascontiguousarray(inp["ssd_norm_g"][:NL].reshape(NL, 4, 128).transpose(0, 2, 1))
    com["qn"] = np.ascontiguousarray(np.broadcast_to(inp["gqa_q_norm"][:NL][:, None, :], (NL, 128, 64)))
    com["kn"] = np.ascontiguousarray(np.broadcast_to(inp["gqa_k_norm"][:NL][:, None, :], (NL, 128, 64)))
    lnp = np.stack([inp["ln1_g"][:NL], inp["ln1_b"][:NL], inp["ln2_g"][:NL], inp["ln2_b"][:NL]], 1)
    com["lnp"] = np.ascontiguousarray(np.broadcast_to(lnp[:, :, None, :], (NL, 4, 128, D)))
    com["w_branch"] = np.ascontiguousarray(inp["w_branch"][:NL])
    com["w_out"] = np.ascontiguousarray(inp["w_out"][:NL])
    com["wq"] = np.ascontiguousarray(inp["peer_wq"][:NL])
    com["keysT"] = np.ascontiguousarray(inp["peer_keys"][:NL].reshape(NL, 16, 128, 128).transpose(0, 1, 3, 2))
    for i in range(NL):
        com["uv%d" % i] = np.ascontiguousarray(np.concatenate([inp["peer_u"][i], inp["peer_v"][i]], axis=1))
    t = np.arange(1024)
    qaug = (t[None, :] // 64 == np.arange(16)[:, None]).astype(f)
    com["qaug"] = qaug
    rpb = inp["na_rpb"][:NL]
    tts = np.zeros((NL, 8, 2, 64, 30, 64), f)
    kc = np.arange(64)[:, None]
    qc = np.arange(64)[None, :]
    dc = np.clip(kc - qc + 15, 0, 30)
    band = colvalid.T
    for p2 in range(2):
        for e in range(30):
            dr = p2 - (e - 14)
            if -7 <= dr <= 7:
                tile = rpb[:, :, dr + 7, :][:, :, dc]
                tts[:, :, p2, :, e, :] = np.where(band[None, None], tile, f(NEG))
    tts = tts.reshape(NL, 8, 128, 30 * 64)
    ttp = np.zeros_like(tts)

    half = 32
    inv = 1.0 / (10000.0 ** (np.arange(0, half, 2, dtype=np.float32) / half))

    maps = []
    for core in range(8):
        m = dict(com)
        sample = core >= 4
        if sample:
            b = core - 4
            m["x0"] = np.ascontiguousarray(inp["x_sample"][b])
            cond = inp["c"][b]
            m["c_nakT"] = np.ascontiguousarray(inp["cache_na_k"][b, :NL].transpose(0, 2, 3, 1))
            m["c_nav"] = np.ascontiguousarray(inp["cache_na_v"][b, :NL].reshape(NL, 256, 512))
            m["c_gkT"] = np.ascontiguousarray(inp["cache_gqa_k"][b, :NL].transpose(0, 2, 3, 1))
            m["c_gv"] = np.ascontiguousarray(inp["cache_gqa_v"][b, :NL].reshape(NL, 256, 128))
            st = inp["state_ssd"][b, :NL]
            st = st.reshape(NL, 2, 2, 4, 64, 64).transpose(0, 1, 2, 5, 3, 4)
            m["h0"] = np.ascontiguousarray(st.reshape(NL, 2, 128, 256))
            m["tt"] = tts
            rm_na = np.where(rowvalid.T, 0.0, NEG * 8).astype(f)
            rm_g = np.zeros((16, 16), f)
            ctxv = 0.0
            pos_r = (t // 64).astype(f)
            pos_c = (t % 64).astype(f)
            cos = np.ones((1024, 64), f)
            sin = np.zeros((1024, 64), f)
            for a, pos in enumerate((pos_r, pos_c)):
                ang = pos[:, None] * inv[None, :]
                cos[:, a * 32:a * 32 + 16] = np.cos(ang)
                cos[:, a * 32 + 16:a * 32 + 32] = np.cos(ang)
                sin[:, a * 32:a * 32 + 16] = -np.sin(ang)
                sin[:, a * 32 + 16:a * 32 + 32] = np.sin(ang)
            keep = np.ones((128, 16), f)
            cm = np.ones((128, 1), f)
        else:
            m["x0"] = np.ascontiguousarray(inp["x_prompt"][core * 4:(core + 1) * 4].reshape(1024, D))
            cond = inp["c_ctx"]
            m["c_nakT"] = np.zeros((NL, 8, 64, 256), f)
            m["c_nav"] = np.zeros((NL, 256, 512), f)
            m["c_gkT"] = np.zeros((NL, 2, 64, 256), f)
            m["c_gv"] = np.zeros((NL, 256, 128), f)
            m["h0"] = np.zeros((NL, 2, 128, 256), f)
            m["tt"] = ttp
            seq = np.arange(16) // 4
            rm_na = np.where(seq[:, None] == seq[None, :], 0.0, NEG * 8).astype(f)
            rm_g = rm_na
            ctxv = NEG * 8
            cos = np.ones((1024, 64), f)
            sin = np.zeros((1024, 64), f)
            keep = np.ones((128, 16), f)
            keep[:, [0, 2, 4, 6]] = 0.0
            keep[:, [8 + 1, 8 + 3, 8 + 5, 8 + 7]] = 0.0
            cm = np.zeros((128, 1), f)
        m["cond_rep"] = np.ascontiguousarray(np.broadcast_to(cond.reshape(8, 128).T[:, :, None], (128, 8, 128)))
        for nm, rm in (("kaug_na", rm_na), ("kaug_g", rm_g)):
            ka = np.zeros((16, 1280), f)
            ka[:, :1024] = rm[t // 64, :].T
            ka[:, 1024:] = ctxv
            m[nm] = ka
        m["cosT"] = np.ascontiguousarray(cos.reshape(8, 128, 64).transpose(1, 0, 2))
        m["sinT"] = np.ascontiguousarray(sin.reshape(8, 128, 64).transpose(1, 0, 2))
        m["keep"] = keep
        m["cmask"] = cm
        maps.append({k: np.ascontiguousarray(v, dtype=np.float32) for k, v in m.items()})
    return maps


def assemble(results, NL=DEPTH):
    f = np.float32
    y_p = np.concatenate([results[c]["y"].reshape(4, 256, D) for c in range(4)], 0)
    y_s = np.stack([results[c]["y"] for c in range(4, 8)], 0)

    def cache(name, nh):
        parts = []
        for c in range(4):
            a = results[c][name].reshape(NL, 4, 256, nh, 64).transpose(1, 0, 2, 3, 4)
            parts.append(a)
        return np.ascontiguousarray(np.concatenate(parts, 0), dtype=f)

    nak, nav, gk, gv = cache("o_nak", 8), cache("o_nav", 8), cache("o_gk", 2), cache("o_gv", 2)
    sts = []
    for c in range(4):
        a = results[c]["o_st"].reshape(NL, 4, 2, 2, 64, 4, 64)
        a = a.transpose(1, 0, 2, 3, 5, 6, 4).reshape(4, NL, 2, 8, 64, 64)
        sts.append(a)
    st = np.ascontiguousarray(np.concatenate(sts, 0), dtype=f)
    return (y_p.astype(f), y_s.astype(f), nak, nav, gk, gv, st)


def kernel(**inputs):
    inp = {k: np.asarray(v) for k, v in inputs.items()}
    nc = bass.Bass("TRN2", target_bir_lowering=False)
    build(nc)
    maps = host_inputs(inp)
    res = run_bass_kernel_spmd(nc, maps, core_ids=list(range(8)))
    return assemble(res.results)
```

```python
import contextlib
import sys
import numpy as np
import concourse.bass as bass
import concourse.mybir as mybir
from concourse.alu_op_type import AluOpType as ALU
from concourse.bass_utils import run_bass_kernel_spmd

F32 = mybir.dt.float32
BF16 = mybir.dt.bfloat16
U32 = mybir.dt.uint32
I32 = mybir.dt.int32
AF = mybir.ActivationFunctionType
AX = mybir.AxisListType

D = 1024
NT = 8
DEPTH = 4
EPS = 1e-6
DN_ALPHA = (2 * DEPTH) ** 0.25
NEG = -30000.0
IN_COLS = 6672
OFF = dict(z=0, xs=512, B=1024, C=1152, dt=1280, naq=1296, nak=1808, nav=2320, gq=2832, gk=3344, gv=3472, gate=3600)


class Res:
    __slots__ = ("name", "last_w", "reads", "ch")

    def __init__(self, name):
        self.name = name
        self.last_w = None
        self.reads = {}
        self.ch = None


class Prog:
    ENG = ("pe", "act", "dve", "pool", "sp")

    def __init__(self, nc):
        self.nc = nc
        self.stack = contextlib.ExitStack()
        self.ops = {e: [] for e in self.ENG}
        self.cnt = {}
        self.sems = {}
        self.seen = {e: {} for e in self.ENG}
        self.pend = {e: {} for e in self.ENG}
        self.out_events = []
        self.nres = 0
        for e in self.ENG:
            self._sem("E_" + e)

    def _sem(self, key):
        if key not in self.sems:
            self.sems[key] = self.stack.enter_context(self.nc.semaphore("s" + key))
            self.cnt[key] = 0
        return self.sems[key]

    def sb(self, name, shape, dtype):
        return self.stack.enter_context(self.nc.sbuf_tensor("sb_" + name, list(shape), dtype))

    def ps(self, name, shape, dtype=F32):
        return self.stack.enter_context(self.nc.psum_tensor("ps_" + name, list(shape), dtype))

    def res(self, name=None):
        self.nres += 1
        return Res("%s_%d" % (name or "r", self.nres))

    def _deps(self, e, reads, writes):
        deps = dict(self.pend[e])
        self.pend[e] = {}

        def add(ev):
            if ev is None:
                return
            k, v = ev
            if deps.get(k, 0) < v:
                deps[k] = v

        for r in reads:
            add(r.last_w)
        for w in writes:
            add(w.last_w)
            for k, v in w.reads.items():
                add((k, v))
        waits = []
        seen = self.seen[e]
        own = "E_" + e
        for k, v in deps.items():
            if e == "pe" and k == own:
                continue
            if seen.get(k, 0) >= v:
                continue
            seen[k] = v
            waits.append((k, v))
        return waits

    def _commit(self, ev, reads, writes):
        k, v = ev
        for r in reads:
            if r.reads.get(k, 0) < v:
                r.reads[k] = v
        for w in writes:
            w.last_w = ev
            w.reads = {}

    def op(self, e, fn, reads=(), writes=()):
        waits = self._deps(e, reads, writes)
        key = "E_" + e
        self.cnt[key] += 1
        ev = (key, self.cnt[key])
        self.ops[e].append((waits, fn, key, 1, self._where()))
        self._commit(ev, reads, writes)
        return ev

    @staticmethod
    def _where():
        f = sys._getframe(2)
        out = []
        while f is not None and len(out) < 5:
            out.append(f.f_lineno)
            f = f.f_back
        return out

    def dma(self, q, fn, reads=(), writes=(), is_output=False):
        tgt = writes[0] if writes else reads[0]
        if tgt.ch is None:
            tgt.ch = "D_" + tgt.name.rsplit("_", 1)[0]
            self._sem(tgt.ch)
        key = tgt.ch
        waits = self._deps(q, reads, writes)
        self.cnt[key] += 16
        ev = (key, self.cnt[key])
        self.ops[q].append((waits, fn, key, 16, self._where()))
        self._commit(ev, reads, writes)
        if is_output:
            self.out_events.append(ev)
        return ev

    def barrier(self):
        snap = dict(self.cnt)
        for e in self.ENG:
            for k, v in snap.items():
                if v > 0 and self.pend[e].get(k, 0) < v:
                    self.pend[e][k] = v

    def finish(self):
        finals = {}
        for k, v in self.out_events:
            finals[k] = max(finals.get(k, 0), v)
        for e in self.ENG:
            if e != "sp" and self.cnt["E_" + e] > 0:
                finals["E_" + e] = self.cnt["E_" + e]
        final_waits = list(finals.items())
        nc, sems, ops = self.nc, self.sems, self.ops

        needed = {}
        for e in self.ENG:
            for waits, fn, key, inc, where in ops[e]:
                for k, v in waits:
                    needed.setdefault(k, set()).add(v)
        for k, v in final_waits:
            needed.setdefault(k, set()).add(v)
        remap = {}
        for e in self.ENG:
            key = "E_" + e
            need = needed.get(key, set())
            m = {}
            new_c = 0
            old_c = 0
            lst = []
            for waits, fn, k2, inc, where in ops[e]:
                if k2 == key:
                    old_c += 1
                    if old_c in need:
                        new_c += 1
                        m[old_c] = new_c
                        lst.append((waits, fn, k2, 1, where))
                    else:
                        lst.append((waits, fn, k2, 0, where))
                else:
                    lst.append((waits, fn, k2, inc, where))
            ops[e] = lst
            remap[key] = m

        def rv(k, v):
            return remap[k][v] if k in remap else v

        final_waits = [(k, rv(k, v)) for k, v in final_waits]

        def replay(eng, lst):
            for waits, fn, key, inc, where in lst:
                for k, v in waits:
                    eng.wait_ge(sems[k], rv(k, v))
                try:
                    inst = fn(eng)
                    if inc:
                        inst.then_inc(sems[key], inc)
                except Exception:
                    print("FAILED OP created at lines", where)
                    raise

        with nc.Block() as block:
            @block.tensor
            def _(eng):
                replay(eng, ops["pe"])

            @block.scalar
            def _(eng):
                replay(eng, ops["act"])

            @block.vector
            def _(eng):
                replay(eng, ops["dve"])

            @block.gpsimd
            def _(eng):
                replay(eng, ops["pool"])

            @block.sync
            def _(eng):
                replay(eng, ops["sp"])
                for k, v in final_waits:
                    eng.wait_ge(sems[k], v)
        self.stack.close()


class Buf:
    __slots__ = ("t", "r")

    def __init__(self, t, r):
        self.t = t
        self.r = r

    def __getitem__(self, idx):
        return self.t[idx]


class _Stop(Exception):
    pass


def build(nc, NL=DEPTH, taps=None, stop=None):
    P = Prog(nc)
    taps = taps or []
    tap_out = {}

    def din(name, shape, dt=F32):
        return nc.dram_tensor(name, list(shape), dt, kind="ExternalInput").ap()

    def dout(name, shape, dt=F32):
        return nc.dram_tensor(name, list(shape), dt, kind="ExternalOutput").ap()

    I = dict(
        x0=din("x0", [1024, D]), cond_rep=din("cond_rep", [128, 8, 128]),
        w_mod=din("w_mod", [NL, D, 6144]), b_mod=din("b_mod", [NL, 1, 6144]),
        w_in=din("w_in", [NL, D, IN_COLS]), cw=din("cw", [NL, 128, 6, 5]), cb=din("cb", [NL, 128, 6]),
        alog=din("alog", [NL, 128, 16]), dtb=din("dtb", [NL, 128, 16]),
        dsk=din("dsk", [NL, 128, 4]), ng=din("ng", [NL, 128, 4]),
        tt=din("tt", [NL, 8, 128, 30 * 64]), qn=din("qn", [NL, 128, 64]), kn=din("kn", [NL, 128, 64]),
        lnp=din("lnp", [NL, 4, 128, D]),
        w_branch=din("w_branch", [NL, 3, 512, D]), w_out=din("w_out", [NL, D, D]),
        wq=din("wq", [NL, D, 2048]), keysT=din("keysT", [NL, 16, 128, 128]),
        uv=[din("uv%d" % i, [16384, 2 * D]) for i in range(NL)],
        c_nakT=din("c_nakT", [NL, 8, 64, 256]), c_nav=din("c_nav", [NL, 256, 512]),
        c_gkT=din("c_gkT", [NL, 2, 64, 256]), c_gv=din("c_gv", [NL, 256, 128]),
        h0=din("h0", [NL, 2, 128, 256]),
        qaug=din("qaug", [16, 1024]), kaug_na=din("kaug_na", [16, 1280]), kaug_g=din("kaug_g", [16, 1280]),
        cosT=din("cosT", [128, 8, 64]), sinT=din("sinT", [128, 8, 64]),
        keep=din("keep", [128, 16]), cmask=din("cmask", [128, 1]),
    )
    O = dict(
        y=dout("y", [1024, D]), o_nak=dout("o_nak", [NL, 1024, 512]), o_nav=dout("o_nav", [NL, 1024, 512]),
        o_gk=dout("o_gk", [NL, 1024, 128]), o_gv=dout("o_gv", [NL, 1024, 128]),
        o_st=dout("o_st", [NL, 4, 2, 128, 256]),
    )

    def B(name, shape, dt):
        return Buf(P.sb(name, shape, dt), P.res(name))

    def OP(e, fn, reads=(), writes=()):
        ws = [b.r for b in writes] + [b.r for b in reads if b.r.name.startswith("bank")]
        P.op(e, fn, [b.r for b in reads], ws)

    def DMA(q, out, in_, reads=(), writes=(), is_output=False):
        P.dma(q, lambda e: e.dma_start(out=out, in_=in_), [b.r for b in reads], [b.r for b in writes], is_output)

    def mm(out, lhsT, rhs, start, stop, reads, writes):
        OP("pe", lambda e: e.matmul(out, lhsT=lhsT, rhs=rhs, start=start, stop=stop), reads, writes)

    def tp(out, in_, ident, reads, writes):
        OP("pe", lambda e: e.transpose(out=out, in_=in_, identity=ident), reads, writes)

    def act(out, in_, func, reads, writes, bias=None, scale=None, accum_out=None):
        kw = {}
        if bias is not None:
            kw["bias"] = bias
        if scale is not None:
            kw["scale"] = scale
        if accum_out is not None:
            kw["accum_out"] = accum_out
        OP("act", lambda e: e.activation(out=out, in_=in_, func=func, **kw), reads, writes)

    def tt_(eng, out, in0, in1, op, reads, writes):
        OP(eng, lambda e: e.tensor_tensor(out=out, in0=in0, in1=in1, op=op), reads, writes)

    def ts_(eng, out, in0, s1, s2, op0, op1, reads, writes):
        if op1 is None:
            OP(eng, lambda e: e.tensor_scalar(out=out, in0=in0, scalar1=s1, scalar2=None, op0=op0), reads, writes)
        else:
            OP(eng, lambda e: e.tensor_scalar(out=out, in0=in0, scalar1=s1, scalar2=s2, op0=op0, op1=op1), reads, writes)

    def stt(out, in0, scalar, in1, op0, op1, reads, writes, accum_out=None):
        if accum_out is None:
            OP("dve", lambda e: e.scalar_tensor_tensor(out=out, in0=in0, scalar=scalar, in1=in1, op0=op0, op1=op1),
               reads, writes)
        else:
            OP("dve", lambda e: e.scalar_tensor_tensor(out=out, in0=in0, scalar=scalar, in1=in1, op0=op0, op1=op1,
                                                       accum_out=accum_out), reads, writes)

    def cp(eng, out, in_, reads, writes):
        if eng == "act":
            OP("act", lambda e: e.copy(out=out, in_=in_), reads, writes)
        else:
            OP(eng, lambda e: e.tensor_copy(out=out, in_=in_), reads, writes)

    def tap(name, buf, ap, shape, dt=F32):
        if name in taps:
            d = dout("tap_" + name, shape, dt)
            tap_out[name] = d
            DMA("sp", d, ap, reads=[buf], is_output=True)

    banks = [Buf(P.ps("bank%d" % i, [128, 512], F32), P.res("bank%d" % i)) for i in range(8)]
    rot = [0, 0]

    def sbank():
        b = banks[rot[0] % 4]
        rot[0] += 1
        return b

    def hbank():
        b = banks[4 + rot[1] % 4]
        rot[1] += 1
        return b

    x = B("x", [128, NT, D], F32)
    mrep = B("mrep", [128, 6144], F32)
    hT = B("hT", [128, 8, 1024], BF16)
    lnp = B("lnp", [128, 2, D], F32)
    ident = B("ident", [128, 128], BF16)
    identf = B("identf", [128, 128], F32)
    TRIf = B("TRIf", [128, 128], F32)
    TRIb = B("TRIb", [128, 128], F32)
    maskf = B("maskf", [128, 128], F32)
    maskb = B("maskb", [128, 128], F32)
    onesb = B("onesb", [128, 128], BF16)
    onesf = B("onesf", [128, 128], F32)
    iota16 = B("iota16", [128, 16], F32)
    condS = B("condS", [128, 8, 128], BF16)
    small = B("small", [128, 64], F32)
    keep = B("keep", [128, 16], F32)
    cmask = B("cmask", [128, 1], F32)
    cosT = B("cosT", [128, 8, 64], F32)
    sinT = B("sinT", [128, 8, 64], F32)
    NW = 3
    wbuf = [B("wbuf%d" % i, [128, 8, 512], BF16) for i in range(NW)]
    wrot = [0]

    uvb = [nc.dram_tensor("uvb%d" % i, [16384, 2 * D], BF16, kind="Internal").ap() for i in range(NL)]
    uvbuf = [Buf(None, P.res("uvb")) for i in range(NL)]
    NSTG = 3
    stgc = [B("stgc%d" % i, [128, 2 * D], BF16) for i in range(NSTG)]
    conv = {"l": 0, "i": 128}

    def start_conv(l_):
        conv["l"] = l_
        conv["i"] = 0

    def pump(n):
        while n > 0 and conv["i"] < 128:
            i_, l_ = conv["i"], conv["l"]
            st = stgc[i_ % NSTG]
            DMA("pool", st[:], I["uv"][l_][i_ * 128:(i_ + 1) * 128, :], writes=[st])
            DMA("sp", uvb[l_][i_ * 128:(i_ + 1) * 128, :], st[:], reads=[st], writes=[uvbuf[l_]])
            conv["i"] += 1
            n -= 1

    ARENA_W = 20480
    arena = P.sb("arena", [128, ARENA_W], F32)
    ar = {"off": 0, "n": 0}

    def stage_begin():
        P.barrier()
        ar["off"] = 0

    def A(name, shape, dt, parts=128):
        free = int(np.prod(shape[1:]))
        words = (free * (2 if dt == BF16 else 4) + 3) // 4
        words = (words + 7) // 8 * 8
        o = ar["off"]
        assert o + words <= ARENA_W, ("arena overflow", name, o, words)
        ar["off"] = o + words
        ar["n"] += 1
        v = arena[0:shape[0], o:o + words]
        if dt != F32:
            v = v.bitcast(dt)
        v = v[:, 0:free]
        if len(shape) == 3:
            v = v.rearrange("p (a b) -> p a b", b=shape[2])
        elif len(shape) == 4:
            v = v.rearrange("p (a b c) -> p a b c", b=shape[2], c=shape[3])
        elif len(shape) == 5:
            v = v.rearrange("p (a b c d) -> p a b c d", b=shape[2], c=shape[3], d=shape[4])
        return Buf(v, P.res(name))

    def load_w(src, ncols):
        wb = wbuf[wrot[0] % NW]
        wrot[0] += 1
        DMA("pool", wb[:, :, 0:ncols], src.rearrange("(k p) c -> p k c", p=128), writes=[wb])
        pump(2)
        return wb

    OP("pool", lambda e: e.memset(onesf[:], 1.0), writes=[onesf])
    OP("pool", lambda e: e.memset(onesb[:], 1.0), writes=[onesb])
    OP("pool", lambda e: e.memset(identf[:], 1.0), writes=[identf])
    OP("pool", lambda e: e.affine_select(out=identf[:], in_=identf[:], pattern=[[-1, 128]], compare_op=ALU.is_equal,
                                         fill=0.0, base=0, channel_multiplier=1), reads=[identf], writes=[identf])
    cp("pool", ident[:], identf[:], [identf], [ident])
    OP("pool", lambda e: e.affine_select(out=TRIf[:], in_=onesf[:], pattern=[[1, 128]], compare_op=ALU.is_ge,
                                         fill=0.0, base=0, channel_multiplier=-1), reads=[onesf], writes=[TRIf])
    OP("pool", lambda e: e.affine_select(out=TRIb[:], in_=onesf[:], pattern=[[-1, 128]], compare_op=ALU.is_ge,
                                         fill=0.0, base=0, channel_multiplier=1), reads=[onesf], writes=[TRIb])
    OP("pool", lambda e: e.memset(maskf[:], 0.0), writes=[maskf])
    OP("pool", lambda e: e.memset(maskb[:], 0.0), writes=[maskb])
    OP("pool", lambda e: e.affine_select(out=maskf[:], in_=maskf[:], pattern=[[1, 128]], compare_op=ALU.is_ge,
                                         fill=NEG, base=0, channel_multiplier=-1), reads=[maskf], writes=[maskf])
    OP("pool", lambda e: e.affine_select(out=maskb[:], in_=maskb[:], pattern=[[-1, 128]], compare_op=ALU.is_ge,
                                         fill=NEG, base=0, channel_multiplier=1), reads=[maskb], writes=[maskb])
    OP("pool", lambda e: e.iota(iota16[:], pattern=[[1, 16]], base=0, channel_multiplier=0,
                                allow_small_or_imprecise_dtypes=True), writes=[iota16])
    DMA("sp", keep[:], I["keep"], writes=[keep])
    DMA("sp", cmask[:], I["cmask"], writes=[cmask])
    DMA("sp", cosT[:], I["cosT"], writes=[cosT])
    DMA("sp", sinT[:], I["sinT"], writes=[sinT])
    condf = A("condf", [128, 8, 128], F32)
    DMA("sp", condf[:], I["cond_rep"], writes=[condf])
    act(condS[:], condf[:], AF.Silu, [condf], [condS])
    for t in range(NT):
        DMA("sp", x[:, t, :], I["x0"][t * 128:(t + 1) * 128, :], writes=[x])

    def ln_stats(src_ap, srcbuf):
        OP("dve", lambda e: e.bn_stats(out=small[:, 0:6], in_=src_ap[:, 0:512]), [srcbuf], [small])
        OP("dve", lambda e: e.bn_stats(out=small[:, 6:12], in_=src_ap[:, 512:1024]), [srcbuf, small], [small])
        OP("dve", lambda e: e.bn_aggr(out=small[:, 12:14], in_=small[:, 0:12]), [small], [small])
        act(small[:, 14:15], small[:, 13:14], AF.Ln, [small], [small], bias=EPS)
        act(small[:, 15:16], small[:, 14:15], AF.Exp, [small], [small], scale=-0.5)
        return small[:, 12:13], small[:, 15:16]

    def ln_modulate(shift_off, scale_off, tmpA, hb, h2tok=None):
        for t in range(NT):
            mean, rstd = ln_stats(x[:, t, :], x)
            ts_("dve", tmpA[:], x[:, t, :], mean, rstd, ALU.subtract, ALU.mult, [x, small], [tmpA])
            tt_("dve", tmpA[:], tmpA[:], mrep[:, scale_off:scale_off + D], ALU.mult, [tmpA, mrep], [tmpA])
            dst = hb if h2tok is None else h2tok
            dst_ap = hb[:] if h2tok is None else h2tok[:, t, :]
            tt_("dve", dst_ap, tmpA[:], mrep[:, shift_off:shift_off + D], ALU.add, [tmpA, mrep], [dst])
            bk = sbank()
            bkb = bk[:].bitcast(BF16)
            for k in range(8):
                tp(bkb[:, k * 128:(k + 1) * 128], dst_ap[:, k * 128:(k + 1) * 128], ident[:], [dst, ident], [bk])
            cp("act", hT[:, :, t * 128:(t + 1) * 128], bkb.rearrange("p (k t) -> p k t", t=128), [bk], [hT])

    def ln_affine(pre, gi, tmpbuf):
        pass

    def chk(k):
        if stop == k:
            raise _Stop()

    def layer_body(l):
        stage_begin()
        start_conv(l)
        bmod = A("bmod", [1, 6144], BF16)
        DMA("pool", bmod[:], I["b_mod"][l], writes=[bmod])
        for cc in range(12):
            wb = load_w(I["w_mod"][l][:, cc * 512:(cc + 1) * 512], 512)
            bk = sbank()
            for k in range(8):
                mm(bk[:], condS[:, k, :], wb[:, k, :], k == 0, False, [condS, wb], [bk])
            mm(bk[:], onesb[0:1, :], bmod[0:1, cc * 512:(cc + 1) * 512], False, True, [onesb, bmod], [bk])
            if cc in (2, 3, 8, 9):
                act(mrep[:, cc * 512:(cc + 1) * 512], bk[:], AF.Identity, [bk], [mrep], bias=1.0)
            else:
                cp("act", mrep[:, cc * 512:(cc + 1) * 512], bk[:], [bk], [mrep])
        if l == 0:
            tap("mrep", mrep, mrep[:], [128, 6144])
        chk(1)

        tmpA = A("tmpA", [128, D], F32)
        hb = A("hb", [128, D], BF16)
        ln_modulate(0, 1024, tmpA, hb)
        if l == 0:
            tap("hT", hT, hT[:], [128, 8, 1024], BF16)
        chk(2)

        stage_begin()
        yA = A("yA", [128, 4, 1024], BF16)
        yB = A("yB", [128, 4, 1024], BF16)
        yC = A("yC", [128, 4, 1024], BF16)
        mix_off = ar["off"]
        xc = A("xc", [128, 6, 1024], BF16)
        x_tok = A("x_tok", [128, 8, 512], BF16)
        B_tok = A("B_tok", [128, 8, 128], BF16)
        Sin = A("Sin", [128, 8, 2, 256], BF16)
        a_all = A("a_all", [128, 8, 16], F32)
        lndt = A("lndt", [128, 8, 16], F32)
        csT = A("csT", [128, 8, 32], F32)
        w_all = A("w_all", [128, 8, 16], F32)
        decG = A("decG", [128, 8, 2, 4], F32)
        Arep = A("Arep", [128, 16], F32)
        dtbr = A("dtbr", [128, 16], F32)
        cwb = A("cwb", [128, 6, 5], F32)
        cbb = A("cbb", [128, 6], F32)
        dskb = A("dskb", [128, 4], F32)
        ngb = A("ngb", [128, 4], F32)
        S = A("S", [128, 2, 256], F32)
        sc16 = A("sc16", [128, 4, 16], F32)
        ssd_off = ar["off"]
        xp = A("xp", [128, 6, 4, 260], BF16)
        acc = A("acc", [128, 4, 256], F32)

        DMA("sp", Arep[:], I["alog"][l], writes=[Arep])
        DMA("sp", dtbr[:], I["dtb"][l], writes=[dtbr])
        DMA("sp", cwb[:], I["cw"][l], writes=[cwb])
        DMA("sp", cbb[:], I["cb"][l], writes=[cbb])
        DMA("sp", dskb[:], I["dsk"][l], writes=[dskb])
        DMA("sp", ngb[:], I["ng"][l], writes=[ngb])
        DMA("sp", S[:], I["h0"][l].rearrange("d p f -> p d f"), writes=[S])
        act(Arep[:], Arep[:], AF.Exp, [Arep], [Arep])
        ts_("dve", Arep[:], Arep[:], -1.0, None, ALU.mult, None, [Arep], [Arep])

        OP("pool", lambda e: e.memset(xp[:], 0.0), writes=[xp])
        w1 = load_w(I["w_in"][l][:, OFF["xs"]:OFF["xs"] + 512], 512)
        w2 = load_w(I["w_in"][l][:, OFF["B"]:OFF["B"] + 272], 272)
        for k in range(6):
            wsrc, c0 = (w1, k * 128) if k < 4 else (w2, (k - 4) * 128)
            for half in range(2):
                bk = sbank()
                for kd in range(8):
                    mm(bk[:], wsrc[:, kd, c0:c0 + 128], hT[:, kd, half * 512:(half + 1) * 512], kd == 0, kd == 7,
                       [wsrc, hT], [bk])
                cp("act", xp[:, k, 2 * half:2 * half + 2, 2:258], bk[:].rearrange("p (s t) -> p s t", t=256), [bk], [xp])
        chk(21)
        for t in range(NT):
            bk = sbank()
            for kd in range(8):
                mm(bk[:, 0:16], hT[:, kd, t * 128:(t + 1) * 128], w2[:, kd, 256:272], kd == 0, kd == 7, [hT, w2], [bk])
            tt_("dve", sc16[:, 0, :], bk[:, 0:16], dtbr[:], ALU.add, [bk, dtbr], [sc16])
            act(sc16[:, 1, :], sc16[:, 0, :], AF.Exp, [sc16], [sc16])
            act(sc16[:, 2, :], sc16[:, 1, :], AF.Ln, [sc16], [sc16], bias=1.0)
            act(lndt[:, t, :], sc16[:, 2, :], AF.Ln, [sc16], [lndt])
            tt_("dve", a_all[:, t, :], sc16[:, 2, :], Arep[:], ALU.mult, [sc16, Arep], [a_all])
        chk(22)
        ts_("dve", xp[:, :, 1:4, 0:2], xp[:, :, 0:3, 256:258], cmask[:, 0:1], None, ALU.mult, None, [xp, cmask], [xp])
        ts_("dve", xp[:, :, 0:3, 258:260], xp[:, :, 1:4, 2:4], cmask[:, 0:1], None, ALU.mult, None, [xp, cmask], [xp])
        for k in range(6):
            ts_("dve", acc[:], xp[:, k, :, 0:256], cwb[:, k, 0:1], None, ALU.mult, None, [xp, cwb], [acc])
            for j in range(1, 5):
                stt(acc[:], xp[:, k, :, j:j + 256], cwb[:, k, j:j + 1], acc[:], ALU.mult, ALU.add, [xp, cwb, acc], [acc])
            act(xc[:, k, :].rearrange("p (s t) -> p s t", t=256), acc[:], AF.Silu, [acc, cbb], [xc], bias=cbb[:, k:k + 1])
        chk(23)
        for t in range(NT):
            bk = sbank()
            bkb = bk[:].bitcast(BF16)
            for k in range(5):
                tp(bkb[:, k * 128:(k + 1) * 128], xc[:, k, t * 128:(t + 1) * 128], ident[:], [xc, ident], [bk])
            cp("act", x_tok[:, t, :], bkb[:, 0:512], [bk], [x_tok])
            chk(24)
            cp("act", B_tok[:, t, :], bkb[:, 512:640], [bk], [B_tok])

        chk(3)
        P.barrier()
        ar["off"] = ssd_off
        for c in range(8):
            bk = sbank()
            mm(bk[:, 0:8], TRIf[:], a_all[:, c, 0:8], True, True, [TRIf, a_all], [bk])
            mm(bk[:, 8:16], TRIb[:], a_all[:, c, 8:16], True, True, [TRIb, a_all], [bk])
            mm(bk[:, 16:32], onesf[:], a_all[:, c, :], True, True, [onesf, a_all], [bk])
            cp("act", csT[:, c, :], bk[:, 0:32], [bk], [csT])
            act(sc16[:, 0, :], bk[:, 16:32], AF.Exp, [bk], [sc16])
            d4 = sc16[:, 0, :].rearrange("p (d g h) -> p d g h", d=2, g=2)
            cp("dve", decG[0:64, c, :, :], d4[0:64, :, 0, :], [sc16], [decG])
            cp("dve", decG[64:128, c, :, :], d4[64:128, :, 1, :], [sc16], [decG])
            tt_("dve", sc16[:, 1, :], csT[:, c, 16:32], csT[:, c, 0:16], ALU.subtract, [csT], [sc16])
            tt_("dve", sc16[:, 1, :], sc16[:, 1, :], lndt[:, c, :], ALU.add, [sc16, lndt], [sc16])
            act(w_all[:, c, :], sc16[:, 1, :], AF.Exp, [sc16], [w_all])

        xws = [A("xw%d" % i, [128, 512], BF16) for i in range(2)]
        Sd = [Buf(S.t[:, d_, :], P.res("Sdir")) for d_ in range(2)]
        P.barrier()
        for step in range(8):
            for d_ in range(2):
                c = step if d_ == 0 else 7 - step
                Sx, xw = Sd[d_], xws[d_]
                if step > 0:
                    ts_("dve", Sx[:], Sx[:], keep[:, d_ * 8 + c:d_ * 8 + c + 1], None, ALU.mult, None, [Sx, keep], [Sx])
                cp("act", Sin[:, c, d_, :], Sx[:], [Sx], [Sin])
                tt_("dve", xw[:].rearrange("p (h q) -> p h q", q=64),
                    x_tok[:, c, :].rearrange("p (h q) -> p h q", q=64),
                    w_all[:, c, d_ * 8:d_ * 8 + 8].unsqueeze(2).broadcast_to([128, 8, 64]), ALU.mult,
                    [x_tok, w_all], [xw])
                bk = sbank()
                for g in range(2):
                    mm(bk[g * 64:(g + 1) * 64, 0:256], B_tok[:, c, g * 64:(g + 1) * 64], xw[:, g * 256:(g + 1) * 256],
                       True, True, [B_tok, xw], [bk])
                tt_("dve", Sx[:].rearrange("p (h q) -> p h q", q=64),
                    Sx[:].rearrange("p (h q) -> p h q", q=64),
                    decG[:, c, d_, :].unsqueeze(2).broadcast_to([128, 4, 64]), ALU.mult, [Sx, decG], [Sx])
                tt_("dve", Sx[:], Sx[:], bk[:, 0:256], ALU.add, [Sx, bk], [Sx])
                if (d_ == 0 and c % 2 == 1) or (d_ == 1 and c % 2 == 0):
                    DMA("sp", O["o_st"][l, c // 2, d_], Sx[:], reads=[Sx], is_output=True)

        chk(4)
        yph_off = ar["off"]
        arep = A("arep", [128, 16, 128], F32)
        cbT = A("cbT", [128, 2, 128], F32)
        LwD = A("LwD", [128, 2, 4, 128], F32)
        MT = A("MT", [128, 2, 128], BF16)
        Eall = A("Eall", [128, 16, 128], BF16)
        Cp = A("Cp", [128, 4, 128], BF16)
        ygate_off = ar["off"]
        ygf = yA
        for c in range(8):
            pump(2)
            cs_ = slice(c * 128, (c + 1) * 128)
            cp("dve", arep[:], a_all[:, c, :].unsqueeze(2).broadcast_to([128, 16, 128]), [a_all], [arep])
            reps = []
            for q in range(4):
                bk = hbank()
                for jj in range(4):
                    j = q * 4 + jj
                    mm(bk[:, jj * 128:(jj + 1) * 128], arep[:, j, :], (TRIf if j < 8 else TRIb)[:], True, True,
                       [arep, TRIf, TRIb], [bk])
                reps.append(bk)
                act(Eall[:, q * 4:(q + 1) * 4, :], bk[:].rearrange("p (j l) -> p j l", l=128), AF.Exp, [bk], [Eall])
            for g in range(2):
                bk = sbank()
                mm(bk[:, 0:128], xc[g * 64:(g + 1) * 64, 4, cs_], xc[g * 64:(g + 1) * 64, 5, cs_], True, True, [xc], [bk])
                cp("act", cbT[:, g, :], bk[:, 0:128], [bk], [cbT])
            ybks = {}

            def S12(h):
                par = h % 2
                rf = reps[h // 4][:, (h % 4) * 128:(h % 4 + 1) * 128]
                rb = reps[2 + h // 4][:, (h % 4) * 128:(h % 4 + 1) * 128]
                stt(LwD[:, par, 0, :], rf, csT[:, c, h:h + 1], maskf[:], ALU.subtract, ALU.add,
                    [reps[h // 4], csT, maskf], [LwD])
                stt(LwD[:, par, 2, :], rb, csT[:, c, 8 + h:9 + h], maskb[:], ALU.subtract, ALU.add,
                    [reps[2 + h // 4], csT, maskb], [LwD])
                act(LwD[:, par, 1, :], LwD[:, par, 0, :], AF.Exp, [LwD, lndt], [LwD], bias=lndt[:, c, h:h + 1])
                act(LwD[:, par, 3, :], LwD[:, par, 2, :], AF.Exp, [LwD, lndt], [LwD], bias=lndt[:, c, 8 + h:9 + h])

            def S34(h):
                par = h % 2
                pr, hh = h // 2, h % 2
                g, h4 = h // 4, h % 4
                gs = slice(g * 64, (g + 1) * 64)
                tt_("dve", LwD[:, par, 1, :], LwD[:, par, 1, :], LwD[:, par, 3, :], ALU.add, [LwD], [LwD])
                tt_("dve", MT[:, par, :], LwD[:, par, 1, :], cbT[:, g, :], ALU.mult, [LwD, cbT], [MT])
                tt_("dve", Cp[gs, par * 2, :], xc[gs, 5, cs_], Eall[gs, h, :], ALU.mult, [xc, Eall], [Cp])
                tt_("dve", Cp[gs, par * 2 + 1, :], xc[gs, 5, cs_], Eall[gs, 8 + h, :], ALU.mult, [xc, Eall], [Cp])
                if hh == 0:
                    ybks[pr] = sbank()
                ybk = ybks[pr]
                yo = ybk[hh * 64:(hh + 1) * 64, 0:128]
                mm(yo, x_tok[:, c, h * 64:(h + 1) * 64], MT[:, par, :], True, False, [x_tok, MT], [ybk])
                mm(yo, Sin[gs, c, 0, h4 * 64:(h4 + 1) * 64], Cp[gs, par * 2, :], False, False, [Sin, Cp], [ybk])
                mm(yo, Sin[gs, c, 1, h4 * 64:(h4 + 1) * 64], Cp[gs, par * 2 + 1, :], False, True, [Sin, Cp], [ybk])
                if hh == 1:
                    stt(ygf[:, pr, cs_], xc[:, pr, cs_], dskb[:, pr:pr + 1], ybk[:, 0:128], ALU.mult, ALU.add,
                        [xc, dskb, ybk], [ygf])

            for k in range(9):
                if k < 8:
                    S12(k)
                if k >= 1:
                    S34(k - 1)
        P.barrier()
        ar["off"] = yph_off
        yg32 = A("yg32", [128, 512], F32)
        ysq = A("ysq", [128, 512], F32)
        chk(5)
        wz = load_w(I["w_in"][l][:, OFF["z"]:OFF["z"] + 512], 512)
        for half in range(2):
            hs = slice(half * 512, (half + 1) * 512)
            sbk = hbank()
            for pr in range(4):
                bk = sbank()
                for kd in range(8):
                    mm(bk[:], wz[:, kd, pr * 128:(pr + 1) * 128], hT[:, kd, hs], kd == 0, kd == 7, [wz, hT], [bk])
                act(yg32[:], bk[:], AF.Silu, [bk], [yg32])
                tt_("dve", ygf[:, pr, hs], ygf[:, pr, hs], yg32[:], ALU.mult, [ygf, yg32], [ygf])
                tt_("dve", ysq[:], ygf[:, pr, hs], ygf[:, pr, hs], ALU.mult, [ygf], [ysq])
                mm(sbk[:], onesf[:], ysq[:], pr == 0, pr == 3, [onesf, ysq], [sbk])
            act(yg32[:], sbk[:], AF.Ln, [sbk], [yg32], bias=EPS, scale=1.0 / 512)
            act(yg32[:], yg32[:], AF.Exp, [yg32], [yg32], scale=-0.5)
            for pr in range(4):
                stt(yA[:, pr, hs], ygf[:, pr, hs], ngb[:, pr:pr + 1], yg32[:], ALU.mult, ALU.mult, [ygf, ngb, yg32], [ygf])

        chk(6)
        for br in range(2):
            P.barrier()
            ar["off"] = mix_off
            is_na = br == 0
            if br == 1:
                chk(7)
            nq = 4 if is_na else 8
            nkb = 4 if is_na else 2
            QT = A("QT", [80, nq, 1024], BF16)
            KT = A("KT", [80, nkb, 1280], BF16)
            vw = 512 if is_na else 128
            V_tok = A("V_tok", [128, 8, vw], BF16)
            V_ctx = A("V_ctx", [128, 2, vw], BF16)
            stg = A("stg", [128, 2, 512], F32)
            PT = A("PT", [128, 3, 512], BF16)
            tmpS = A("tmpS", [128, 2, 512], F32)
            rs = A("rs", [128, 512], F32)
            yout = yB if is_na else yC
            ka = I["kaug_na"] if is_na else I["kaug_g"]
            ck = I["c_nakT"] if is_na else I["c_gkT"]
            cv = I["c_nav"] if is_na else I["c_gv"]
            DMA("pool", V_ctx[:], cv[l].rearrange("(j p) c -> p j c", p=128), writes=[V_ctx])

            def fill_aug(h0_, n_):
                DMA("pool", QT[64:80, :, :], I["qaug"].unsqueeze(1).broadcast_to([16, nq, 1024]), writes=[QT])
                DMA("pool", KT[64:80, :, :], ka.unsqueeze(1).broadcast_to([16, nkb, 1280]), writes=[KT])
                DMA("pool", KT[0:64, :, 1024:1280], ck[l, h0_:h0_ + n_].rearrange("h d k -> d h k"), writes=[KT])

            if is_na:
                ttb = [A("ttb%d" % i, [128, 30 * 64], BF16) for i in range(2)]
                wqn = load_w(I["w_in"][l][:, OFF["naq"]:OFF["naq"] + 512], 512)
                wkn = load_w(I["w_in"][l][:, OFF["nak"]:OFF["nak"] + 512], 512)
                wvn = load_w(I["w_in"][l][:, OFF["nav"]:OFF["nav"] + 512], 512)
                for t in range(NT):
                    for (wsrc, dram, keepbf) in ((wkn, O["o_nak"], False), (wvn, O["o_nav"], True)):
                        bk = sbank()
                        for kd in range(8):
                            mm(bk[:], hT[:, kd, t * 128:(t + 1) * 128], wsrc[:, kd, :], kd == 0, kd == 7, [hT, wsrc], [bk])
                        si_ = 0 if not keepbf else 1
                        cp("act", stg[:, si_, :], bk[:], [bk], [stg])
                        if keepbf:
                            cp("dve", V_tok[:, t, :], bk[:], [bk], [V_tok])
                        DMA("sp", dram[l, t * 128:(t + 1) * 128, :], stg[:, si_, :], reads=[stg], is_output=True)
                groups = [list(range(0, 4)), list(range(4, 8))]
            else:
                fill_aug(0, 2)
                wqg = load_w(I["w_in"][l][:, OFF["gq"]:OFF["gq"] + 512], 512)
                wkv = load_w(I["w_in"][l][:, OFF["gk"]:OFF["gk"] + 256], 256)
                gq = A("gq", [128, 512], F32)
                gsq = A("gsq", [128, 512], F32)
                gr = A("gr", [128, 512], F32)
                qr = A("qr", [128, 512], BF16)
                g8 = A("g8", [128, 4, 8], F32)
                qnb = A("qnb", [128, 64], F32)
                knb = A("knb", [128, 64], F32)
                DMA("sp", qnb[:], I["qn"][l], writes=[qnb])
                DMA("sp", knb[:], I["kn"][l], writes=[knb])

                def norm_rope(src_ap, src_bk, nh, gain, t, cache_dram):
                    n = nh * 64
                    v3 = lambda ap: ap.rearrange("p (h q) -> p h q", q=64)
                    cp("act", gq[:, 0:n], src_ap, [src_bk], [gq])
                    act(gsq[:, 0:n], src_ap, AF.Square, [src_bk], [gsq])
                    OP("dve", lambda e: e.tensor_reduce(out=g8[:, 0, 0:nh], in_=v3(gsq[:, 0:n]), axis=AX.X, op=ALU.add),
                       [gsq], [g8])
                    act(g8[:, 1, 0:nh], g8[:, 0, 0:nh], AF.Ln, [g8], [g8], bias=EPS, scale=1.0 / 64)
                    act(g8[:, 2, 0:nh], g8[:, 1, 0:nh], AF.Exp, [g8], [g8], scale=-0.5)
                    tt_("dve", v3(gq[:, 0:n]), v3(gq[:, 0:n]), g8[:, 2, 0:nh].unsqueeze(2).broadcast_to([128, nh, 64]),
                        ALU.mult, [gq, g8], [gq])
                    tt_("dve", v3(gq[:, 0:n]), v3(gq[:, 0:n]), gain[:].unsqueeze(1).broadcast_to([128, nh, 64]),
                        ALU.mult, [gq, gain], [gq])
                    if cache_dram is not None:
                        DMA("sp", cache_dram, gq[:, 0:n], reads=[gq], is_output=True)
                    v5 = lambda ap: ap.rearrange("p (h a b c) -> p h a b c", a=2, b=2, c=16)
                    sn = sinT[:, t, :].rearrange("p (a b c) -> p a b c", a=2, b=2)
                    for b_ in range(2):
                        tt_("dve", v5(gr[:, 0:n])[:, :, :, b_, :], v5(gq[:, 0:n])[:, :, :, 1 - b_, :],
                            sn[:, :, b_, :].unsqueeze(1).broadcast_to([128, nh, 2, 16]), ALU.mult, [gq, sinT], [gr])
                    tt_("dve", v3(gsq[:, 0:n]), v3(gq[:, 0:n]), cosT[:, t, :].unsqueeze(1).broadcast_to([128, nh, 64]),
                        ALU.mult, [gq, cosT], [gsq])
                    tt_("dve", qr[:, 0:n], gsq[:, 0:n], gr[:, 0:n], ALU.add, [gsq, gr], [qr])

                for t in range(NT):
                    ts = slice(t * 128, (t + 1) * 128)
                    bk = sbank()
                    for kd in range(8):
                        mm(bk[:], hT[:, kd, ts], wqg[:, kd, :], kd == 0, kd == 7, [hT, wqg], [bk])
                    norm_rope(bk[:, 0:512], bk, 8, qnb, t, None)
                    bk2 = sbank()
                    b2 = bk2[:].bitcast(BF16)
                    for h in range(8):
                        tp(b2[0:64, h * 128:(h + 1) * 128], qr[:, h * 64:(h + 1) * 64], ident[:], [qr, ident], [bk2])
                    cp("act", QT[0:64, :, ts], b2[0:64, :].rearrange("p (h t) -> p h t", t=128), [bk2], [QT])
                    bk = sbank()
                    for kd in range(8):
                        mm(bk[:, 0:256], hT[:, kd, ts], wkv[:, kd, 0:256], kd == 0, kd == 7, [hT, wkv], [bk])
                    cp("act", stg[:, 1, 0:128], bk[:, 128:256], [bk], [stg])
                    cp("dve", V_tok[:, t, :], bk[:, 128:256], [bk], [V_tok])
                    DMA("sp", O["o_gv"][l, ts, :], stg[:, 1, 0:128], reads=[stg], is_output=True)
                    norm_rope(bk[:, 0:128], bk, 2, knb, t, O["o_gk"][l, ts, :])
                    bk2 = sbank()
                    b2 = bk2[:].bitcast(BF16)
                    for h in range(2):
                        tp(b2[0:64, h * 128:(h + 1) * 128], qr[:, h * 64:(h + 1) * 64], ident[:], [qr, ident], [bk2])
                    cp("act", KT[0:64, 0:2, ts], b2[0:64, 0:256].rearrange("p (h t) -> p h t", t=128), [bk2], [KT])
                groups = [list(range(8))]

            if is_na:
                chk(61)
            else:
                chk(71)
            pcnt = 0
            for heads in groups:
                if is_na:
                    fill_aug(heads[0], 4)
                    for (wsrc, dst) in ((wqn, QT), (wkn, KT)):
                        for hi, h in enumerate(heads):
                            for half in range(2):
                                bk = sbank()
                                for kd in range(8):
                                    mm(bk[0:64, :], wsrc[:, kd, h * 64:(h + 1) * 64], hT[:, kd, half * 512:(half + 1) * 512],
                                       kd == 0, kd == 7, [wsrc, hT], [bk])
                                cp("act" if half == 0 else "dve", dst[0:64, hi, half * 512:(half + 1) * 512], bk[0:64, :],
                                   [bk], [dst])
                steps = [(hi, h, qc, kt) for hi, h in enumerate(heads) for qc in range(2) for kt in range(10)]
                sbk_of = {}
                acc_of = {}

                def issue_S(i):
                    hi, h, qc, kt = steps[i]
                    kvb = hi if is_na else h // 4
                    if is_na and qc == 0 and kt == 0:
                        tb = ttb[h % 2]
                        DMA("pool", tb[:], I["tt"][l, h], writes=[tb])
                    sbk = sbank()
                    mm(sbk[:], KT[0:80, kvb, kt * 128:(kt + 1) * 128], QT[0:80, hi, qc * 512:(qc + 1) * 512], True, True,
                       [KT, QT], [sbk])
                    sbk_of[i] = sbk

                for i in range(min(2, len(steps))):
                    issue_S(i)
                for i, (hi, h, qc, kt) in enumerate(steps):
                    if kt == 0:
                        pump(1)
                    if i + 2 < len(steps):
                        issue_S(i + 2)
                    kv = h if is_na else h // 4
                    ps_ = slice((h % 2) * 64, (h % 2) * 64 + 64)
                    qs = slice(qc * 512, (qc + 1) * 512)
                    if kt == 0:
                        acc_of[(h, qc)] = (hbank(), hbank())
                    ob, sb_ = acc_of[(h, qc)]
                    sbk = sbk_of.pop(i)
                    p2 = i % 3
                    if is_na and kt < 8:
                        tb = ttb[h % 2]
                        e0 = qc * 8 - 2 * kt + 14
                        stt(tmpS[:, i % 2, :], sbk[:], 0.125, tb[:, e0 * 64:(e0 + 8) * 64], ALU.mult, ALU.add,
                            [sbk, tb], [tmpS])
                        act(PT[:, p2, :], tmpS[:, i % 2, :], AF.Exp, [tmpS], [PT])
                    else:
                        act(PT[:, p2, :], sbk[:], AF.Exp, [sbk], [PT], scale=0.125)
                    if kt < 8:
                        vop = V_tok[:, kt, kv * 64:(kv + 1) * 64]
                        vb = V_tok
                    else:
                        vop = V_ctx[:, kt - 8, kv * 64:(kv + 1) * 64]
                        vb = V_ctx
                    mm(ob[ps_, :], vop, PT[:, p2, :], kt == 0, kt == 9, [vb, PT], [ob])
                    mm(sb_[ps_, :], onesb[:, 0:64], PT[:, p2, :], kt == 0, kt == 9, [onesb, PT], [sb_])
                    if kt == 9:
                        act(rs[ps_, :], sb_[ps_, :], AF.Ln, [sb_], [rs])
                        act(rs[ps_, :], rs[ps_, :], AF.Exp, [rs], [rs], scale=-1.0)
                        tt_("dve", yout[ps_, h // 2, qs], ob[ps_, :], rs[ps_, :], ALU.mult, [ob, rs], [yout])

        chk(8)
        P.barrier()
        ar["off"] = mix_off
        mT = A("mT", [128, 8, 1024], BF16)
        Wg = [[A("Wg%d_%d" % (i, j), [128, 8, 128], BF16) for j in range(3)] for i in range(2)]
        Wb = [[A("Wb%d_%d" % (i, j), [128, 4, 128], BF16) for j in range(3)] for i in range(2)]
        sg = A("sg", [128, 3, 512], F32)
        mtmp = A("mtmp", [128, 2, 512], F32)
        ys = (yA, yB, yC)
        for dc in range(8):
            pump(2)
            wg_, wb_ = Wg[dc % 2], Wb[dc % 2]
            for i in range(3):
                c0 = OFF["gate"] + i * 1024 + dc * 128
                DMA("pool", wg_[i][:], I["w_in"][l][:, c0:c0 + 128].rearrange("(k p) c -> p k c", p=128),
                    writes=[wg_[i]])
                DMA("pool", wb_[i][:], I["w_branch"][l, i][:, dc * 128:(dc + 1) * 128].rearrange("(k p) c -> p k c", p=128),
                    writes=[wb_[i]])
            for half in range(2):
                hs = slice(half * 512, (half + 1) * 512)
                for i in range(3):
                    gb = sbank()
                    for kd in range(8):
                        mm(gb[:], wg_[i][:, kd, :], hT[:, kd, hs], kd == 0, kd == 7, [wg_[i], hT], [gb])
                    act(sg[:, i, :], gb[:], AF.Sigmoid, [gb], [sg])
                    pb = sbank()
                    for e4 in range(4):
                        mm(pb[:], wb_[i][:, e4, :], ys[i][:, e4, hs], e4 == 0, e4 == 3, [wb_[i], ys[i]], [pb])
                    if i == 0:
                        tt_("dve", mtmp[:, 0, :], pb[:], sg[:, i, :], ALU.mult, [pb, sg], [mtmp])
                    else:
                        tt_("dve", mtmp[:, 1, :], pb[:], sg[:, i, :], ALU.mult, [pb, sg], [mtmp])
                        if i == 1:
                            tt_("dve", mtmp[:, 0, :], mtmp[:, 0, :], mtmp[:, 1, :], ALU.add, [mtmp], [mtmp])
                        else:
                            tt_("dve", mT[:, dc, hs], mtmp[:, 0, :], mtmp[:, 1, :], ALU.add, [mtmp], [mT])
        wo = [load_w(I["w_out"][l][:, hf * 512:(hf + 1) * 512], 512) for hf in range(2)]
        DMA("sp", lnp[:, 0, :], I["lnp"][l, 0], writes=[lnp])
        DMA("sp", lnp[:, 1, :], I["lnp"][l, 1], writes=[lnp])
        pre = A("pre", [128, D], F32)
        for t in range(NT):
            ts = slice(t * 128, (t + 1) * 128)
            for hf in range(2):
                bk = sbank()
                for dc in range(8):
                    mm(bk[:], mT[:, dc, ts], wo[hf][:, dc, :], dc == 0, dc == 7, [mT, wo[hf]], [bk])
                tt_("dve", pre[:, hf * 512:(hf + 1) * 512], bk[:], mrep[:, 2048 + hf * 512:2048 + (hf + 1) * 512], ALU.mult,
                    [bk, mrep], [pre])
            stt(pre[:], x[:, t, :], DN_ALPHA, pre[:], ALU.mult, ALU.add, [x, pre], [pre])
            mean, rstd = ln_stats(pre[:], pre)
            ts_("dve", pre[:], pre[:], mean, rstd, ALU.subtract, ALU.mult, [pre, small], [pre])
            tt_("dve", pre[:], pre[:], lnp[:, 0, :], ALU.mult, [pre, lnp], [pre])
            tt_("dve", x[:, t, :], pre[:], lnp[:, 1, :], ALU.add, [pre, lnp], [x])
        if l == 0:
            tap("x1", x, x[:], [128, NT, D])

        chk(9)
        stage_begin()
        h2tok = A("h2tok", [128, NT, D], BF16)
        e_idx = A("e_idx", [128, NT, 128], I32)
        g_all = A("g_all", [128, NT, 128], F32)
        asel = A("asel", [128, NT, 128], F32)
        peer_off = ar["off"]
        tmpA = A("tmpA2", [128, D], F32)
        ln_modulate(3072, 4096, tmpA, None, h2tok)
        P.barrier()
        ar["off"] = peer_off
        keysT = A("keysT", [128, 16, 128], BF16)
        qTh = A("qTh", [128, 16, 512], BF16)
        sv = A("sv", [128, 16, 16], F32)
        si = A("si", [128, 16, 16], U32)
        sif = A("sif", [128, 16, 16], F32)
        wk16 = A("wk16", [128, 16, 128], F32)
        wk8 = A("wk8", [128, 8, 256], F32)
        oh = Buf(wk16.t.rearrange("p a b -> p (a b)").rearrange("p (h i j) -> p h i j", h=8, i=16), wk16.r)
        cand = A("cand", [128, 8, 16, 16], F32)
        cvv = A("cvv", [128, 8, 16], F32)
        ci = A("ci", [128, 8, 16], U32)
        cij = A("cij", [128, 2, 8, 16], U32)
        cijf = A("cijf", [128, 2, 8, 16], F32)
        k01 = A("k01", [128, 2, 8, 16], F32)
        g8p = A("g8p", [128, 2, 8], F32)
        DMA("pool", keysT[:], I["keysT"][l].rearrange("h d k -> d h k"), writes=[keysT])
        for half in range(2):
            hs = slice(half * 512, (half + 1) * 512)
            for qb in range(4):
                wqb = load_w(I["wq"][l][:, qb * 512:(qb + 1) * 512], 512)
                for j in range(4):
                    bk = sbank()
                    for kd in range(8):
                        mm(bk[:], wqb[:, kd, j * 128:(j + 1) * 128], hT[:, kd, hs], kd == 0, kd == 7, [wqb, hT], [bk])
                    cp("act", qTh[:, qb * 4 + j, :], bk[:], [bk], [qTh])
            for tl in range(4):
                t = half * 4 + tl
                for grp in range(4):
                    bk = sbank()
                    for j in range(4):
                        hp = grp * 4 + j
                        mm(bk[:, j * 128:(j + 1) * 128], qTh[:, hp, tl * 128:(tl + 1) * 128], keysT[:, hp, :], True, True,
                           [qTh, keysT], [bk])
                    srcs = [(grp * 4 + j, bk[:, j * 128:(j + 1) * 128]) for j in range(4)]
                    for hp, src in srcs:
                        OP("dve", lambda e, src=src, hp=hp: e.max(out=sv[:, hp, 0:8], in_=src), [bk], [sv])
                    for hp, src in srcs:
                        OP("dve", lambda e, src=src, hp=hp: e.max_index(out=si[:, hp, 0:8], in_max=sv[:, hp, 0:8], in_values=src),
                           [bk, sv], [si])
                    for hp, src in srcs:
                        OP("dve", lambda e, src=src, hp=hp: e.match_replace(out=wk16[:, hp, :], in_to_replace=sv[:, hp, 0:8],
                                                                            in_values=src, imm_value=-1e30), [bk, sv], [wk16])
                    for hp, src in srcs:
                        OP("dve", lambda e, hp=hp: e.max(out=sv[:, hp, 8:16], in_=wk16[:, hp, :]), [wk16], [sv])
                    for hp, src in srcs:
                        OP("dve", lambda e, hp=hp: e.max_index(out=si[:, hp, 8:16], in_max=sv[:, hp, 8:16], in_values=wk16[:, hp, :]),
                           [wk16, sv], [si])
                cp("dve", sif[:], si[:], [si], [sif])
                sv4 = sv[:].rearrange("p (h q) k -> p h q k", q=2)
                sif4 = sif[:].rearrange("p (h q) k -> p h q k", q=2)
                tt_("dve", cand[:], sv4[:, :, 0, :].unsqueeze(3).broadcast_to([128, 8, 16, 16]),
                    sv4[:, :, 1, :].unsqueeze(2).broadcast_to([128, 8, 16, 16]), ALU.add, [sv], [cand])
                c2s = [cand[:, h, :, :].rearrange("p a b -> p (a b)") for h in range(8)]
                for h in range(8):
                    OP("dve", lambda e, c2=c2s[h], h=h: e.max(out=cvv[:, h, 0:8], in_=c2), [cand], [cvv])
                for h in range(8):
                    OP("dve", lambda e, c2=c2s[h], h=h: e.max_index(out=ci[:, h, 0:8], in_max=cvv[:, h, 0:8], in_values=c2),
                       [cand, cvv], [ci])
                for h in range(8):
                    OP("dve", lambda e, c2=c2s[h], h=h: e.match_replace(out=wk8[:, h, :], in_to_replace=cvv[:, h, 0:8], in_values=c2,
                                                                         imm_value=-1e30), [cand, cvv], [wk8])
                for h in range(8):
                    OP("dve", lambda e, h=h: e.max(out=cvv[:, h, 8:16], in_=wk8[:, h, :]), [wk8], [cvv])
                for h in range(8):
                    OP("dve", lambda e, h=h: e.max_index(out=ci[:, h, 8:16], in_max=cvv[:, h, 8:16], in_values=wk8[:, h, :]),
                       [wk8, cvv], [ci])
                OP("dve", lambda e: e.tensor_single_scalar(out=cij[:, 0, :, :], in_=ci[:], scalar=4, op=ALU.logical_shift_right),
                   [ci], [cij])
                OP("dve", lambda e: e.tensor_single_scalar(out=cij[:, 1, :, :], in_=ci[:], scalar=15, op=ALU.bitwise_and),
                   [ci], [cij])
                cp("dve", cijf[:], cij[:], [cij], [cijf])
                for q in range(2):
                    tt_("dve", oh[:], cijf[:, q, :, :].unsqueeze(3).broadcast_to([128, 8, 16, 16]),
                        iota16[:].unsqueeze(1).unsqueeze(1).broadcast_to([128, 8, 16, 16]), ALU.is_equal, [cijf, iota16], [oh])
                    tt_("dve", oh[:], oh[:], sif4[:, :, q, :].unsqueeze(2).broadcast_to([128, 8, 16, 16]), ALU.mult,
                        [oh, sif], [oh])
                    OP("dve", lambda e, q=q: e.tensor_reduce(out=k01[:, q, :, :], in_=oh[:], axis=AX.X, op=ALU.add), [oh], [k01])
                stt(k01[:, 0, :, :], k01[:, 0, :, :], 128.0, k01[:, 1, :, :], ALU.mult, ALU.add, [k01], [k01])
                cp("dve", e_idx[:, t, :].rearrange("p (h k) -> p h k", k=16), k01[:, 0, :, :], [k01], [e_idx])
                tt_("dve", cvv[:], cvv[:], cvv[:, :, 0:1].broadcast_to([128, 8, 16]), ALU.subtract, [cvv], [cvv])
                act(cvv[:], cvv[:], AF.Exp, [cvv], [cvv])
                OP("dve", lambda e: e.tensor_reduce(out=g8p[:, 0, :], in_=cvv[:], axis=AX.X, op=ALU.add), [cvv], [g8p])
                OP("dve", lambda e: e.reciprocal(out=g8p[:, 1, :], in_=g8p[:, 0, :]), [g8p], [g8p])
                tt_("dve", g_all[:, t, :].rearrange("p (h k) -> p h k", k=16), cvv[:],
                    g8p[:, 1, :].unsqueeze(2).broadcast_to([128, 8, 16]), ALU.mult, [cvv, g8p], [g_all])
        if l == 0:
            tap("e_idx", e_idx, e_idx[:], [128, NT, 128], I32)
            tap("g_all", g_all, g_all[:], [128, NT, 128])

        chk(11)
        P.barrier()
        ar["off"] = peer_off
        pump(128)
        NG = 10
        gb_ = [A("gbuf%d" % i, [128, 2 * D], BF16) for i in range(NG)]
        junks = [A("junk%d" % i, [128, D], BF16) for i in range(2)]
        dgs = [A("dg%d" % i, [128, 2, 128], BF16) for i in range(4)]
        asg = [A("asg%d" % i, [128, 2], F32) for i in range(8)]
        awg = [A("awg%d" % i, [128, 2], F32) for i in range(8)]
        pre = A("pre2", [128, D], F32)
        DMA("sp", lnp[:, 0, :], I["lnp"][l, 2], writes=[lnp])
        DMA("sp", lnp[:, 1, :], I["lnp"][l, 3], writes=[lnp])

        def finish_v(t, a0, a1):
            for hf, a_ in ((0, a0), (1, a1)):
                tt_("dve", pre[:, hf * 512:(hf + 1) * 512], a_[:], mrep[:, 5120 + hf * 512:5120 + (hf + 1) * 512], ALU.mult,
                    [a_, mrep], [pre])
            stt(pre[:], x[:, t, :], DN_ALPHA, pre[:], ALU.mult, ALU.add, [x, pre], [pre])
            mean, rstd = ln_stats(pre[:], pre)
            ts_("dve", pre[:], pre[:], mean, rstd, ALU.subtract, ALU.mult, [pre, small], [pre])
            tt_("dve", pre[:], pre[:], lnp[:, 0, :], ALU.mult, [pre, lnp], [pre])
            tt_("dve", x[:, t, :], pre[:], lnp[:, 1, :], ALU.add, [pre, lnp], [x])

        NSTEP = NT * 64
        accs = {}

        def st_A(i):
            t, q = divmod(i, 64)
            as_ = asg[i % 8]
            for j in range(2):
                s_ = q * 2 + j
                gb = gb_[(2 * i + j) % NG]
                P.dma("pool", lambda e, gb=gb, s_=s_, t=t: e.indirect_dma_start(
                    out=gb[:], out_offset=None, in_=uvb[l],
                    in_offset=bass.IndirectOffsetOnAxis(ap=e_idx[:, t, s_:s_ + 1], axis=0)),
                    [e_idx.r, uvbuf[l].r], [gb.r])
            for j in range(2):
                jk = junks[j]
                gb = gb_[(2 * i + j) % NG]
                stt(jk[:], gb[:, 0:D], 1.0, h2tok[:, t, :], ALU.mult, ALU.mult, [gb, h2tok], [jk, as_],
                    accum_out=as_[:, j:j + 1])

        def st_B(i):
            act(awg[i % 8][:], asg[i % 8][:], AF.Gelu, [asg[i % 8]], [awg[i % 8]])

        def st_CDE(i):
            t, q = divmod(i, 64)
            aw_, dg = awg[i % 8], dgs[i % 4]
            if q == 0:
                accs[t] = (hbank(), hbank())
            a0, a1 = accs[t]
            tt_("dve", aw_[:], aw_[:], g_all[:, t, q * 2:q * 2 + 2], ALU.mult, [aw_, g_all], [aw_])
            for j in range(2):
                act(dg[:, j, :], ident[:], AF.Identity, [ident, aw_], [dg], scale=aw_[:, j:j + 1])
            for j in range(2):
                s_ = q * 2 + j
                gb = gb_[(2 * i + j) % NG]
                mm(a0[:], dg[:, j, :], gb[:, D:D + 512], s_ == 0, s_ == 127, [dg, gb], [a0])
                mm(a1[:], dg[:, j, :], gb[:, D + 512:2 * D], s_ == 0, s_ == 127, [dg, gb], [a1])
            if q == 63:
                finish_v(t, a0, a1)

        for i in range(NSTEP + 2):
            if i < NSTEP:
                st_A(i)
            if 0 <= i - 1 < NSTEP:
                st_B(i - 1)
            if 0 <= i - 2 < NSTEP:
                st_CDE(i - 2)
        if l == 0:
            tap("x2", x, x[:], [128, NT, D])

    try:
        for l in range(NL):
            layer_body(l)
    except _Stop:
        pass

    P.barrier()
    for t in range(NT):
        DMA("sp", O["y"][t * 128:(t + 1) * 128, :], x[:, t, :], reads=[x], is_output=True)
    P.finish()
    return tap_out


def _na_tables():
    rows, kr = 16, 8
    r = np.arange(rows)
    start = np.clip(r - kr // 2, 0, rows - kr)
    rowvalid = np.zeros((rows, rows), bool)
    for i in range(rows):
        rowvalid[i, start[i]:start[i] + kr] = True
    qc = np.arange(64)
    qstart = np.clip(qc - 8, 0, 48)
    kc = np.arange(64)
    colvalid = (kc[None, :] >= qstart[:, None]) & (kc[None, :] < qstart[:, None] + 16)
    return rowvalid, colvalid


def host_inputs(inp, NL=DEPTH):
    f = np.float32
    rowvalid, colvalid = _na_tables()
    com = {}
    com["w_mod"] = np.ascontiguousarray(inp["w_mod"][:NL])
    com["b_mod"] = np.ascontiguousarray(inp["b_mod"][:NL].reshape(NL, 1, 6144))
    com["w_in"] = np.ascontiguousarray(inp["w_in"][:NL])
    cw = inp["conv_w"][:NL]
    com["cw"] = np.ascontiguousarray(cw.reshape(NL, 5, 6, 128).transpose(0, 3, 2, 1))
    com["cb"] = np.ascontiguousarray(inp["conv_b"][:NL].reshape(NL, 6, 128).transpose(0, 2, 1))
    com["alog"] = np.ascontiguousarray(np.broadcast_to(inp["ssd_a_log"][:NL].reshape(NL, 1, 16), (NL, 128, 16)))
    com["dtb"] = np.ascontiguousarray(np.broadcast_to(inp["ssd_dt_bias"][:NL].reshape(NL, 1, 16), (NL, 128, 16)))
    dsk = np.zeros((NL, 128, 4), f)
    for h in range(8):
        dsk[:, (h % 2) * 64:(h % 2) * 64 + 64, h // 2] = inp["ssd_d"][:NL, h][:, None]
    com["dsk"] = dsk
    com["ng"] = np.ascontiguousarray(inp["ssd_norm_g"][:NL].reshape(NL, 4, 128).transpose(0, 2, 1))
    com["qn"] = np.ascontiguousarray(np.broadcast_to(inp["gqa_q_norm"][:NL][:, None, :], (NL, 128, 64)))
    com["kn"] = np.ascontiguousarray(np.broadcast_to(inp["gqa_k_norm"][:NL][:, None, :], (NL, 128, 64)))
    lnp = np.stack([inp["ln1_g"][:NL], inp["ln1_b"][:NL], inp["ln2_g"][:NL], inp["ln2_b"][:NL]], 1)
    com["lnp"] = np.ascontiguousarray(np.broadcast_to(lnp[:, :, None, :], (NL, 4, 128, D)))
    com["w_branch"] = np.ascontiguousarray(inp["w_branch"][:NL])
    com["w_out"] = np.ascontiguousarray(inp["w_out"][:NL])
    com["wq"] = np.ascontiguousarray(inp["peer_wq"][:NL])
    com["keysT"] = np.ascontiguousarray(inp["peer_keys"][:NL].reshape(NL, 16, 128, 128).transpose(0, 1, 3, 2))
    for i in range(NL):
        com["uv%d" % i] = np.ascontiguousarray(np.concatenate([inp["peer_u"][i], inp["peer_v"][i]], axis=1))
    t = np.arange(1024)
    qaug = (t[None, :] // 64 == np.arange(16)[:, None]).astype(f)
    com["qaug"] = qaug
    rpb = inp["na_rpb"][:NL]
    tts = np.zeros((NL, 8, 2, 64, 30, 64), f)
    kc = np.arange(64)[:, None]
    qc = np.arange(64)[None, :]
    dc = np.clip(kc - qc + 15, 0, 30)
    band = colvalid.T
    for p2 in range(2):
        for e in range(30):
            dr = p2 - (e - 14)
            if -7 <= dr <= 7:
                tile = rpb[:, :, dr + 7, :][:, :, dc]
                tts[:, :, p2, :, e, :] = np.where(band[None, None], tile, f(NEG))
    tts = tts.reshape(NL, 8, 128, 30 * 64)
    ttp = np.zeros_like(tts)

    half = 32
    inv = 1.0 / (10000.0 ** (np.arange(0, half, 2, dtype=np.float32) / half))

    maps = []
    for core in range(8):
        m = dict(com)
        sample = core >= 4
        if sample:
            b = core - 4
            m["x0"] = np.ascontiguousarray(inp["x_sample"][b])
            cond = inp["c"][b]
            m["c_nakT"] = np.ascontiguousarray(inp["cache_na_k"][b, :NL].transpose(0, 2, 3, 1))
            m["c_nav"] = np.ascontiguousarray(inp["cache_na_v"][b, :NL].reshape(NL, 256, 512))
            m["c_gkT"] = np.ascontiguousarray(inp["cache_gqa_k"][b, :NL].transpose(0, 2, 3, 1))
            m["c_gv"] = np.ascontiguousarray(inp["cache_gqa_v"][b, :NL].reshape(NL, 256, 128))
            st = inp["state_ssd"][b, :NL]
            st = st.reshape(NL, 2, 2, 4, 64, 64).transpose(0, 1, 2, 5, 3, 4)
            m["h0"] = np.ascontiguousarray(st.reshape(NL, 2, 128, 256))
            m["tt"] = tts
            rm_na = np.where(rowvalid.T, 0.0, NEG * 8).astype(f)
            rm_g = np.zeros((16, 16), f)
            ctxv = 0.0
            pos_r = (t // 64).astype(f)
            pos_c = (t % 64).astype(f)
            cos = np.ones((1024, 64), f)
            sin = np.zeros((1024, 64), f)
            for a, pos in enumerate((pos_r, pos_c)):
                ang = pos[:, None] * inv[None, :]
                cos[:, a * 32:a * 32 + 16] = np.cos(ang)
                cos[:, a * 32 + 16:a * 32 + 32] = np.cos(ang)
                sin[:, a * 32:a * 32 + 16] = -np.sin(ang)
                sin[:, a * 32 + 16:a * 32 + 32] = np.sin(ang)
            keep = np.ones((128, 16), f)
            cm = np.ones((128, 1), f)
        else:
            m["x0"] = np.ascontiguousarray(inp["x_prompt"][core * 4:(core + 1) * 4].reshape(1024, D))
            cond = inp["c_ctx"]
            m["c_nakT"] = np.zeros((NL, 8, 64, 256), f)
            m["c_nav"] = np.zeros((NL, 256, 512), f)
            m["c_gkT"] = np.zeros((NL, 2, 64, 256), f)
            m["c_gv"] = np.zeros((NL, 256, 128), f)
            m["h0"] = np.zeros((NL, 2, 128, 256), f)
            m["tt"] = ttp
            seq = np.arange(16) // 4
            rm_na = np.where(seq[:, None] == seq[None, :], 0.0, NEG * 8).astype(f)
            rm_g = rm_na
            ctxv = NEG * 8
            cos = np.ones((1024, 64), f)
            sin = np.zeros((1024, 64), f)
            keep = np.ones((128, 16), f)
            keep[:, [0, 2, 4, 6]] = 0.0
            keep[:, [8 + 1, 8 + 3, 8 + 5, 8 + 7]] = 0.0
            cm = np.zeros((128, 1), f)
        m["cond_rep"] = np.ascontiguousarray(np.broadcast_to(cond.reshape(8, 128).T[:, :, None], (128, 8, 128)))
        for nm, rm in (("kaug_na", rm_na), ("kaug_g", rm_g)):
            ka = np.zeros((16, 1280), f)
            ka[:, :1024] = rm[t // 64, :].T
            ka[:, 1024:] = ctxv
            m[nm] = ka
        m["cosT"] = np.ascontiguousarray(cos.reshape(8, 128, 64).transpose(1, 0, 2))
        m["sinT"] = np.ascontiguousarray(sin.reshape(8, 128, 64).transpose(1, 0, 2))
        m["keep"] = keep
        m["cmask"] = cm
        maps.append({k: np.ascontiguousarray(v, dtype=np.float32) for k, v in m.items()})
    return maps


def assemble(results, NL=DEPTH):
    f = np.float32
    y_p = np.concatenate([results[c]["y"].reshape(4, 256, D) for c in range(4)], 0)
    y_s = np.stack([results[c]["y"] for c in range(4, 8)], 0)

    def cache(name, nh):
        parts = []
        for c in range(4):
            a = results[c][name].reshape(NL, 4, 256, nh, 64).transpose(1, 0, 2, 3, 4)
            parts.append(a)
        return np.ascontiguousarray(np.concatenate(parts, 0), dtype=f)

    nak, nav, gk, gv = cache("o_nak", 8), cache("o_nav", 8), cache("o_gk", 2), cache("o_gv", 2)
    sts = []
    for c in range(4):
        a = results[c]["o_st"].reshape(NL, 4, 2, 2, 64, 4, 64)
        a = a.transpose(1, 0, 2, 3, 5, 6, 4).reshape(4, NL, 2, 8, 64, 64)
        sts.append(a)
    st = np.ascontiguousarray(np.concatenate(sts, 0), dtype=f)
    return (y_p.astype(f), y_s.astype(f), nak, nav, gk, gv, st)


def kernel(**inputs):
    inp = {k: np.asarray(v) for k, v in inputs.items()}
    nc = bass.Bass("TRN2", target_bir_lowering=False)
    build(nc)
    maps = host_inputs(inp)
    res = run_bass_kernel_spmd(nc, maps, core_ids=list(range(8)))
    return assemble(res.results)
```

```python
import contextlib
import sys
import numpy as np
import concourse.bass as bass
import concourse.mybir as mybir
from concourse.alu_op_type import AluOpType as ALU
from concourse.bass_utils import run_bass_kernel_spmd

F32 = mybir.dt.float32
BF16 = mybir.dt.bfloat16
U32 = mybir.dt.uint32
I32 = mybir.dt.int32
AF = mybir.ActivationFunctionType
AX = mybir.AxisListType

D = 1024
NT = 8
DEPTH = 4
EPS = 1e-6
DN_ALPHA = (2 * DEPTH) ** 0.25
NEG = -30000.0
IN_COLS = 6672
OFF = dict(z=0, xs=512, B=1024, C=1152, dt=1280, naq=1296, nak=1808, nav=2320, gq=2832, gk=3344, gv=3472, gate=3600)


class Res:
    __slots__ = ("name", "last_w", "reads", "ch")

    def __init__(self, name):
        self.name = name
        self.last_w = None
        self.reads = {}
        self.ch = None


class Prog:
    ENG = ("pe", "act", "dve", "pool", "sp")

    def __init__(self, nc):
        self.nc = nc
        self.stack = contextlib.ExitStack()
        self.ops = {e: [] for e in self.ENG}
        self.cnt = {}
        self.sems = {}
        self.seen = {e: {} for e in self.ENG}
        self.pend = {e: {} for e in self.ENG}
        self.out_events = []
        self.nres = 0
        for e in self.ENG:
            self._sem("E_" + e)

    def _sem(self, key):
        if key not in self.sems:
            self.sems[key] = self.stack.enter_context(self.nc.semaphore("s" + key))
            self.cnt[key] = 0
        return self.sems[key]

    def sb(self, name, shape, dtype):
        return self.stack.enter_context(self.nc.sbuf_tensor("sb_" + name, list(shape), dtype))

    def ps(self, name, shape, dtype=F32):
        return self.stack.enter_context(self.nc.psum_tensor("ps_" + name, list(shape), dtype))

    def res(self, name=None):
        self.nres += 1
        return Res("%s_%d" % (name or "r", self.nres))

    def _deps(self, e, reads, writes):
        deps = dict(self.pend[e])
        self.pend[e] = {}

        def add(ev):
            if ev is None:
                return
            k, v = ev
            if deps.get(k, 0) < v:
                deps[k] = v

        for r in reads:
            add(r.last_w)
        for w in writes:
            add(w.last_w)
            for k, v in w.reads.items():
                add((k, v))
        waits = []
        seen = self.seen[e]
        own = "E_" + e
        for k, v in deps.items():
            if e == "pe" and k == own:
                continue
            if seen.get(k, 0) >= v:
                continue
            seen[k] = v
            waits.append((k, v))
        return waits

    def _commit(self, ev, reads, writes):
        k, v = ev
        for r in reads:
            if r.reads.get(k, 0) < v:
                r.reads[k] = v
        for w in writes:
            w.last_w = ev
            w.reads = {}

    def op(self, e, fn, reads=(), writes=()):
        waits = self._deps(e, reads, writes)
        key = "E_" + e
        self.cnt[key] += 1
        ev = (key, self.cnt[key])
        self.ops[e].append((waits, fn, key, 1, self._where()))
        self._commit(ev, reads, writes)
        return ev

    @staticmethod
    def _where():
        f = sys._getframe(2)
        out = []
        while f is not None and len(out) < 5:
            out.append(f.f_lineno)
            f = f.f_back
        return out

    def dma(self, q, fn, reads=(), writes=(), is_output=False):
        tgt = writes[0] if writes else reads[0]
        if tgt.ch is None:
            tgt.ch = "D_" + tgt.name.rsplit("_", 1)[0]
            self._sem(tgt.ch)
        key = tgt.ch
        waits = self._deps(q, reads, writes)
        self.cnt[key] += 16
        ev = (key, self.cnt[key])
        self.ops[q].append((waits, fn, key, 16, self._where()))
        self._commit(ev, reads, writes)
        if is_output:
            self.out_events.append(ev)
        return ev

    def barrier(self):
        snap = dict(self.cnt)
        for e in self.ENG:
            for k, v in snap.items():
                if v > 0 and self.pend[e].get(k, 0) < v:
                    self.pend[e][k] = v

    def finish(self):
        finals = {}
        for k, v in self.out_events:
            finals[k] = max(finals.get(k, 0), v)
        for e in self.ENG:
            if e != "sp" and self.cnt["E_" + e] > 0:
                finals["E_" + e] = self.cnt["E_" + e]
        final_waits = list(finals.items())
        nc, sems, ops = self.nc, self.sems, self.ops

        needed = {}
        for e in self.ENG:
            for waits, fn, key, inc, where in ops[e]:
                for k, v in waits:
                    needed.setdefault(k, set()).add(v)
        for k, v in final_waits:
            needed.setdefault(k, set()).add(v)
        remap = {}
        for e in self.ENG:
            key = "E_" + e
            need = needed.get(key, set())
            m = {}
            new_c = 0
            old_c = 0
            lst = []
            for waits, fn, k2, inc, where in ops[e]:
                if k2 == key:
                    old_c += 1
                    if old_c in need:
                        new_c += 1
                        m[old_c] = new_c
                        lst.append((waits, fn, k2, 1, where))
                    else:
                        lst.append((waits, fn, k2, 0, where))
                else:
                    lst.append((waits, fn, k2, inc, where))
            ops[e] = lst
            remap[key] = m

        def rv(k, v):
            return remap[k][v] if k in remap else v

        final_waits = [(k, rv(k, v)) for k, v in final_waits]

        def replay(eng, lst):
            for waits, fn, key, inc, where in lst:
                for k, v in waits:
                    eng.wait_ge(sems[k], rv(k, v))
                try:
                    inst = fn(eng)
                    if inc:
                        inst.then_inc(sems[key], inc)
                except Exception:
                    print("FAILED OP created at lines", where)
                    raise

        with nc.Block() as block:
            @block.tensor
            def _(eng):
                replay(eng, ops["pe"])

            @block.scalar
            def _(eng):
                replay(eng, ops["act"])

            @block.vector
            def _(eng):
                replay(eng, ops["dve"])

            @block.gpsimd
            def _(eng):
                replay(eng, ops["pool"])

            @block.sync
            def _(eng):
                replay(eng, ops["sp"])
                for k, v in final_waits:
                    eng.wait_ge(sems[k], v)
        self.stack.close()


class Buf:
    __slots__ = ("t", "r")

    def __init__(self, t, r):
        self.t = t
        self.r = r

    def __getitem__(self, idx):
        return self.t[idx]


class _Stop(Exception):
    pass


def build(nc, NL=DEPTH, taps=None, stop=None):
    P = Prog(nc)
    taps = taps or []
    tap_out = {}

    def din(name, shape, dt=F32):
        return nc.dram_tensor(name, list(shape), dt, kind="ExternalInput").ap()

    def dout(name, shape, dt=F32):
        return nc.dram_tensor(name, list(shape), dt, kind="ExternalOutput").ap()

    I = dict(
        x0=din("x0", [1024, D]), cond_rep=din("cond_rep", [128, 8, 128]),
        w_mod=din("w_mod", [NL, D, 6144]), b_mod=din("b_mod", [NL, 1, 6144]),
        w_in=din("w_in", [NL, D, IN_COLS]), cw=din("cw", [NL, 128, 6, 5]), cb=din("cb", [NL, 128, 6]),
        alog=din("alog", [NL, 128, 16]), dtb=din("dtb", [NL, 128, 16]),
        dsk=din("dsk", [NL, 128, 4]), ng=din("ng", [NL, 128, 4]),
        tt=din("tt", [NL, 8, 128, 30 * 64]), qn=din("qn", [NL, 128, 64]), kn=din("kn", [NL, 128, 64]),
        lnp=din("lnp", [NL, 4, 128, D]),
        w_branch=din("w_branch", [NL, 3, 512, D]), w_out=din("w_out", [NL, D, D]),
        wq=din("wq", [NL, D, 2048]), keysT=din("keysT", [NL, 16, 128, 128]),
        uv=[din("uv%d" % i, [16384, 2 * D]) for i in range(NL)],
        c_nakT=din("c_nakT", [NL, 8, 64, 256]), c_nav=din("c_nav", [NL, 256, 512]),
        c_gkT=din("c_gkT", [NL, 2, 64, 256]), c_gv=din("c_gv", [NL, 256, 128]),
        h0=din("h0", [NL, 2, 128, 256]),
        qaug=din("qaug", [16, 1024]), kaug_na=din("kaug_na", [16, 1280]), kaug_g=din("kaug_g", [16, 1280]),
        cosT=din("cosT", [128, 8, 64]), sinT=din("sinT", [128, 8, 64]),
        keep=din("keep", [128, 16]), cmask=din("cmask", [128, 1]),
    )
    O = dict(
        y=dout("y", [1024, D]), o_nak=dout("o_nak", [NL, 1024, 512]), o_nav=dout("o_nav", [NL, 1024, 512]),
        o_gk=dout("o_gk", [NL, 1024, 128]), o_gv=dout("o_gv", [NL, 1024, 128]),
        o_st=dout("o_st", [NL, 4, 2, 128, 256]),
    )

    def B(name, shape, dt):
        return Buf(P.sb(name, shape, dt), P.res(name))

    def OP(e, fn, reads=(), writes=()):
        ws = [b.r for b in writes] + [b.r for b in reads if b.r.name.startswith("bank")]
        P.op(e, fn, [b.r for b in reads], ws)

    def DMA(q, out, in_, reads=(), writes=(), is_output=False):
        P.dma(q, lambda e: e.dma_start(out=out, in_=in_), [b.r for b in reads], [b.r for b in writes], is_output)

    def mm(out, lhsT, rhs, start, stop, reads, writes):
        OP("pe", lambda e: e.matmul(out, lhsT=lhsT, rhs=rhs, start=start, stop=stop), reads, writes)

    def tp(out, in_, ident, reads, writes):
        OP("pe", lambda e: e.transpose(out=out, in_=in_, identity=ident), reads, writes)

    def act(out, in_, func, reads, writes, bias=None, scale=None, accum_out=None):
        kw = {}
        if bias is not None:
            kw["bias"] = bias
        if scale is not None:
            kw["scale"] = scale
        if accum_out is not None:
            kw["accum_out"] = accum_out
        OP("act", lambda e: e.activation(out=out, in_=in_, func=func, **kw), reads, writes)

    def tt_(eng, out, in0, in1, op, reads, writes):
        OP(eng, lambda e: e.tensor_tensor(out=out, in0=in0, in1=in1, op=op), reads, writes)

    def ts_(eng, out, in0, s1, s2, op0, op1, reads, writes):
        if op1 is None:
            OP(eng, lambda e: e.tensor_scalar(out=out, in0=in0, scalar1=s1, scalar2=None, op0=op0), reads, writes)
        else:
            OP(eng, lambda e: e.tensor_scalar(out=out, in0=in0, scalar1=s1, scalar2=s2, op0=op0, op1=op1), reads, writes)

    def stt(out, in0, scalar, in1, op0, op1, reads, writes, accum_out=None):
        if accum_out is None:
            OP("dve", lambda e: e.scalar_tensor_tensor(out=out, in0=in0, scalar=scalar, in1=in1, op0=op0, op1=op1),
               reads, writes)
        else:
            OP("dve", lambda e: e.scalar_tensor_tensor(out=out, in0=in0, scalar=scalar, in1=in1, op0=op0, op1=op1,
                                                       accum_out=accum_out), reads, writes)

    def cp(eng, out, in_, reads, writes):
        if eng == "act":
            OP("act", lambda e: e.copy(out=out, in_=in_), reads, writes)
        else:
            OP(eng, lambda e: e.tensor_copy(out=out, in_=in_), reads, writes)

    def tap(name, buf, ap, shape, dt=F32):
        if name in taps:
            d = dout("tap_" + name, shape, dt)
            tap_out[name] = d
            DMA("sp", d, ap, reads=[buf], is_output=True)

    banks = [Buf(P.ps("bank%d" % i, [128, 512], F32), P.res("bank%d" % i)) for i in range(8)]
    rot = [0, 0]

    def sbank():
        b = banks[rot[0] % 4]
        rot[0] += 1
        return b

    def hbank():
        b = banks[4 + rot[1] % 4]
        rot[1] += 1
        return b

    x = B("x", [128, NT, D], F32)
    mrep = B("mrep", [128, 6144], F32)
    hT = B("hT", [128, 8, 1024], BF16)
    lnp = B("lnp", [128, 2, D], F32)
    ident = B("ident", [128, 128], BF16)
    identf = B("identf", [128, 128], F32)
    TRIf = B("TRIf", [128, 128], F32)
    TRIb = B("TRIb", [128, 128], F32)
    maskf = B("maskf", [128, 128], F32)
    maskb = B("maskb", [128, 128], F32)
    onesb = B("onesb", [128, 128], BF16)
    onesf = B("onesf", [128, 128], F32)
    iota16 = B("iota16", [128, 16], F32)
    condS = B("condS", [128, 8, 128], BF16)
    small = B("small", [128, 64], F32)
    keep = B("keep", [128, 16], F32)
    cmask = B("cmask", [128, 1], F32)
    cosT = B("cosT", [128, 8, 64], F32)
    sinT = B("sinT", [128, 8, 64], F32)
    NW = 3
    wbuf = [B("wbuf%d" % i, [128, 8, 512], BF16) for i in range(NW)]
    wrot = [0]

    uvb = [nc.dram_tensor("uvb%d" % i, [16384, 2 * D], BF16, kind="Internal").ap() for i in range(NL)]
    uvbuf = [Buf(None, P.res("uvb")) for i in range(NL)]
    NSTG = 3
    stgc = [B("stgc%d" % i, [128, 2 * D], BF16) for i in range(NSTG)]
    conv = {"l": 0, "i": 128}

    def start_conv(l_):
        conv["l"] = l_
        conv["i"] = 0

    def pump(n):
        while n > 0 and conv["i"] < 128:
            i_, l_ = conv["i"], conv["l"]
            st = stgc[i_ % NSTG]
            DMA("pool", st[:], I["uv"][l_][i_ * 128:(i_ + 1) * 128, :], writes=[st])
            DMA("sp", uvb[l_][i_ * 128:(i_ + 1) * 128, :], st[:], reads=[st], writes=[uvbuf[l_]])
            conv["i"] += 1
            n -= 1

    ARENA_W = 20480
    arena = P.sb("arena", [128, ARENA_W], F32)
    ar = {"off": 0, "n": 0}

    def stage_begin():
        P.barrier()
        ar["off"] = 0

    def A(name, shape, dt, parts=128):
        free = int(np.prod(shape[1:]))
        words = (free * (2 if dt == BF16 else 4) + 3) // 4
        words = (words + 7) // 8 * 8
        o = ar["off"]
        assert o + words <= ARENA_W, ("arena overflow", name, o, words)
        ar["off"] = o + words
        ar["n"] += 1
        v = arena[0:shape[0], o:o + words]
        if dt != F32:
            v = v.bitcast(dt)
        v = v[:, 0:free]
        if len(shape) == 3:
            v = v.rearrange("p (a b) -> p a b", b=shape[2])
        elif len(shape) == 4:
            v = v.rearrange("p (a b c) -> p a b c", b=shape[2], c=shape[3])
        elif len(shape) == 5:
            v = v.rearrange("p (a b c d) -> p a b c d", b=shape[2], c=shape[3], d=shape[4])
        return Buf(v, P.res(name))

    def load_w(src, ncols, npump=3):
        wb = wbuf[wrot[0] % NW]
        wrot[0] += 1
        DMA("pool", wb[:, :, 0:ncols], src.rearrange("(k p) c -> p k c", p=128), writes=[wb])
        pump(npump)
        return wb

    OP("pool", lambda e: e.memset(onesf[:], 1.0), writes=[onesf])
    OP("pool", lambda e: e.memset(onesb[:], 1.0), writes=[onesb])
    OP("pool", lambda e: e.memset(identf[:], 1.0), writes=[identf])
    OP("pool", lambda e: e.affine_select(out=identf[:], in_=identf[:], pattern=[[-1, 128]], compare_op=ALU.is_equal,
                                         fill=0.0, base=0, channel_multiplier=1), reads=[identf], writes=[identf])
    cp("pool", ident[:], identf[:], [identf], [ident])
    OP("pool", lambda e: e.affine_select(out=TRIf[:], in_=onesf[:], pattern=[[1, 128]], compare_op=ALU.is_ge,
                                         fill=0.0, base=0, channel_multiplier=-1), reads=[onesf], writes=[TRIf])
    OP("pool", lambda e: e.affine_select(out=TRIb[:], in_=onesf[:], pattern=[[-1, 128]], compare_op=ALU.is_ge,
                                         fill=0.0, base=0, channel_multiplier=1), reads=[onesf], writes=[TRIb])
    OP("pool", lambda e: e.memset(maskf[:], 0.0), writes=[maskf])
    OP("pool", lambda e: e.memset(maskb[:], 0.0), writes=[maskb])
    OP("pool", lambda e: e.affine_select(out=maskf[:], in_=maskf[:], pattern=[[1, 128]], compare_op=ALU.is_ge,
                                         fill=NEG, base=0, channel_multiplier=-1), reads=[maskf], writes=[maskf])
    OP("pool", lambda e: e.affine_select(out=maskb[:], in_=maskb[:], pattern=[[-1, 128]], compare_op=ALU.is_ge,
                                         fill=NEG, base=0, channel_multiplier=1), reads=[maskb], writes=[maskb])
    OP("pool", lambda e: e.iota(iota16[:], pattern=[[1, 16]], base=0, channel_multiplier=0,
                                allow_small_or_imprecise_dtypes=True), writes=[iota16])
    DMA("sp", keep[:], I["keep"], writes=[keep])
    DMA("sp", cmask[:], I["cmask"], writes=[cmask])
    DMA("sp", cosT[:], I["cosT"], writes=[cosT])
    DMA("sp", sinT[:], I["sinT"], writes=[sinT])
    condf = A("condf", [128, 8, 128], F32)
    DMA("sp", condf[:], I["cond_rep"], writes=[condf])
    act(condS[:], condf[:], AF.Silu, [condf], [condS])
    for t in range(NT):
        DMA("sp", x[:, t, :], I["x0"][t * 128:(t + 1) * 128, :], writes=[x])

    def ln_stats(src_ap, srcbuf):
        OP("dve", lambda e: e.bn_stats(out=small[:, 0:6], in_=src_ap[:, 0:512]), [srcbuf], [small])
        OP("dve", lambda e: e.bn_stats(out=small[:, 6:12], in_=src_ap[:, 512:1024]), [srcbuf, small], [small])
        OP("dve", lambda e: e.bn_aggr(out=small[:, 12:14], in_=small[:, 0:12]), [small], [small])
        act(small[:, 14:15], small[:, 13:14], AF.Ln, [small], [small], bias=EPS)
        act(small[:, 15:16], small[:, 14:15], AF.Exp, [small], [small], scale=-0.5)
        return small[:, 12:13], small[:, 15:16]

    def ln_modulate(shift_off, scale_off, tmpA, hb, h2tok=None):
        for t in range(NT):
            mean, rstd = ln_stats(x[:, t, :], x)
            ts_("dve", tmpA[:], x[:, t, :], mean, rstd, ALU.subtract, ALU.mult, [x, small], [tmpA])
            tt_("dve", tmpA[:], tmpA[:], mrep[:, scale_off:scale_off + D], ALU.mult, [tmpA, mrep], [tmpA])
            dst = hb if h2tok is None else h2tok
            dst_ap = hb[:] if h2tok is None else h2tok[:, t, :]
            tt_("dve", dst_ap, tmpA[:], mrep[:, shift_off:shift_off + D], ALU.add, [tmpA, mrep], [dst])
            bk = sbank()
            bkb = bk[:].bitcast(BF16)
            for k in range(8):
                tp(bkb[:, k * 128:(k + 1) * 128], dst_ap[:, k * 128:(k + 1) * 128], ident[:], [dst, ident], [bk])
            cp("act", hT[:, :, t * 128:(t + 1) * 128], bkb.rearrange("p (k t) -> p k t", t=128), [bk], [hT])

    def ln_affine(pre, gi, tmpbuf):
        pass

    def chk(k):
        if stop == k:
            raise _Stop()

    def layer_body(l):
        stage_begin()
        start_conv(l)
        bmod = A("bmod", [1, 6144], BF16)
        DMA("pool", bmod[:], I["b_mod"][l], writes=[bmod])
        for cc in range(12):
            wb = load_w(I["w_mod"][l][:, cc * 512:(cc + 1) * 512], 512, npump=0)
            bk = sbank()
            for k in range(8):
                mm(bk[:], condS[:, k, :], wb[:, k, :], k == 0, False, [condS, wb], [bk])
            mm(bk[:], onesb[0:1, :], bmod[0:1, cc * 512:(cc + 1) * 512], False, True, [onesb, bmod], [bk])
            if cc in (2, 3, 8, 9):
                act(mrep[:, cc * 512:(cc + 1) * 512], bk[:], AF.Identity, [bk], [mrep], bias=1.0)
            else:
                cp("act", mrep[:, cc * 512:(cc + 1) * 512], bk[:], [bk], [mrep])
        if l == 0:
            tap("mrep", mrep, mrep[:], [128, 6144])
        chk(1)

        tmpA = A("tmpA", [128, D], F32)
        hb = A("hb", [128, D], BF16)
        ln_modulate(0, 1024, tmpA, hb)
        if l == 0:
            tap("hT", hT, hT[:], [128, 8, 1024], BF16)
        chk(2)

        stage_begin()
        yA = A("yA", [128, 4, 1024], BF16)
        yB = A("yB", [128, 4, 1024], BF16)
        yC = A("yC", [128, 4, 1024], BF16)
        mix_off = ar["off"]
        xc = A("xc", [128, 6, 1024], BF16)
        x_tok = A("x_tok", [128, 8, 512], BF16)
        B_tok = A("B_tok", [128, 8, 128], BF16)
        Sin = A("Sin", [128, 8, 2, 256], BF16)
        a_all = A("a_all", [128, 8, 16], F32)
        lndt = A("lndt", [128, 8, 16], F32)
        csT = A("csT", [128, 8, 32], F32)
        w_all = A("w_all", [128, 8, 16], F32)
        decG = A("decG", [128, 8, 2, 4], F32)
        Arep = A("Arep", [128, 16], F32)
        dtbr = A("dtbr", [128, 16], F32)
        cwb = A("cwb", [128, 6, 5], F32)
        cbb = A("cbb", [128, 6], F32)
        dskb = A("dskb", [128, 4], F32)
        ngb = A("ngb", [128, 4], F32)
        S = A("S", [128, 2, 256], F32)
        sc16 = A("sc16", [128, 4, 16], F32)
        ssd_off = ar["off"]
        xp = A("xp", [128, 6, 4, 260], BF16)
        acc = A("acc", [128, 4, 256], F32)

        DMA("sp", Arep[:], I["alog"][l], writes=[Arep])
        DMA("sp", dtbr[:], I["dtb"][l], writes=[dtbr])
        DMA("sp", cwb[:], I["cw"][l], writes=[cwb])
        DMA("sp", cbb[:], I["cb"][l], writes=[cbb])
        DMA("sp", dskb[:], I["dsk"][l], writes=[dskb])
        DMA("sp", ngb[:], I["ng"][l], writes=[ngb])
        DMA("sp", S[:], I["h0"][l].rearrange("d p f -> p d f"), writes=[S])
        act(Arep[:], Arep[:], AF.Exp, [Arep], [Arep])
        ts_("dve", Arep[:], Arep[:], -1.0, None, ALU.mult, None, [Arep], [Arep])

        OP("pool", lambda e: e.memset(xp[:], 0.0), writes=[xp])
        w1 = load_w(I["w_in"][l][:, OFF["xs"]:OFF["xs"] + 512], 512)
        w2 = load_w(I["w_in"][l][:, OFF["B"]:OFF["B"] + 272], 272)
        for k in range(6):
            wsrc, c0 = (w1, k * 128) if k < 4 else (w2, (k - 4) * 128)
            for half in range(2):
                bk = sbank()
                for kd in range(8):
                    mm(bk[:], wsrc[:, kd, c0:c0 + 128], hT[:, kd, half * 512:(half + 1) * 512], kd == 0, kd == 7,
                       [wsrc, hT], [bk])
                cp("act", xp[:, k, 2 * half:2 * half + 2, 2:258], bk[:].rearrange("p (s t) -> p s t", t=256), [bk], [xp])
        chk(21)
        for t in range(NT):
            bk = sbank()
            for kd in range(8):
                mm(bk[:, 0:16], hT[:, kd, t * 128:(t + 1) * 128], w2[:, kd, 256:272], kd == 0, kd == 7, [hT, w2], [bk])
            tt_("dve", sc16[:, 0, :], bk[:, 0:16], dtbr[:], ALU.add, [bk, dtbr], [sc16])
            act(sc16[:, 1, :], sc16[:, 0, :], AF.Exp, [sc16], [sc16])
            act(sc16[:, 2, :], sc16[:, 1, :], AF.Ln, [sc16], [sc16], bias=1.0)
            act(lndt[:, t, :], sc16[:, 2, :], AF.Ln, [sc16], [lndt])
            tt_("dve", a_all[:, t, :], sc16[:, 2, :], Arep[:], ALU.mult, [sc16, Arep], [a_all])
        chk(22)
        ts_("dve", xp[:, :, 1:4, 0:2], xp[:, :, 0:3, 256:258], cmask[:, 0:1], None, ALU.mult, None, [xp, cmask], [xp])
        ts_("dve", xp[:, :, 0:3, 258:260], xp[:, :, 1:4, 2:4], cmask[:, 0:1], None, ALU.mult, None, [xp, cmask], [xp])
        for k in range(6):
            ts_("dve", acc[:], xp[:, k, :, 0:256], cwb[:, k, 0:1], None, ALU.mult, None, [xp, cwb], [acc])
            for j in range(1, 5):
                stt(acc[:], xp[:, k, :, j:j + 256], cwb[:, k, j:j + 1], acc[:], ALU.mult, ALU.add, [xp, cwb, acc], [acc])
            act(xc[:, k, :].rearrange("p (s t) -> p s t", t=256), acc[:], AF.Silu, [acc, cbb], [xc], bias=cbb[:, k:k + 1])
        chk(23)
        for t in range(NT):
            bk = sbank()
            bkb = bk[:].bitcast(BF16)
            for k in range(5):
                tp(bkb[:, k * 128:(k + 1) * 128], xc[:, k, t * 128:(t + 1) * 128], ident[:], [xc, ident], [bk])
            cp("act", x_tok[:, t, :], bkb[:, 0:512], [bk], [x_tok])
            chk(24)
            cp("act", B_tok[:, t, :], bkb[:, 512:640], [bk], [B_tok])

        chk(3)
        P.barrier()
        ar["off"] = ssd_off
        for c in range(8):
            bk = sbank()
            mm(bk[:, 0:8], TRIf[:], a_all[:, c, 0:8], True, True, [TRIf, a_all], [bk])
            mm(bk[:, 8:16], TRIb[:], a_all[:, c, 8:16], True, True, [TRIb, a_all], [bk])
            mm(bk[:, 16:32], onesf[:], a_all[:, c, :], True, True, [onesf, a_all], [bk])
            cp("act", csT[:, c, :], bk[:, 0:32], [bk], [csT])
            act(sc16[:, 0, :], bk[:, 16:32], AF.Exp, [bk], [sc16])
            d4 = sc16[:, 0, :].rearrange("p (d g h) -> p d g h", d=2, g=2)
            cp("dve", decG[0:64, c, :, :], d4[0:64, :, 0, :], [sc16], [decG])
            cp("dve", decG[64:128, c, :, :], d4[64:128, :, 1, :], [sc16], [decG])
            tt_("dve", sc16[:, 1, :], csT[:, c, 16:32], csT[:, c, 0:16], ALU.subtract, [csT], [sc16])
            tt_("dve", sc16[:, 1, :], sc16[:, 1, :], lndt[:, c, :], ALU.add, [sc16, lndt], [sc16])
            act(w_all[:, c, :], sc16[:, 1, :], AF.Exp, [sc16], [w_all])

        xw = A("xw", [128, 512], BF16)
        for d_, order in ((0, range(8)), (1, range(7, -1, -1))):
            first = True
            for c in order:
                if not first:
                    ts_("dve", S[:, d_, :], S[:, d_, :], keep[:, d_ * 8 + c:d_ * 8 + c + 1], None, ALU.mult, None,
                        [S, keep], [S])
                first = False
                cp("act", Sin[:, c, d_, :], S[:, d_, :], [S], [Sin])
                tt_("dve", xw[:].rearrange("p (h q) -> p h q", q=64),
                    x_tok[:, c, :].rearrange("p (h q) -> p h q", q=64),
                    w_all[:, c, d_ * 8:d_ * 8 + 8].unsqueeze(2).broadcast_to([128, 8, 64]), ALU.mult,
                    [x_tok, w_all], [xw])
                bk = sbank()
                for g in range(2):
                    mm(bk[g * 64:(g + 1) * 64, 0:256], B_tok[:, c, g * 64:(g + 1) * 64], xw[:, g * 256:(g + 1) * 256],
                       True, True, [B_tok, xw], [bk])
                tt_("dve", S[:, d_, :].rearrange("p (h q) -> p h q", q=64),
                    S[:, d_, :].rearrange("p (h q) -> p h q", q=64),
                    decG[:, c, d_, :].unsqueeze(2).broadcast_to([128, 4, 64]), ALU.mult, [S, decG], [S])
                tt_("dve", S[:, d_, :], S[:, d_, :], bk[:, 0:256], ALU.add, [S, bk], [S])
                if (d_ == 0 and c % 2 == 1) or (d_ == 1 and c % 2 == 0):
                    DMA("sp", O["o_st"][l, c // 2, d_], S[:, d_, :], reads=[S], is_output=True)

        chk(4)
        arep = A("arep", [128, 16, 128], F32)
        cbT = A("cbT", [128, 2, 128], F32)
        Lw = A("Lw", [128, 4, 128], F32)
        MT = A("MT", [128, 2, 128], BF16)
        Ew = A("Ew", [128, 2, 128], F32)
        Cp = A("Cp", [128, 4, 128], BF16)
        yg32 = A("yg32", [128, 512], F32)
        ysq = A("ysq", [128, 512], F32)
        ygf = yA
        mcnt = 0
        for c in range(8):
            pump(4)
            cs_ = slice(c * 128, (c + 1) * 128)
            cp("dve", arep[:], a_all[:, c, :].unsqueeze(2).broadcast_to([128, 16, 128]), [a_all], [arep])
            reps = []
            for q in range(4):
                bk = hbank()
                for jj in range(4):
                    j = q * 4 + jj
                    mm(bk[:, jj * 128:(jj + 1) * 128], arep[:, j, :], (TRIf if j < 8 else TRIb)[:], True, True,
                       [arep, TRIf, TRIb], [bk])
                reps.append(bk)
            for g in range(2):
                bk = sbank()
                mm(bk[:, 0:128], xc[g * 64:(g + 1) * 64, 4, cs_], xc[g * 64:(g + 1) * 64, 5, cs_], True, True, [xc], [bk])
                cp("act", cbT[:, g, :], bk[:, 0:128], [bk], [cbT])
            for pr in range(4):
                ybk = sbank()
                for hh in range(2):
                    h = pr * 2 + hh
                    g, h4 = h // 4, h % 4
                    gs = slice(g * 64, (g + 1) * 64)
                    rf = reps[h // 4][:, (h % 4) * 128:(h % 4 + 1) * 128]
                    rb = reps[2 + h // 4][:, (h % 4) * 128:(h % 4 + 1) * 128]
                    m2 = mcnt % 2
                    mcnt += 1
                    stt(Lw[:, 0, :], rf, csT[:, c, h:h + 1], maskf[:], ALU.subtract, ALU.add,
                        [reps[h // 4], csT, maskf], [Lw])
                    act(Lw[:, 1, :], Lw[:, 0, :], AF.Exp, [Lw, lndt], [Lw], bias=lndt[:, c, h:h + 1])
                    stt(Lw[:, 2, :], rb, csT[:, c, 8 + h:9 + h], maskb[:], ALU.subtract, ALU.add,
                        [reps[2 + h // 4], csT, maskb], [Lw])
                    act(Lw[:, 3, :], Lw[:, 2, :], AF.Exp, [Lw, lndt], [Lw], bias=lndt[:, c, 8 + h:9 + h])
                    tt_("dve", Lw[:, 1, :], Lw[:, 1, :], Lw[:, 3, :], ALU.add, [Lw], [Lw])
                    tt_("dve", MT[:, m2, :], Lw[:, 1, :], cbT[:, g, :], ALU.mult, [Lw, cbT], [MT])
                    act(Ew[gs, 0, :], rf[gs, :], AF.Exp, [reps[h // 4]], [Ew])
                    act(Ew[gs, 1, :], rb[gs, :], AF.Exp, [reps[2 + h // 4]], [Ew])
                    tt_("dve", Cp[gs, m2 * 2, :], xc[gs, 5, cs_], Ew[gs, 0, :], ALU.mult, [xc, Ew], [Cp])
                    tt_("dve", Cp[gs, m2 * 2 + 1, :], xc[gs, 5, cs_], Ew[gs, 1, :], ALU.mult, [xc, Ew], [Cp])
                    yo = ybk[hh * 64:(hh + 1) * 64, 0:128]
                    mm(yo, x_tok[:, c, h * 64:(h + 1) * 64], MT[:, m2, :], True, False, [x_tok, MT], [ybk])
                    mm(yo, Sin[gs, c, 0, h4 * 64:(h4 + 1) * 64], Cp[gs, m2 * 2, :], False, False, [Sin, Cp], [ybk])
                    mm(yo, Sin[gs, c, 1, h4 * 64:(h4 + 1) * 64], Cp[gs, m2 * 2 + 1, :], False, True, [Sin, Cp], [ybk])
                stt(ygf[:, pr, cs_], xc[:, pr, cs_], dskb[:, pr:pr + 1], ybk[:, 0:128], ALU.mult, ALU.add,
                    [xc, dskb, ybk], [ygf])
        chk(5)
        wz = load_w(I["w_in"][l][:, OFF["z"]:OFF["z"] + 512], 512)
        for half in range(2):
            hs = slice(half * 512, (half + 1) * 512)
            sbk = hbank()
            for pr in range(4):
                bk = sbank()
                for kd in range(8):
                    mm(bk[:], wz[:, kd, pr * 128:(pr + 1) * 128], hT[:, kd, hs], kd == 0, kd == 7, [wz, hT], [bk])
                act(yg32[:], bk[:], AF.Silu, [bk], [yg32])
                tt_("dve", ygf[:, pr, hs], ygf[:, pr, hs], yg32[:], ALU.mult, [ygf, yg32], [ygf])
                tt_("dve", ysq[:], ygf[:, pr, hs], ygf[:, pr, hs], ALU.mult, [ygf], [ysq])
                mm(sbk[:], onesf[:], ysq[:], pr == 0, pr == 3, [onesf, ysq], [sbk])
            act(yg32[:], sbk[:], AF.Ln, [sbk], [yg32], bias=EPS, scale=1.0 / 512)
            act(yg32[:], yg32[:], AF.Exp, [yg32], [yg32], scale=-0.5)
            for pr in range(4):
                stt(yA[:, pr, hs], ygf[:, pr, hs], ngb[:, pr:pr + 1], yg32[:], ALU.mult, ALU.mult, [ygf, ngb, yg32], [ygf])

        chk(6)
        for br in range(2):
            P.barrier()
            ar["off"] = mix_off
            is_na = br == 0
            if br == 1:
                chk(7)
            nq = 4 if is_na else 8
            nkb = 4 if is_na else 2
            QT = A("QT", [80, nq, 1024], BF16)
            KT = A("KT", [80, nkb, 1280], BF16)
            vw = 512 if is_na else 128
            V_tok = A("V_tok", [128, 8, vw], BF16)
            V_ctx = A("V_ctx", [128, 2, vw], BF16)
            stg = A("stg", [128, 2, 512], F32)
            PT = A("PT", [128, 3, 512], BF16)
            tmpS = A("tmpS", [128, 2, 512], F32)
            rs = A("rs", [128, 512], F32)
            yout = yB if is_na else yC
            ka = I["kaug_na"] if is_na else I["kaug_g"]
            ck = I["c_nakT"] if is_na else I["c_gkT"]
            cv = I["c_nav"] if is_na else I["c_gv"]
            DMA("pool", V_ctx[:], cv[l].rearrange("(j p) c -> p j c", p=128), writes=[V_ctx])

            def fill_aug(h0_, n_):
                DMA("pool", QT[64:80, :, :], I["qaug"].unsqueeze(1).broadcast_to([16, nq, 1024]), writes=[QT])
                DMA("pool", KT[64:80, :, :], ka.unsqueeze(1).broadcast_to([16, nkb, 1280]), writes=[KT])
                DMA("pool", KT[0:64, :, 1024:1280], ck[l, h0_:h0_ + n_].rearrange("h d k -> d h k"), writes=[KT])

            if is_na:
                ttb = [A("ttb%d" % i, [128, 30 * 64], BF16) for i in range(2)]
                wqn = load_w(I["w_in"][l][:, OFF["naq"]:OFF["naq"] + 512], 512)
                wkn = load_w(I["w_in"][l][:, OFF["nak"]:OFF["nak"] + 512], 512)
                wvn = load_w(I["w_in"][l][:, OFF["nav"]:OFF["nav"] + 512], 512)
                for t in range(NT):
                    for (wsrc, dram, keepbf) in ((wkn, O["o_nak"], False), (wvn, O["o_nav"], True)):
                        bk = sbank()
                        for kd in range(8):
                            mm(bk[:], hT[:, kd, t * 128:(t + 1) * 128], wsrc[:, kd, :], kd == 0, kd == 7, [hT, wsrc], [bk])
                        si_ = 0 if not keepbf else 1
                        cp("act", stg[:, si_, :], bk[:], [bk], [stg])
                        if keepbf:
                            cp("dve", V_tok[:, t, :], bk[:], [bk], [V_tok])
                        DMA("sp", dram[l, t * 128:(t + 1) * 128, :], stg[:, si_, :], reads=[stg], is_output=True)
                groups = [list(range(0, 4)), list(range(4, 8))]
            else:
                fill_aug(0, 2)
                wqg = load_w(I["w_in"][l][:, OFF["gq"]:OFF["gq"] + 512], 512)
                wkv = load_w(I["w_in"][l][:, OFF["gk"]:OFF["gk"] + 256], 256)
                gq = A("gq", [128, 512], F32)
                gsq = A("gsq", [128, 512], F32)
                gr = A("gr", [128, 512], F32)
                qr = A("qr", [128, 512], BF16)
                g8 = A("g8", [128, 4, 8], F32)
                qnb = A("qnb", [128, 64], F32)
                knb = A("knb", [128, 64], F32)
                DMA("sp", qnb[:], I["qn"][l], writes=[qnb])
                DMA("sp", knb[:], I["kn"][l], writes=[knb])

                def norm_rope(src_ap, src_bk, nh, gain, t, cache_dram):
                    n = nh * 64
                    v3 = lambda ap: ap.rearrange("p (h q) -> p h q", q=64)
                    cp("act", gq[:, 0:n], src_ap, [src_bk], [gq])
                    act(gsq[:, 0:n], src_ap, AF.Square, [src_bk], [gsq])
                    OP("dve", lambda e: e.tensor_reduce(out=g8[:, 0, 0:nh], in_=v3(gsq[:, 0:n]), axis=AX.X, op=ALU.add),
                       [gsq], [g8])
                    act(g8[:, 1, 0:nh], g8[:, 0, 0:nh], AF.Ln, [g8], [g8], bias=EPS, scale=1.0 / 64)
                    act(g8[:, 2, 0:nh], g8[:, 1, 0:nh], AF.Exp, [g8], [g8], scale=-0.5)
                    tt_("dve", v3(gq[:, 0:n]), v3(gq[:, 0:n]), g8[:, 2, 0:nh].unsqueeze(2).broadcast_to([128, nh, 64]),
                        ALU.mult, [gq, g8], [gq])
                    tt_("dve", v3(gq[:, 0:n]), v3(gq[:, 0:n]), gain[:].unsqueeze(1).broadcast_to([128, nh, 64]),
                        ALU.mult, [gq, gain], [gq])
                    if cache_dram is not None:
                        DMA("sp", cache_dram, gq[:, 0:n], reads=[gq], is_output=True)
                    v5 = lambda ap: ap.rearrange("p (h a b c) -> p h a b c", a=2, b=2, c=16)
                    sn = sinT[:, t, :].rearrange("p (a b c) -> p a b c", a=2, b=2)
                    for b_ in range(2):
                        tt_("dve", v5(gr[:, 0:n])[:, :, :, b_, :], v5(gq[:, 0:n])[:, :, :, 1 - b_, :],
                            sn[:, :, b_, :].unsqueeze(1).broadcast_to([128, nh, 2, 16]), ALU.mult, [gq, sinT], [gr])
                    tt_("dve", v3(gsq[:, 0:n]), v3(gq[:, 0:n]), cosT[:, t, :].unsqueeze(1).broadcast_to([128, nh, 64]),
                        ALU.mult, [gq, cosT], [gsq])
                    tt_("dve", qr[:, 0:n], gsq[:, 0:n], gr[:, 0:n], ALU.add, [gsq, gr], [qr])

                for t in range(NT):
                    ts = slice(t * 128, (t + 1) * 128)
                    bk = sbank()
                    for kd in range(8):
                        mm(bk[:], hT[:, kd, ts], wqg[:, kd, :], kd == 0, kd == 7, [hT, wqg], [bk])
                    norm_rope(bk[:, 0:512], bk, 8, qnb, t, None)
                    bk2 = sbank()
                    b2 = bk2[:].bitcast(BF16)
                    for h in range(8):
                        tp(b2[0:64, h * 128:(h + 1) * 128], qr[:, h * 64:(h + 1) * 64], ident[:], [qr, ident], [bk2])
                    cp("act", QT[0:64, :, ts], b2[0:64, :].rearrange("p (h t) -> p h t", t=128), [bk2], [QT])
                    bk = sbank()
                    for kd in range(8):
                        mm(bk[:, 0:256], hT[:, kd, ts], wkv[:, kd, 0:256], kd == 0, kd == 7, [hT, wkv], [bk])
                    cp("act", stg[:, 1, 0:128], bk[:, 128:256], [bk], [stg])
                    cp("dve", V_tok[:, t, :], bk[:, 128:256], [bk], [V_tok])
                    DMA("sp", O["o_gv"][l, ts, :], stg[:, 1, 0:128], reads=[stg], is_output=True)
                    norm_rope(bk[:, 0:128], bk, 2, knb, t, O["o_gk"][l, ts, :])
                    bk2 = sbank()
                    b2 = bk2[:].bitcast(BF16)
                    for h in range(2):
                        tp(b2[0:64, h * 128:(h + 1) * 128], qr[:, h * 64:(h + 1) * 64], ident[:], [qr, ident], [bk2])
                    cp("act", KT[0:64, 0:2, ts], b2[0:64, 0:256].rearrange("p (h t) -> p h t", t=128), [bk2], [KT])
                groups = [list(range(8))]

            if is_na:
                chk(61)
            else:
                chk(71)
            pcnt = 0
            for heads in groups:
                if is_na:
                    fill_aug(heads[0], 4)
                    for (wsrc, dst) in ((wqn, QT), (wkn, KT)):
                        for hi, h in enumerate(heads):
                            for half in range(2):
                                bk = sbank()
                                for kd in range(8):
                                    mm(bk[0:64, :], wsrc[:, kd, h * 64:(h + 1) * 64], hT[:, kd, half * 512:(half + 1) * 512],
                                       kd == 0, kd == 7, [wsrc, hT], [bk])
                                cp("act" if half == 0 else "dve", dst[0:64, hi, half * 512:(half + 1) * 512], bk[0:64, :],
                                   [bk], [dst])
                steps = [(hi, h, qc, kt) for hi, h in enumerate(heads) for qc in range(2) for kt in range(10)]
                sbk_of = {}
                acc_of = {}

                def issue_S(i):
                    hi, h, qc, kt = steps[i]
                    kvb = hi if is_na else h // 4
                    if is_na and qc == 0 and kt == 0:
                        tb = ttb[h % 2]
                        DMA("pool", tb[:], I["tt"][l, h], writes=[tb])
                    sbk = sbank()
                    mm(sbk[:], KT[0:80, kvb, kt * 128:(kt + 1) * 128], QT[0:80, hi, qc * 512:(qc + 1) * 512], True, True,
                       [KT, QT], [sbk])
                    sbk_of[i] = sbk

                for i in range(min(2, len(steps))):
                    issue_S(i)
                for i, (hi, h, qc, kt) in enumerate(steps):
                    if kt == 0:
                        pump(2)
                    if i + 2 < len(steps):
                        issue_S(i + 2)
                    kv = h if is_na else h // 4
                    ps_ = slice((h % 2) * 64, (h % 2) * 64 + 64)
                    qs = slice(qc * 512, (qc + 1) * 512)
                    if kt == 0:
                        acc_of[(h, qc)] = (hbank(), hbank())
                    ob, sb_ = acc_of[(h, qc)]
                    sbk = sbk_of.pop(i)
                    p2 = i % 3
                    if is_na and kt < 8:
                        tb = ttb[h % 2]
                        e0 = qc * 8 - 2 * kt + 14
                        stt(tmpS[:, i % 2, :], sbk[:], 0.125, tb[:, e0 * 64:(e0 + 8) * 64], ALU.mult, ALU.add,
                            [sbk, tb], [tmpS])
                        act(PT[:, p2, :], tmpS[:, i % 2, :], AF.Exp, [tmpS], [PT])
                    else:
                        act(PT[:, p2, :], sbk[:], AF.Exp, [sbk], [PT], scale=0.125)
                    if kt < 8:
                        vop = V_tok[:, kt, kv * 64:(kv + 1) * 64]
                        vb = V_tok
                    else:
                        vop = V_ctx[:, kt - 8, kv * 64:(kv + 1) * 64]
                        vb = V_ctx
                    mm(ob[ps_, :], vop, PT[:, p2, :], kt == 0, kt == 9, [vb, PT], [ob])
                    mm(sb_[ps_, :], onesb[:, 0:64], PT[:, p2, :], kt == 0, kt == 9, [onesb, PT], [sb_])
                    if kt == 9:
                        act(rs[ps_, :], sb_[ps_, :], AF.Ln, [sb_], [rs])
                        act(rs[ps_, :], rs[ps_, :], AF.Exp, [rs], [rs], scale=-1.0)
                        tt_("dve", yout[ps_, h // 2, qs], ob[ps_, :], rs[ps_, :], ALU.mult, [ob, rs], [yout])

        chk(8)
        P.barrier()
        ar["off"] = mix_off
        mT = A("mT", [128, 8, 1024], BF16)
        Wg = [A("Wg%d" % i, [128, 8, 384], BF16) for i in range(2)]
        Wb = [A("Wb%d" % i, [128, 3, 4, 128], BF16) for i in range(2)]
        sg = A("sg", [128, 3, 512], F32)
        mtmp = A("mtmp", [128, 2, 512], F32)
        ys = (yA, yB, yC)
        for dc in range(8):
            pump(3)
            wg_, wb_ = Wg[dc % 2], Wb[dc % 2]
            for i in range(3):
                c0 = OFF["gate"] + i * 1024 + dc * 128
                DMA("pool", wg_[:, :, i * 128:(i + 1) * 128], I["w_in"][l][:, c0:c0 + 128].rearrange("(k p) c -> p k c", p=128),
                    writes=[wg_])
                DMA("pool", wb_[:, i, :, :], I["w_branch"][l, i][:, dc * 128:(dc + 1) * 128].rearrange("(k p) c -> p k c", p=128),
                    writes=[wb_])
            for half in range(2):
                hs = slice(half * 512, (half + 1) * 512)
                for i in range(3):
                    gb = sbank()
                    for kd in range(8):
                        mm(gb[:], wg_[:, kd, i * 128:(i + 1) * 128], hT[:, kd, hs], kd == 0, kd == 7, [wg_, hT], [gb])
                    act(sg[:, i, :], gb[:], AF.Sigmoid, [gb], [sg])
                    pb = sbank()
                    for e4 in range(4):
                        mm(pb[:], wb_[:, i, e4, :], ys[i][:, e4, hs], e4 == 0, e4 == 3, [wb_, ys[i]], [pb])
                    if i == 0:
                        tt_("dve", mtmp[:, 0, :], pb[:], sg[:, i, :], ALU.mult, [pb, sg], [mtmp])
                    else:
                        tt_("dve", mtmp[:, 1, :], pb[:], sg[:, i, :], ALU.mult, [pb, sg], [mtmp])
                        if i == 1:
                            tt_("dve", mtmp[:, 0, :], mtmp[:, 0, :], mtmp[:, 1, :], ALU.add, [mtmp], [mtmp])
                        else:
                            tt_("dve", mT[:, dc, hs], mtmp[:, 0, :], mtmp[:, 1, :], ALU.add, [mtmp], [mT])
        wo = [load_w(I["w_out"][l][:, hf * 512:(hf + 1) * 512], 512) for hf in range(2)]
        DMA("sp", lnp[:, 0, :], I["lnp"][l, 0], writes=[lnp])
        DMA("sp", lnp[:, 1, :], I["lnp"][l, 1], writes=[lnp])
        pre = A("pre", [128, D], F32)
        for t in range(NT):
            ts = slice(t * 128, (t + 1) * 128)
            for hf in range(2):
                bk = sbank()
                for dc in range(8):
                    mm(bk[:], mT[:, dc, ts], wo[hf][:, dc, :], dc == 0, dc == 7, [mT, wo[hf]], [bk])
                tt_("dve", pre[:, hf * 512:(hf + 1) * 512], bk[:], mrep[:, 2048 + hf * 512:2048 + (hf + 1) * 512], ALU.mult,
                    [bk, mrep], [pre])
            stt(pre[:], x[:, t, :], DN_ALPHA, pre[:], ALU.mult, ALU.add, [x, pre], [pre])
            mean, rstd = ln_stats(pre[:], pre)
            ts_("dve", pre[:], pre[:], mean, rstd, ALU.subtract, ALU.mult, [pre, small], [pre])
            tt_("dve", pre[:], pre[:], lnp[:, 0, :], ALU.mult, [pre, lnp], [pre])
            tt_("dve", x[:, t, :], pre[:], lnp[:, 1, :], ALU.add, [pre, lnp], [x])
        if l == 0:
            tap("x1", x, x[:], [128, NT, D])

        chk(9)
        stage_begin()
        h2tok = A("h2tok", [128, NT, D], BF16)
        e_idx = A("e_idx", [128, NT, 128], I32)
        g_all = A("g_all", [128, NT, 128], F32)
        asel = A("asel", [128, NT, 128], F32)
        peer_off = ar["off"]
        tmpA = A("tmpA2", [128, D], F32)
        ln_modulate(3072, 4096, tmpA, None, h2tok)
        P.barrier()
        ar["off"] = peer_off
        keysT = A("keysT", [128, 16, 128], BF16)
        qTh = A("qTh", [128, 16, 512], BF16)
        sv = A("sv", [128, 16, 16], F32)
        si = A("si", [128, 16, 16], U32)
        sif = A("sif", [128, 16, 16], F32)
        wk16 = A("wk16", [128, 16, 128], F32)
        wk8 = A("wk8", [128, 8, 256], F32)
        oh = Buf(wk16.t.rearrange("p a b -> p (a b)").rearrange("p (h i j) -> p h i j", h=8, i=16), wk16.r)
        cand = A("cand", [128, 8, 16, 16], F32)
        cvv = A("cvv", [128, 8, 16], F32)
        ci = A("ci", [128, 8, 16], U32)
        cij = A("cij", [128, 2, 8, 16], U32)
        cijf = A("cijf", [128, 2, 8, 16], F32)
        k01 = A("k01", [128, 2, 8, 16], F32)
        g8p = A("g8p", [128, 2, 8], F32)
        DMA("pool", keysT[:], I["keysT"][l].rearrange("h d k -> d h k"), writes=[keysT])
        for half in range(2):
            hs = slice(half * 512, (half + 1) * 512)
            for qb in range(4):
                wqb = load_w(I["wq"][l][:, qb * 512:(qb + 1) * 512], 512)
                for j in range(4):
                    bk = sbank()
                    for kd in range(8):
                        mm(bk[:], wqb[:, kd, j * 128:(j + 1) * 128], hT[:, kd, hs], kd == 0, kd == 7, [wqb, hT], [bk])
                    cp("act", qTh[:, qb * 4 + j, :], bk[:], [bk], [qTh])
            for tl in range(4):
                t = half * 4 + tl
                for grp in range(4):
                    bk = sbank()
                    for j in range(4):
                        hp = grp * 4 + j
                        mm(bk[:, j * 128:(j + 1) * 128], qTh[:, hp, tl * 128:(tl + 1) * 128], keysT[:, hp, :], True, True,
                           [qTh, keysT], [bk])
                    srcs = [(grp * 4 + j, bk[:, j * 128:(j + 1) * 128]) for j in range(4)]
                    for hp, src in srcs:
                        OP("dve", lambda e, src=src, hp=hp: e.max(out=sv[:, hp, 0:8], in_=src), [bk], [sv])
                    for hp, src in srcs:
                        OP("dve", lambda e, src=src, hp=hp: e.max_index(out=si[:, hp, 0:8], in_max=sv[:, hp, 0:8], in_values=src),
                           [bk, sv], [si])
                    for hp, src in srcs:
                        OP("dve", lambda e, src=src, hp=hp: e.match_replace(out=wk16[:, hp, :], in_to_replace=sv[:, hp, 0:8],
                                                                            in_values=src, imm_value=-1e30), [bk, sv], [wk16])
                    for hp, src in srcs:
                        OP("dve", lambda e, hp=hp: e.max(out=sv[:, hp, 8:16], in_=wk16[:, hp, :]), [wk16], [sv])
                    for hp, src in srcs:
                        OP("dve", lambda e, hp=hp: e.max_index(out=si[:, hp, 8:16], in_max=sv[:, hp, 8:16], in_values=wk16[:, hp, :]),
                           [wk16, sv], [si])
                cp("dve", sif[:], si[:], [si], [sif])
                sv4 = sv[:].rearrange("p (h q) k -> p h q k", q=2)
                sif4 = sif[:].rearrange("p (h q) k -> p h q k", q=2)
                tt_("dve", cand[:], sv4[:, :, 0, :].unsqueeze(3).broadcast_to([128, 8, 16, 16]),
                    sv4[:, :, 1, :].unsqueeze(2).broadcast_to([128, 8, 16, 16]), ALU.add, [sv], [cand])
                c2s = [cand[:, h, :, :].rearrange("p a b -> p (a b)") for h in range(8)]
                for h in range(8):
                    OP("dve", lambda e, c2=c2s[h], h=h: e.max(out=cvv[:, h, 0:8], in_=c2), [cand], [cvv])
                for h in range(8):
                    OP("dve", lambda e, c2=c2s[h], h=h: e.max_index(out=ci[:, h, 0:8], in_max=cvv[:, h, 0:8], in_values=c2),
                       [cand, cvv], [ci])
                for h in range(8):
                    OP("dve", lambda e, c2=c2s[h], h=h: e.match_replace(out=wk8[:, h, :], in_to_replace=cvv[:, h, 0:8], in_values=c2,
                                                                         imm_value=-1e30), [cand, cvv], [wk8])
                for h in range(8):
                    OP("dve", lambda e, h=h: e.max(out=cvv[:, h, 8:16], in_=wk8[:, h, :]), [wk8], [cvv])
                for h in range(8):
                    OP("dve", lambda e, h=h: e.max_index(out=ci[:, h, 8:16], in_max=cvv[:, h, 8:16], in_values=wk8[:, h, :]),
                       [wk8, cvv], [ci])
                OP("dve", lambda e: e.tensor_single_scalar(out=cij[:, 0, :, :], in_=ci[:], scalar=4, op=ALU.logical_shift_right),
                   [ci], [cij])
                OP("dve", lambda e: e.tensor_single_scalar(out=cij[:, 1, :, :], in_=ci[:], scalar=15, op=ALU.bitwise_and),
                   [ci], [cij])
                cp("dve", cijf[:], cij[:], [cij], [cijf])
                for q in range(2):
                    tt_("dve", oh[:], cijf[:, q, :, :].unsqueeze(3).broadcast_to([128, 8, 16, 16]),
                        iota16[:].unsqueeze(1).unsqueeze(1).broadcast_to([128, 8, 16, 16]), ALU.is_equal, [cijf, iota16], [oh])
                    tt_("dve", oh[:], oh[:], sif4[:, :, q, :].unsqueeze(2).broadcast_to([128, 8, 16, 16]), ALU.mult,
                        [oh, sif], [oh])
                    OP("dve", lambda e, q=q: e.tensor_reduce(out=k01[:, q, :, :], in_=oh[:], axis=AX.X, op=ALU.add), [oh], [k01])
                stt(k01[:, 0, :, :], k01[:, 0, :, :], 128.0, k01[:, 1, :, :], ALU.mult, ALU.add, [k01], [k01])
                cp("dve", e_idx[:, t, :].rearrange("p (h k) -> p h k", k=16), k01[:, 0, :, :], [k01], [e_idx])
                tt_("dve", cvv[:], cvv[:], cvv[:, :, 0:1].broadcast_to([128, 8, 16]), ALU.subtract, [cvv], [cvv])
                act(cvv[:], cvv[:], AF.Exp, [cvv], [cvv])
                OP("dve", lambda e: e.tensor_reduce(out=g8p[:, 0, :], in_=cvv[:], axis=AX.X, op=ALU.add), [cvv], [g8p])
                OP("dve", lambda e: e.reciprocal(out=g8p[:, 1, :], in_=g8p[:, 0, :]), [g8p], [g8p])
                tt_("dve", g_all[:, t, :].rearrange("p (h k) -> p h k", k=16), cvv[:],
                    g8p[:, 1, :].unsqueeze(2).broadcast_to([128, 8, 16]), ALU.mult, [cvv, g8p], [g_all])
        if l == 0:
            tap("e_idx", e_idx, e_idx[:], [128, NT, 128], I32)
            tap("g_all", g_all, g_all[:], [128, NT, 128])

        chk(11)
        P.barrier()
        ar["off"] = peer_off
        pump(128)
        NG = 10
        gb_ = [A("gbuf%d" % i, [128, 2 * D], BF16) for i in range(NG)]
        junks = [A("junk%d" % i, [128, D], BF16) for i in range(2)]
        dgs = [A("dg%d" % i, [128, 2, 128], BF16) for i in range(4)]
        asg = [A("asg%d" % i, [128, 2], F32) for i in range(8)]
        awg = [A("awg%d" % i, [128, 2], F32) for i in range(8)]
        pre = A("pre2", [128, D], F32)
        DMA("sp", lnp[:, 0, :], I["lnp"][l, 2], writes=[lnp])
        DMA("sp", lnp[:, 1, :], I["lnp"][l, 3], writes=[lnp])

        def finish_v(t, a0, a1):
            for hf, a_ in ((0, a0), (1, a1)):
                tt_("dve", pre[:, hf * 512:(hf + 1) * 512], a_[:], mrep[:, 5120 + hf * 512:5120 + (hf + 1) * 512], ALU.mult,
                    [a_, mrep], [pre])
            stt(pre[:], x[:, t, :], DN_ALPHA, pre[:], ALU.mult, ALU.add, [x, pre], [pre])
            mean, rstd = ln_stats(pre[:], pre)
            ts_("dve", pre[:], pre[:], mean, rstd, ALU.subtract, ALU.mult, [pre, small], [pre])
            tt_("dve", pre[:], pre[:], lnp[:, 0, :], ALU.mult, [pre, lnp], [pre])
            tt_("dve", x[:, t, :], pre[:], lnp[:, 1, :], ALU.add, [pre, lnp], [x])

        NSTEP = NT * 64
        accs = {}

        def st_A(i):
            t, q = divmod(i, 64)
            as_ = asg[i % 8]
            for j in range(2):
                s_ = q * 2 + j
                gb = gb_[(2 * i + j) % NG]
                P.dma("pool", lambda e, gb=gb, s_=s_, t=t: e.indirect_dma_start(
                    out=gb[:], out_offset=None, in_=uvb[l],
                    in_offset=bass.IndirectOffsetOnAxis(ap=e_idx[:, t, s_:s_ + 1], axis=0)),
                    [e_idx.r, uvbuf[l].r], [gb.r])
            for j in range(2):
                jk = junks[j]
                gb = gb_[(2 * i + j) % NG]
                stt(jk[:], gb[:, 0:D], 1.0, h2tok[:, t, :], ALU.mult, ALU.mult, [gb, h2tok], [jk, as_],
                    accum_out=as_[:, j:j + 1])

        def st_B(i):
            act(awg[i % 8][:], asg[i % 8][:], AF.Gelu, [asg[i % 8]], [awg[i % 8]])

        def st_CDE(i):
            t, q = divmod(i, 64)
            aw_, dg = awg[i % 8], dgs[i % 4]
            if q == 0:
                accs[t] = (hbank(), hbank())
            a0, a1 = accs[t]
            tt_("dve", aw_[:], aw_[:], g_all[:, t, q * 2:q * 2 + 2], ALU.mult, [aw_, g_all], [aw_])
            for j in range(2):
                act(dg[:, j, :], ident[:], AF.Identity, [ident, aw_], [dg], scale=aw_[:, j:j + 1])
            for j in range(2):
                s_ = q * 2 + j
                gb = gb_[(2 * i + j) % NG]
                mm(a0[:], dg[:, j, :], gb[:, D:D + 512], s_ == 0, s_ == 127, [dg, gb], [a0])
                mm(a1[:], dg[:, j, :], gb[:, D + 512:2 * D], s_ == 0, s_ == 127, [dg, gb], [a1])
            if q == 63:
                finish_v(t, a0, a1)

        for i in range(NSTEP + 2):
            if i < NSTEP:
                st_A(i)
            if 0 <= i - 1 < NSTEP:
                st_B(i - 1)
            if 0 <= i - 2 < NSTEP:
                st_CDE(i - 2)
        if l == 0:
            tap("x2", x, x[:], [128, NT, D])

    try:
        for l in range(NL):
            layer_body(l)
    except _Stop:
        pass

    P.barrier()
    for t in range(NT):
        DMA("sp", O["y"][t * 128:(t + 1) * 128, :], x[:, t, :], reads=[x], is_output=True)
    P.finish()
    return tap_out


def _na_tables():
    rows, kr = 16, 8
    r = np.arange(rows)
    start = np.clip(r - kr // 2, 0, rows - kr)
    rowvalid = np.zeros((rows, rows), bool)
    for i in range(rows):
        rowvalid[i, start[i]:start[i] + kr] = True
    qc = np.arange(64)
    qstart = np.clip(qc - 8, 0, 48)
    kc = np.arange(64)
    colvalid = (kc[None, :] >= qstart[:, None]) & (kc[None, :] < qstart[:, None] + 16)
    return rowvalid, colvalid


def host_inputs(inp, NL=DEPTH):
    f = np.float32
    rowvalid, colvalid = _na_tables()
    com = {}
    com["w_mod"] = np.ascontiguousarray(inp["w_mod"][:NL])
    com["b_mod"] = np.ascontiguousarray(inp["b_mod"][:NL].reshape(NL, 1, 6144))
    com["w_in"] = np.ascontiguousarray(inp["w_in"][:NL])
    cw = inp["conv_w"][:NL]
    com["cw"] = np.ascontiguousarray(cw.reshape(NL, 5, 6, 128).transpose(0, 3, 2, 1))
    com["cb"] = np.ascontiguousarray(inp["conv_b"][:NL].reshape(NL, 6, 128).transpose(0, 2, 1))
    com["alog"] = np.ascontiguousarray(np.broadcast_to(inp["ssd_a_log"][:NL].reshape(NL, 1, 16), (NL, 128, 16)))
    com["dtb"] = np.ascontiguousarray(np.broadcast_to(inp["ssd_dt_bias"][:NL].reshape(NL, 1, 16), (NL, 128, 16)))
    dsk = np.zeros((NL, 128, 4), f)
    for h in range(8):
        dsk[:, (h % 2) * 64:(h % 2) * 64 + 64, h // 2] = inp["ssd_d"][:NL, h][:, None]
    com["dsk"] = dsk
    com["ng"] = np.ascontiguousarray(inp["ssd_norm_g"][:NL].reshape(NL, 4, 128).transpose(0, 2, 1))
    com["qn"] = np.ascontiguousarray(np.broadcast_to(inp["gqa_q_norm"][:NL][:, None, :], (NL, 128, 64)))
    com["kn"] = np.ascontiguousarray(np.broadcast_to(inp["gqa_k_norm"][:NL][:, None, :], (NL, 128, 64)))
    lnp = np.stack([inp["ln1_g"][:NL], inp["ln1_b"][:NL], inp["ln2_g"][:NL], inp["ln2_b"][:NL]], 1)
    com["lnp"] = np.ascontiguousarray(np.broadcast_to(lnp[:, :, None, :], (NL, 4, 128, D)))
    com["w_branch"] = np.ascontiguousarray(inp["w_branch"][:NL])
    com["w_out"] = np.ascontiguousarray(inp["w_out"][:NL])
    com["wq"] = np.ascontiguousarray(inp["peer_wq"][:NL])
    com["keysT"] = np.ascontiguousarray(inp["peer_keys"][:NL].reshape(NL, 16, 128, 128).transpose(0, 1, 3, 2))
    for i in range(NL):
        com["uv%d" % i] = np.ascontiguousarray(np.concatenate([inp["peer_u"][i], inp["peer_v"][i]], axis=1))
    t = np.arange(1024)
    qaug = (t[None, :] // 64 == np.arange(16)[:, None]).astype(f)
    com["qaug"] = qaug
    rpb = inp["na_rpb"][:NL]
    tts = np.zeros((NL, 8, 2, 64, 30, 64), f)
    kc = np.arange(64)[:, None]
    qc = np.arange(64)[None, :]
    dc = np.clip(kc - qc + 15, 0, 30)
    band = colvalid.T
    for p2 in range(2):
        for e in range(30):
            dr = p2 - (e - 14)
            if -7 <= dr <= 7:
                tile = rpb[:, :, dr + 7, :][:, :, dc]
                tts[:, :, p2, :, e, :] = np.where(band[None, None], tile, f(NEG))
    tts = tts.reshape(NL, 8, 128, 30 * 64)
    ttp = np.zeros_like(tts)

    half = 32
    inv = 1.0 / (10000.0 ** (np.arange(0, half, 2, dtype=np.float32) / half))

    maps = []
    for core in range(8):
        m = dict(com)
        sample = core >= 4
        if sample:
            b = core - 4
            m["x0"] = np.ascontiguousarray(inp["x_sample"][b])
            cond = inp["c"][b]
            m["c_nakT"] = np.ascontiguousarray(inp["cache_na_k"][b, :NL].transpose(0, 2, 3, 1))
            m["c_nav"] = np.ascontiguousarray(inp["cache_na_v"][b, :NL].reshape(NL, 256, 512))
            m["c_gkT"] = np.ascontiguousarray(inp["cache_gqa_k"][b, :NL].transpose(0, 2, 3, 1))
            m["c_gv"] = np.ascontiguousarray(inp["cache_gqa_v"][b, :NL].reshape(NL, 256, 128))
            st = inp["state_ssd"][b, :NL]
            st = st.reshape(NL, 2, 2, 4, 64, 64).transpose(0, 1, 2, 5, 3, 4)
            m["h0"] = np.ascontiguousarray(st.reshape(NL, 2, 128, 256))
            m["tt"] = tts
            rm_na = np.where(rowvalid.T, 0.0, NEG * 8).astype(f)
            rm_g = np.zeros((16, 16), f)
            ctxv = 0.0
            pos_r = (t // 64).astype(f)
            pos_c = (t % 64).astype(f)
            cos = np.ones((1024, 64), f)
            sin = np.zeros((1024, 64), f)
            for a, pos in enumerate((pos_r, pos_c)):
                ang = pos[:, None] * inv[None, :]
                cos[:, a * 32:a * 32 + 16] = np.cos(ang)
                cos[:, a * 32 + 16:a * 32 + 32] = np.cos(ang)
                sin[:, a * 32:a * 32 + 16] = -np.sin(ang)
                sin[:, a * 32 + 16:a * 32 + 32] = np.sin(ang)
            keep = np.ones((128, 16), f)
            cm = np.ones((128, 1), f)
        else:
            m["x0"] = np.ascontiguousarray(inp["x_prompt"][core * 4:(core + 1) * 4].reshape(1024, D))
            cond = inp["c_ctx"]
            m["c_nakT"] = np.zeros((NL, 8, 64, 256), f)
            m["c_nav"] = np.zeros((NL, 256, 512), f)
            m["c_gkT"] = np.zeros((NL, 2, 64, 256), f)
            m["c_gv"] = np.zeros((NL, 256, 128), f)
            m["h0"] = np.zeros((NL, 2, 128, 256), f)
            m["tt"] = ttp
            seq = np.arange(16) // 4
            rm_na = np.where(seq[:, None] == seq[None, :], 0.0, NEG * 8).astype(f)
            rm_g = rm_na
            ctxv = NEG * 8
            cos = np.ones((1024, 64), f)
            sin = np.zeros((1024, 64), f)
            keep = np.ones((128, 16), f)
            keep[:, [0, 2, 4, 6]] = 0.0
            keep[:, [8 + 1, 8 + 3, 8 + 5, 8 + 7]] = 0.0
            cm = np.zeros((128, 1), f)
        m["cond_rep"] = np.ascontiguousarray(np.broadcast_to(cond.reshape(8, 128).T[:, :, None], (128, 8, 128)))
        for nm, rm in (("kaug_na", rm_na), ("kaug_g", rm_g)):
            ka = np.zeros((16, 1280), f)
            ka[:, :1024] = rm[t // 64, :].T
            ka[:, 1024:] = ctxv
            m[nm] = ka
        m["cosT"] = np.ascontiguousarray(cos.reshape(8, 128, 64).transpose(1, 0, 2))
        m["sinT"] = np.ascontiguousarray(sin.reshape(8, 128, 64).transpose(1, 0, 2))
        m["keep"] = keep
        m["cmask"] = cm
        maps.append({k: np.ascontiguousarray(v, dtype=np.float32) for k, v in m.items()})
    return maps


def assemble(results, NL=DEPTH):
    f = np.float32
    y_p = np.concatenate([results[c]["y"].reshape(4, 256, D) for c in range(4)], 0)
    y_s = np.stack([results[c]["y"] for c in range(4, 8)], 0)

    def cache(name, nh):
        parts = []
        for c in range(4):
            a = results[c][name].reshape(NL, 4, 256, nh, 64).transpose(1, 0, 2, 3, 4)
            parts.append(a)
        return np.ascontiguousarray(np.concatenate(parts, 0), dtype=f)

    nak, nav, gk, gv = cache("o_nak", 8), cache("o_nav", 8), cache("o_gk", 2), cache("o_gv", 2)
    sts = []
    for c in range(4):
        a = results[c]["o_st"].reshape(NL, 4, 2, 2, 64, 4, 64)
        a = a.transpose(1, 0, 2, 3, 5, 6, 4).reshape(4, NL, 2, 8, 64, 64)
        sts.append(a)
    st = np.ascontiguousarray(np.concatenate(sts, 0), dtype=f)
    return (y_p.astype(f), y_s.astype(f), nak, nav, gk, gv, st)


def kernel(**inputs):
    inp = {k: np.asarray(v) for k, v in inputs.items()}
    nc = bass.Bass("TRN2", target_bir_lowering=False)
    build(nc)
    maps = host_inputs(inp)
    res = run_bass_kernel_spmd(nc, maps, core_ids=list(range(8)))
    return assemble(res.results)
```

```python
import contextlib
import sys
import numpy as np
import concourse.bass as bass
import concourse.mybir as mybir
from concourse.alu_op_type import AluOpType as ALU
from concourse.bass_utils import run_bass_kernel_spmd

F32 = mybir.dt.float32
BF16 = mybir.dt.bfloat16
U32 = mybir.dt.uint32
I32 = mybir.dt.int32
AF = mybir.ActivationFunctionType
AX = mybir.AxisListType

D = 1024
NT = 8
DEPTH = 4
EPS = 1e-6
DN_ALPHA = (2 * DEPTH) ** 0.25
NEG = -30000.0
IN_COLS = 6672
OFF = dict(z=0, xs=512, B=1024, C=1152, dt=1280, naq=1296, nak=1808, nav=2320, gq=2832, gk=3344, gv=3472, gate=3600)


class Res:
    __slots__ = ("name", "last_w", "reads", "ch")

    def __init__(self, name):
        self.name = name
        self.last_w = None
        self.reads = {}
        self.ch = None


class Prog:
    ENG = ("pe", "act", "dve", "pool", "sp")

    def __init__(self, nc):
        self.nc = nc
        self.stack = contextlib.ExitStack()
        self.ops = {e: [] for e in self.ENG}
        self.cnt = {}
        self.sems = {}
        self.seen = {e: {} for e in self.ENG}
        self.pend = {e: {} for e in self.ENG}
        self.out_events = []
        self.nres = 0
        for e in self.ENG:
            self._sem("E_" + e)

    def _sem(self, key):
        if key not in self.sems:
            self.sems[key] = self.stack.enter_context(self.nc.semaphore("s" + key))
            self.cnt[key] = 0
        return self.sems[key]

    def sb(self, name, shape, dtype):
        return self.stack.enter_context(self.nc.sbuf_tensor("sb_" + name, list(shape), dtype))

    def ps(self, name, shape, dtype=F32):
        return self.stack.enter_context(self.nc.psum_tensor("ps_" + name, list(shape), dtype))

    def res(self, name=None):
        self.nres += 1
        return Res("%s_%d" % (name or "r", self.nres))

    def _deps(self, e, reads, writes):
        deps = dict(self.pend[e])
        self.pend[e] = {}

        def add(ev):
            if ev is None:
                return
            k, v = ev
            if deps.get(k, 0) < v:
                deps[k] = v

        for r in reads:
            add(r.last_w)
        for w in writes:
            add(w.last_w)
            for k, v in w.reads.items():
                add((k, v))
        waits = []
        seen = self.seen[e]
        own = "E_" + e
        for k, v in deps.items():
            if e == "pe" and k == own:
                continue
            if seen.get(k, 0) >= v:
                continue
            seen[k] = v
            waits.append((k, v))
        return waits

    def _commit(self, ev, reads, writes):
        k, v = ev
        for r in reads:
            if r.reads.get(k, 0) < v:
                r.reads[k] = v
        for w in writes:
            w.last_w = ev
            w.reads = {}

    def op(self, e, fn, reads=(), writes=()):
        waits = self._deps(e, reads, writes)
        key = "E_" + e
        self.cnt[key] += 1
        ev = (key, self.cnt[key])
        self.ops[e].append((waits, fn, key, 1, self._where()))
        self._commit(ev, reads, writes)
        return ev

    @staticmethod
    def _where():
        f = sys._getframe(2)
        out = []
        while f is not None and len(out) < 5:
            out.append(f.f_lineno)
            f = f.f_back
        return out

    def dma(self, q, fn, reads=(), writes=(), is_output=False):
        tgt = writes[0] if writes else reads[0]
        if tgt.ch is None:
            tgt.ch = "D_" + tgt.name.rsplit("_", 1)[0]
            self._sem(tgt.ch)
        key = tgt.ch
        waits = self._deps(q, reads, writes)
        self.cnt[key] += 16
        ev = (key, self.cnt[key])
        self.ops[q].append((waits, fn, key, 16, self._where()))
        self._commit(ev, reads, writes)
        if is_output:
            self.out_events.append(ev)
        return ev

    def barrier(self):
        snap = dict(self.cnt)
        for e in self.ENG:
            for k, v in snap.items():
                if v > 0 and self.pend[e].get(k, 0) < v:
                    self.pend[e][k] = v

    def finish(self):
        finals = {}
        for k, v in self.out_events:
            finals[k] = max(finals.get(k, 0), v)
        for e in self.ENG:
            if e != "sp" and self.cnt["E_" + e] > 0:
                finals["E_" + e] = self.cnt["E_" + e]
        final_waits = list(finals.items())
        nc, sems, ops = self.nc, self.sems, self.ops

        needed = {}
        for e in self.ENG:
            for waits, fn, key, inc, where in ops[e]:
                for k, v in waits:
                    needed.setdefault(k, set()).add(v)
        for k, v in final_waits:
            needed.setdefault(k, set()).add(v)
        remap = {}
        for e in self.ENG:
            key = "E_" + e
            need = needed.get(key, set())
            m = {}
            new_c = 0
            old_c = 0
            lst = []
            for waits, fn, k2, inc, where in ops[e]:
                if k2 == key:
                    old_c += 1
                    if old_c in need:
                        new_c += 1
                        m[old_c] = new_c
                        lst.append((waits, fn, k2, 1, where))
                    else:
                        lst.append((waits, fn, k2, 0, where))
                else:
                    lst.append((waits, fn, k2, inc, where))
            ops[e] = lst
            remap[key] = m

        def rv(k, v):
            return remap[k][v] if k in remap else v

        final_waits = [(k, rv(k, v)) for k, v in final_waits]

        def replay(eng, lst):
            for waits, fn, key, inc, where in lst:
                for k, v in waits:
                    eng.wait_ge(sems[k], rv(k, v))
                try:
                    inst = fn(eng)
                    if inc:
                        inst.then_inc(sems[key], inc)
                except Exception:
                    print("FAILED OP created at lines", where)
                    raise

        with nc.Block() as block:
            @block.tensor
            def _(eng):
                replay(eng, ops["pe"])

            @block.scalar
            def _(eng):
                replay(eng, ops["act"])

            @block.vector
            def _(eng):
                replay(eng, ops["dve"])

            @block.gpsimd
            def _(eng):
                replay(eng, ops["pool"])

            @block.sync
            def _(eng):
                replay(eng, ops["sp"])
                for k, v in final_waits:
                    eng.wait_ge(sems[k], v)
        self.stack.close()


class Buf:
    __slots__ = ("t", "r")

    def __init__(self, t, r):
        self.t = t
        self.r = r

    def __getitem__(self, idx):
        return self.t[idx]


class _Stop(Exception):
    pass


def build(nc, NL=DEPTH, taps=None, stop=None):
    P = Prog(nc)
    taps = taps or []
    tap_out = {}

    def din(name, shape, dt=F32):
        return nc.dram_tensor(name, list(shape), dt, kind="ExternalInput").ap()

    def dout(name, shape, dt=F32):
        return nc.dram_tensor(name, list(shape), dt, kind="ExternalOutput").ap()

    I = dict(
        x0=din("x0", [1024, D]), cond_rep=din("cond_rep", [128, 8, 128]),
        w_mod=din("w_mod", [NL, D, 6144]), b_mod=din("b_mod", [NL, 1, 6144]),
        w_in=din("w_in", [NL, D, IN_COLS]), cw=din("cw", [NL, 128, 6, 5]), cb=din("cb", [NL, 128, 6]),
        alog=din("alog", [NL, 128, 16]), dtb=din("dtb", [NL, 128, 16]),
        dsk=din("dsk", [NL, 128, 4]), ng=din("ng", [NL, 128, 4]),
        tt=din("tt", [NL, 8, 128, 30 * 64]), qn=din("qn", [NL, 128, 64]), kn=din("kn", [NL, 128, 64]),
        lnp=din("lnp", [NL, 4, 128, D]),
        w_branch=din("w_branch", [NL, 3, 512, D]), w_out=din("w_out", [NL, D, D]),
        wq=din("wq", [NL, D, 2048]), keysT=din("keysT", [NL, 16, 128, 128]),
        uv=[din("uv%d" % i, [16384, 2 * D]) for i in range(NL)],
        c_nakT=din("c_nakT", [NL, 8, 64, 256]), c_nav=din("c_nav", [NL, 256, 512]),
        c_gkT=din("c_gkT", [NL, 2, 64, 256]), c_gv=din("c_gv", [NL, 256, 128]),
        h0=din("h0", [NL, 2, 128, 256]),
        qaug=din("qaug", [16, 1024]), kaug_na=din("kaug_na", [16, 1280]), kaug_g=din("kaug_g", [16, 1280]),
        cosT=din("cosT", [128, 8, 64]), sinT=din("sinT", [128, 8, 64]),
        keep=din("keep", [128, 16]), cmask=din("cmask", [128, 1]),
    )
    O = dict(
        y=dout("y", [1024, D]), o_nak=dout("o_nak", [NL, 1024, 512]), o_nav=dout("o_nav", [NL, 1024, 512]),
        o_gk=dout("o_gk", [NL, 1024, 128]), o_gv=dout("o_gv", [NL, 1024, 128]),
        o_st=dout("o_st", [NL, 4, 2, 128, 256]),
    )

    def B(name, shape, dt):
        return Buf(P.sb(name, shape, dt), P.res(name))

    def OP(e, fn, reads=(), writes=()):
        ws = [b.r for b in writes] + [b.r for b in reads if b.r.name.startswith("bank")]
        P.op(e, fn, [b.r for b in reads], ws)

    def DMA(q, out, in_, reads=(), writes=(), is_output=False):
        P.dma(q, lambda e: e.dma_start(out=out, in_=in_), [b.r for b in reads], [b.r for b in writes], is_output)

    def mm(out, lhsT, rhs, start, stop, reads, writes):
        OP("pe", lambda e: e.matmul(out, lhsT=lhsT, rhs=rhs, start=start, stop=stop), reads, writes)

    def tp(out, in_, ident, reads, writes):
        OP("pe", lambda e: e.transpose(out=out, in_=in_, identity=ident), reads, writes)

    def act(out, in_, func, reads, writes, bias=None, scale=None, accum_out=None):
        kw = {}
        if bias is not None:
            kw["bias"] = bias
        if scale is not None:
            kw["scale"] = scale
        if accum_out is not None:
            kw["accum_out"] = accum_out
        OP("act", lambda e: e.activation(out=out, in_=in_, func=func, **kw), reads, writes)

    def tt_(eng, out, in0, in1, op, reads, writes):
        OP(eng, lambda e: e.tensor_tensor(out=out, in0=in0, in1=in1, op=op), reads, writes)

    def ts_(eng, out, in0, s1, s2, op0, op1, reads, writes):
        if op1 is None:
            OP(eng, lambda e: e.tensor_scalar(out=out, in0=in0, scalar1=s1, scalar2=None, op0=op0), reads, writes)
        else:
            OP(eng, lambda e: e.tensor_scalar(out=out, in0=in0, scalar1=s1, scalar2=s2, op0=op0, op1=op1), reads, writes)

    def stt(out, in0, scalar, in1, op0, op1, reads, writes, accum_out=None):
        if accum_out is None:
            OP("dve", lambda e: e.scalar_tensor_tensor(out=out, in0=in0, scalar=scalar, in1=in1, op0=op0, op1=op1),
               reads, writes)
        else:
            OP("dve", lambda e: e.scalar_tensor_tensor(out=out, in0=in0, scalar=scalar, in1=in1, op0=op0, op1=op1,
                                                       accum_out=accum_out), reads, writes)

    def cp(eng, out, in_, reads, writes):
        if eng == "act":
            OP("act", lambda e: e.copy(out=out, in_=in_), reads, writes)
        else:
            OP(eng, lambda e: e.tensor_copy(out=out, in_=in_), reads, writes)

    def tap(name, buf, ap, shape, dt=F32):
        if name in taps:
            d = dout("tap_" + name, shape, dt)
            tap_out[name] = d
            DMA("sp", d, ap, reads=[buf], is_output=True)

    banks = [Buf(P.ps("bank%d" % i, [128, 512], F32), P.res("bank%d" % i)) for i in range(8)]
    rot = [0, 0]

    def sbank():
        b = banks[rot[0] % 4]
        rot[0] += 1
        return b

    def hbank():
        b = banks[4 + rot[1] % 4]
        rot[1] += 1
        return b

    x = B("x", [128, NT, D], F32)
    mrep = B("mrep", [128, 6144], F32)
    hT = B("hT", [128, 8, 1024], BF16)
    lnp = B("lnp", [128, 2, D], F32)
    ident = B("ident", [128, 128], BF16)
    identf = B("identf", [128, 128], F32)
    TRIf = B("TRIf", [128, 128], F32)
    TRIb = B("TRIb", [128, 128], F32)
    maskf = B("maskf", [128, 128], F32)
    maskb = B("maskb", [128, 128], F32)
    onesb = B("onesb", [128, 128], BF16)
    onesf = B("onesf", [128, 128], F32)
    iota16 = B("iota16", [128, 16], F32)
    condS = B("condS", [128, 8, 128], BF16)
    small = B("small", [128, 64], F32)
    keep = B("keep", [128, 16], F32)
    cmask = B("cmask", [128, 1], F32)
    cosT = B("cosT", [128, 8, 64], F32)
    sinT = B("sinT", [128, 8, 64], F32)
    NW = 3
    wbuf = [B("wbuf%d" % i, [128, 8, 512], BF16) for i in range(NW)]
    wrot = [0]

    uvb = [nc.dram_tensor("uvb%d" % i, [16384, 2 * D], BF16, kind="Internal").ap() for i in range(NL)]
    uvbuf = [Buf(None, P.res("uvb")) for i in range(NL)]
    NSTG = 3
    stgc = [B("stgc%d" % i, [128, 2 * D], BF16) for i in range(NSTG)]
    conv = {"l": 0, "i": 128}

    def start_conv(l_):
        conv["l"] = l_
        conv["i"] = 0

    def pump(n):
        while n > 0 and conv["i"] < 128:
            i_, l_ = conv["i"], conv["l"]
            st = stgc[i_ % NSTG]
            DMA("pool", st[:], I["uv"][l_][i_ * 128:(i_ + 1) * 128, :], writes=[st])
            DMA("sp", uvb[l_][i_ * 128:(i_ + 1) * 128, :], st[:], reads=[st], writes=[uvbuf[l_]])
            conv["i"] += 1
            n -= 1

    ARENA_W = 20480
    arena = P.sb("arena", [128, ARENA_W], F32)
    ar = {"off": 0, "n": 0}

    def stage_begin():
        P.barrier()
        ar["off"] = 0

    def A(name, shape, dt, parts=128):
        free = int(np.prod(shape[1:]))
        words = (free * (2 if dt == BF16 else 4) + 3) // 4
        words = (words + 7) // 8 * 8
        o = ar["off"]
        assert o + words <= ARENA_W, ("arena overflow", name, o, words)
        ar["off"] = o + words
        ar["n"] += 1
        v = arena[0:shape[0], o:o + words]
        if dt != F32:
            v = v.bitcast(dt)
        v = v[:, 0:free]
        if len(shape) == 3:
            v = v.rearrange("p (a b) -> p a b", b=shape[2])
        elif len(shape) == 4:
            v = v.rearrange("p (a b c) -> p a b c", b=shape[2], c=shape[3])
        elif len(shape) == 5:
            v = v.rearrange("p (a b c d) -> p a b c d", b=shape[2], c=shape[3], d=shape[4])
        return Buf(v, P.res(name))

    def load_w(src, ncols, npump=3):
        wb = wbuf[wrot[0] % NW]
        wrot[0] += 1
        DMA("pool", wb[:, :, 0:ncols], src.rearrange("(k p) c -> p k c", p=128), writes=[wb])
        pump(npump)
        return wb

    OP("pool", lambda e: e.memset(onesf[:], 1.0), writes=[onesf])
    OP("pool", lambda e: e.memset(onesb[:], 1.0), writes=[onesb])
    OP("pool", lambda e: e.memset(identf[:], 1.0), writes=[identf])
    OP("pool", lambda e: e.affine_select(out=identf[:], in_=identf[:], pattern=[[-1, 128]], compare_op=ALU.is_equal,
                                         fill=0.0, base=0, channel_multiplier=1), reads=[identf], writes=[identf])
    cp("pool", ident[:], identf[:], [identf], [ident])
    OP("pool", lambda e: e.affine_select(out=TRIf[:], in_=onesf[:], pattern=[[1, 128]], compare_op=ALU.is_ge,
                                         fill=0.0, base=0, channel_multiplier=-1), reads=[onesf], writes=[TRIf])
    OP("pool", lambda e: e.affine_select(out=TRIb[:], in_=onesf[:], pattern=[[-1, 128]], compare_op=ALU.is_ge,
                                         fill=0.0, base=0, channel_multiplier=1), reads=[onesf], writes=[TRIb])
    OP("pool", lambda e: e.memset(maskf[:], 0.0), writes=[maskf])
    OP("pool", lambda e: e.memset(maskb[:], 0.0), writes=[maskb])
    OP("pool", lambda e: e.affine_select(out=maskf[:], in_=maskf[:], pattern=[[1, 128]], compare_op=ALU.is_ge,
                                         fill=NEG, base=0, channel_multiplier=-1), reads=[maskf], writes=[maskf])
    OP("pool", lambda e: e.affine_select(out=maskb[:], in_=maskb[:], pattern=[[-1, 128]], compare_op=ALU.is_ge,
                                         fill=NEG, base=0, channel_multiplier=1), reads=[maskb], writes=[maskb])
    OP("pool", lambda e: e.iota(iota16[:], pattern=[[1, 16]], base=0, channel_multiplier=0,
                                allow_small_or_imprecise_dtypes=True), writes=[iota16])
    DMA("sp", keep[:], I["keep"], writes=[keep])
    DMA("sp", cmask[:], I["cmask"], writes=[cmask])
    DMA("sp", cosT[:], I["cosT"], writes=[cosT])
    DMA("sp", sinT[:], I["sinT"], writes=[sinT])
    condf = A("condf", [128, 8, 128], F32)
    DMA("sp", condf[:], I["cond_rep"], writes=[condf])
    act(condS[:], condf[:], AF.Silu, [condf], [condS])
    DMA("sp", x[:], I["x0"].rearrange("(t p) d -> p t d", p=128), writes=[x])

    def ln_stats(src_ap, srcbuf):
        OP("dve", lambda e: e.bn_stats(out=small[:, 0:6], in_=src_ap[:, 0:512]), [srcbuf], [small])
        OP("dve", lambda e: e.bn_stats(out=small[:, 6:12], in_=src_ap[:, 512:1024]), [srcbuf, small], [small])
        OP("dve", lambda e: e.bn_aggr(out=small[:, 12:14], in_=small[:, 0:12]), [small], [small])
        act(small[:, 14:15], small[:, 13:14], AF.Ln, [small], [small], bias=EPS)
        act(small[:, 15:16], small[:, 14:15], AF.Exp, [small], [small], scale=-0.5)
        return small[:, 12:13], small[:, 15:16]

    def ln_modulate(shift_off, scale_off, tmpA, hb, h2tok=None):
        for t in range(NT):
            mean, rstd = ln_stats(x[:, t, :], x)
            ts_("dve", tmpA[:], x[:, t, :], mean, rstd, ALU.subtract, ALU.mult, [x, small], [tmpA])
            tt_("dve", tmpA[:], tmpA[:], mrep[:, scale_off:scale_off + D], ALU.mult, [tmpA, mrep], [tmpA])
            dst = hb if h2tok is None else h2tok
            dst_ap = hb[:] if h2tok is None else h2tok[:, t, :]
            tt_("dve", dst_ap, tmpA[:], mrep[:, shift_off:shift_off + D], ALU.add, [tmpA, mrep], [dst])
            bk = sbank()
            bkb = bk[:].bitcast(BF16)
            for k in range(8):
                tp(bkb[:, k * 128:(k + 1) * 128], dst_ap[:, k * 128:(k + 1) * 128], ident[:], [dst, ident], [bk])
            cp("act", hT[:, :, t * 128:(t + 1) * 128], bkb.rearrange("p (k t) -> p k t", t=128), [bk], [hT])

    def ln_affine(pre, gi, tmpbuf):
        pass

    def chk(k):
        if stop == k:
            raise _Stop()

    def layer_body(l):
        stage_begin()
        start_conv(l)
        bmod = A("bmod", [1, 6144], BF16)
        DMA("pool", bmod[:], I["b_mod"][l], writes=[bmod])
        for cc in range(12):
            wb = load_w(I["w_mod"][l][:, cc * 512:(cc + 1) * 512], 512, npump=0)
            bk = sbank()
            for k in range(8):
                mm(bk[:], condS[:, k, :], wb[:, k, :], k == 0, False, [condS, wb], [bk])
            mm(bk[:], onesb[0:1, :], bmod[0:1, cc * 512:(cc + 1) * 512], False, True, [onesb, bmod], [bk])
            if cc in (2, 3, 8, 9):
                act(mrep[:, cc * 512:(cc + 1) * 512], bk[:], AF.Identity, [bk], [mrep], bias=1.0)
            else:
                cp("act", mrep[:, cc * 512:(cc + 1) * 512], bk[:], [bk], [mrep])
        if l == 0:
            tap("mrep", mrep, mrep[:], [128, 6144])
        chk(1)

        tmpA = A("tmpA", [128, D], F32)
        hb = A("hb", [128, D], BF16)
        ln_modulate(0, 1024, tmpA, hb)
        if l == 0:
            tap("hT", hT, hT[:], [128, 8, 1024], BF16)
        chk(2)

        stage_begin()
        yA = A("yA", [128, 4, 1024], BF16)
        yB = A("yB", [128, 4, 1024], BF16)
        yC = A("yC", [128, 4, 1024], BF16)
        mix_off = ar["off"]
        xc = A("xc", [128, 6, 1024], BF16)
        x_tok = A("x_tok", [128, 8, 512], BF16)
        B_tok = A("B_tok", [128, 8, 128], BF16)
        Sin = A("Sin", [128, 8, 2, 256], BF16)
        a_all = A("a_all", [128, 8, 16], F32)
        lndt = A("lndt", [128, 8, 16], F32)
        csT = A("csT", [128, 8, 32], F32)
        w_all = A("w_all", [128, 8, 16], F32)
        decG = A("decG", [128, 8, 2, 4], F32)
        Arep = A("Arep", [128, 16], F32)
        dtbr = A("dtbr", [128, 16], F32)
        cwb = A("cwb", [128, 6, 5], F32)
        cbb = A("cbb", [128, 6], F32)
        dskb = A("dskb", [128, 4], F32)
        ngb = A("ngb", [128, 4], F32)
        S = A("S", [128, 2, 256], F32)
        sc16 = A("sc16", [128, 4, 16], F32)
        ssd_off = ar["off"]
        xp = A("xp", [128, 6, 4, 260], BF16)
        acc = A("acc", [128, 4, 256], F32)

        DMA("sp", Arep[:], I["alog"][l], writes=[Arep])
        DMA("sp", dtbr[:], I["dtb"][l], writes=[dtbr])
        DMA("sp", cwb[:], I["cw"][l], writes=[cwb])
        DMA("sp", cbb[:], I["cb"][l], writes=[cbb])
        DMA("sp", dskb[:], I["dsk"][l], writes=[dskb])
        DMA("sp", ngb[:], I["ng"][l], writes=[ngb])
        DMA("sp", S[:], I["h0"][l].rearrange("d p f -> p d f"), writes=[S])
        act(Arep[:], Arep[:], AF.Exp, [Arep], [Arep])
        ts_("dve", Arep[:], Arep[:], -1.0, None, ALU.mult, None, [Arep], [Arep])

        OP("pool", lambda e: e.memset(xp[:], 0.0), writes=[xp])
        w1 = load_w(I["w_in"][l][:, OFF["xs"]:OFF["xs"] + 512], 512)
        w2 = load_w(I["w_in"][l][:, OFF["B"]:OFF["B"] + 272], 272)
        for k in range(6):
            wsrc, c0 = (w1, k * 128) if k < 4 else (w2, (k - 4) * 128)
            for half in range(2):
                bk = sbank()
                for kd in range(8):
                    mm(bk[:], wsrc[:, kd, c0:c0 + 128], hT[:, kd, half * 512:(half + 1) * 512], kd == 0, kd == 7,
                       [wsrc, hT], [bk])
                cp("act", xp[:, k, 2 * half:2 * half + 2, 2:258], bk[:].rearrange("p (s t) -> p s t", t=256), [bk], [xp])
        chk(21)
        for t in range(NT):
            bk = sbank()
            for kd in range(8):
                mm(bk[:, 0:16], hT[:, kd, t * 128:(t + 1) * 128], w2[:, kd, 256:272], kd == 0, kd == 7, [hT, w2], [bk])
            tt_("dve", sc16[:, 0, :], bk[:, 0:16], dtbr[:], ALU.add, [bk, dtbr], [sc16])
            act(sc16[:, 1, :], sc16[:, 0, :], AF.Exp, [sc16], [sc16])
            act(sc16[:, 2, :], sc16[:, 1, :], AF.Ln, [sc16], [sc16], bias=1.0)
            act(lndt[:, t, :], sc16[:, 2, :], AF.Ln, [sc16], [lndt])
            tt_("dve", a_all[:, t, :], sc16[:, 2, :], Arep[:], ALU.mult, [sc16, Arep], [a_all])
        chk(22)
        ts_("dve", xp[:, :, 1:4, 0:2], xp[:, :, 0:3, 256:258], cmask[:, 0:1], None, ALU.mult, None, [xp, cmask], [xp])
        ts_("dve", xp[:, :, 0:3, 258:260], xp[:, :, 1:4, 2:4], cmask[:, 0:1], None, ALU.mult, None, [xp, cmask], [xp])
        for k in range(6):
            ts_("dve", acc[:], xp[:, k, :, 0:256], cwb[:, k, 0:1], None, ALU.mult, None, [xp, cwb], [acc])
            for j in range(1, 5):
                stt(acc[:], xp[:, k, :, j:j + 256], cwb[:, k, j:j + 1], acc[:], ALU.mult, ALU.add, [xp, cwb, acc], [acc])
            act(xc[:, k, :].rearrange("p (s t) -> p s t", t=256), acc[:], AF.Silu, [acc, cbb], [xc], bias=cbb[:, k:k + 1])
        chk(23)
        for t in range(NT):
            bk = sbank()
            bkb = bk[:].bitcast(BF16)
            for k in range(5):
                tp(bkb[:, k * 128:(k + 1) * 128], xc[:, k, t * 128:(t + 1) * 128], ident[:], [xc, ident], [bk])
            cp("act", x_tok[:, t, :], bkb[:, 0:512], [bk], [x_tok])
            chk(24)
            cp("act", B_tok[:, t, :], bkb[:, 512:640], [bk], [B_tok])

        chk(3)
        P.barrier()
        ar["off"] = ssd_off
        for c in range(8):
            bk = sbank()
            mm(bk[:, 0:8], TRIf[:], a_all[:, c, 0:8], True, True, [TRIf, a_all], [bk])
            mm(bk[:, 8:16], TRIb[:], a_all[:, c, 8:16], True, True, [TRIb, a_all], [bk])
            mm(bk[:, 16:32], onesf[:], a_all[:, c, :], True, True, [onesf, a_all], [bk])
            cp("act", csT[:, c, :], bk[:, 0:32], [bk], [csT])
            act(sc16[:, 0, :], bk[:, 16:32], AF.Exp, [bk], [sc16])
            d4 = sc16[:, 0, :].rearrange("p (d g h) -> p d g h", d=2, g=2)
            cp("dve", decG[0:64, c, :, :], d4[0:64, :, 0, :], [sc16], [decG])
            cp("dve", decG[64:128, c, :, :], d4[64:128, :, 1, :], [sc16], [decG])
            tt_("dve", sc16[:, 1, :], csT[:, c, 16:32], csT[:, c, 0:16], ALU.subtract, [csT], [sc16])
            tt_("dve", sc16[:, 1, :], sc16[:, 1, :], lndt[:, c, :], ALU.add, [sc16, lndt], [sc16])
            act(w_all[:, c, :], sc16[:, 1, :], AF.Exp, [sc16], [w_all])

        xw = A("xw", [128, 512], BF16)
        for d_, order in ((0, range(8)), (1, range(7, -1, -1))):
            first = True
            for c in order:
                if not first:
                    ts_("dve", S[:, d_, :], S[:, d_, :], keep[:, d_ * 8 + c:d_ * 8 + c + 1], None, ALU.mult, None,
                        [S, keep], [S])
                first = False
                cp("act", Sin[:, c, d_, :], S[:, d_, :], [S], [Sin])
                tt_("dve", xw[:].rearrange("p (h q) -> p h q", q=64),
                    x_tok[:, c, :].rearrange("p (h q) -> p h q", q=64),
                    w_all[:, c, d_ * 8:d_ * 8 + 8].unsqueeze(2).broadcast_to([128, 8, 64]), ALU.mult,
                    [x_tok, w_all], [xw])
                bk = sbank()
                for g in range(2):
                    mm(bk[g * 64:(g + 1) * 64, 0:256], B_tok[:, c, g * 64:(g + 1) * 64], xw[:, g * 256:(g + 1) * 256],
                       True, True, [B_tok, xw], [bk])
                tt_("dve", S[:, d_, :].rearrange("p (h q) -> p h q", q=64),
                    S[:, d_, :].rearrange("p (h q) -> p h q", q=64),
                    decG[:, c, d_, :].unsqueeze(2).broadcast_to([128, 4, 64]), ALU.mult, [S, decG], [S])
                tt_("dve", S[:, d_, :], S[:, d_, :], bk[:, 0:256], ALU.add, [S, bk], [S])
                if (d_ == 0 and c % 2 == 1) or (d_ == 1 and c % 2 == 0):
                    DMA("sp", O["o_st"][l, c // 2, d_], S[:, d_, :], reads=[S], is_output=True)

        chk(4)
        arep = A("arep", [128, 16, 128], F32)
        cbT = A("cbT", [128, 2, 128], F32)
        Lw = A("Lw", [128, 4, 128], F32)
        MT = A("MT", [128, 2, 128], BF16)
        Ew = A("Ew", [128, 2, 128], F32)
        Cp = A("Cp", [128, 4, 128], BF16)
        yg32 = A("yg32", [128, 512], F32)
        ysq = A("ysq", [128, 512], F32)
        ygf = yA
        mcnt = 0
        for c in range(8):
            pump(4)
            cs_ = slice(c * 128, (c + 1) * 128)
            cp("dve", arep[:], a_all[:, c, :].unsqueeze(2).broadcast_to([128, 16, 128]), [a_all], [arep])
            reps = []
            for q in range(4):
                bk = hbank()
                for jj in range(4):
                    j = q * 4 + jj
                    mm(bk[:, jj * 128:(jj + 1) * 128], arep[:, j, :], (TRIf if j < 8 else TRIb)[:], True, True,
                       [arep, TRIf, TRIb], [bk])
                reps.append(bk)
            for g in range(2):
                bk = sbank()
                mm(bk[:, 0:128], xc[g * 64:(g + 1) * 64, 4, cs_], xc[g * 64:(g + 1) * 64, 5, cs_], True, True, [xc], [bk])
                cp("act", cbT[:, g, :], bk[:, 0:128], [bk], [cbT])
            for pr in range(4):
                ybk = sbank()
                for hh in range(2):
                    h = pr * 2 + hh
                    g, h4 = h // 4, h % 4
                    gs = slice(g * 64, (g + 1) * 64)
                    rf = reps[h // 4][:, (h % 4) * 128:(h % 4 + 1) * 128]
                    rb = reps[2 + h // 4][:, (h % 4) * 128:(h % 4 + 1) * 128]
                    m2 = mcnt % 2
                    mcnt += 1
                    stt(Lw[:, 0, :], rf, csT[:, c, h:h + 1], maskf[:], ALU.subtract, ALU.add,
                        [reps[h // 4], csT, maskf], [Lw])
                    act(Lw[:, 1, :], Lw[:, 0, :], AF.Exp, [Lw, lndt], [Lw], bias=lndt[:, c, h:h + 1])
                    stt(Lw[:, 2, :], rb, csT[:, c, 8 + h:9 + h], maskb[:], ALU.subtract, ALU.add,
                        [reps[2 + h // 4], csT, maskb], [Lw])
                    act(Lw[:, 3, :], Lw[:, 2, :], AF.Exp, [Lw, lndt], [Lw], bias=lndt[:, c, 8 + h:9 + h])
                    tt_("dve", Lw[:, 1, :], Lw[:, 1, :], Lw[:, 3, :], ALU.add, [Lw], [Lw])
                    tt_("dve", MT[:, m2, :], Lw[:, 1, :], cbT[:, g, :], ALU.mult, [Lw, cbT], [MT])
                    act(Ew[gs, 0, :], rf[gs, :], AF.Exp, [reps[h // 4]], [Ew])
                    act(Ew[gs, 1, :], rb[gs, :], AF.Exp, [reps[2 + h // 4]], [Ew])
                    tt_("dve", Cp[gs, m2 * 2, :], xc[gs, 5, cs_], Ew[gs, 0, :], ALU.mult, [xc, Ew], [Cp])
                    tt_("dve", Cp[gs, m2 * 2 + 1, :], xc[gs, 5, cs_], Ew[gs, 1, :], ALU.mult, [xc, Ew], [Cp])
                    yo = ybk[hh * 64:(hh + 1) * 64, 0:128]
                    mm(yo, x_tok[:, c, h * 64:(h + 1) * 64], MT[:, m2, :], True, False, [x_tok, MT], [ybk])
                    mm(yo, Sin[gs, c, 0, h4 * 64:(h4 + 1) * 64], Cp[gs, m2 * 2, :], False, False, [Sin, Cp], [ybk])
                    mm(yo, Sin[gs, c, 1, h4 * 64:(h4 + 1) * 64], Cp[gs, m2 * 2 + 1, :], False, True, [Sin, Cp], [ybk])
                stt(ygf[:, pr, cs_], xc[:, pr, cs_], dskb[:, pr:pr + 1], ybk[:, 0:128], ALU.mult, ALU.add,
                    [xc, dskb, ybk], [ygf])
        chk(5)
        wz = load_w(I["w_in"][l][:, OFF["z"]:OFF["z"] + 512], 512)
        for half in range(2):
            hs = slice(half * 512, (half + 1) * 512)
            sbk = hbank()
            for pr in range(4):
                bk = sbank()
                for kd in range(8):
                    mm(bk[:], wz[:, kd, pr * 128:(pr + 1) * 128], hT[:, kd, hs], kd == 0, kd == 7, [wz, hT], [bk])
                act(yg32[:], bk[:], AF.Silu, [bk], [yg32])
                tt_("dve", ygf[:, pr, hs], ygf[:, pr, hs], yg32[:], ALU.mult, [ygf, yg32], [ygf])
                tt_("dve", ysq[:], ygf[:, pr, hs], ygf[:, pr, hs], ALU.mult, [ygf], [ysq])
                mm(sbk[:], onesf[:], ysq[:], pr == 0, pr == 3, [onesf, ysq], [sbk])
            act(yg32[:], sbk[:], AF.Ln, [sbk], [yg32], bias=EPS, scale=1.0 / 512)
            act(yg32[:], yg32[:], AF.Exp, [yg32], [yg32], scale=-0.5)
            for pr in range(4):
                stt(yA[:, pr, hs], ygf[:, pr, hs], ngb[:, pr:pr + 1], yg32[:], ALU.mult, ALU.mult, [ygf, ngb, yg32], [ygf])

        chk(6)
        for br in range(2):
            P.barrier()
            ar["off"] = mix_off
            is_na = br == 0
            if br == 1:
                chk(7)
            nq = 4 if is_na else 8
            nkb = 4 if is_na else 2
            QT = A("QT", [80, nq, 1024], BF16)
            KT = A("KT", [80, nkb, 1280], BF16)
            vw = 512 if is_na else 128
            V_tok = A("V_tok", [128, 8, vw], BF16)
            V_ctx = A("V_ctx", [128, 2, vw], BF16)
            stg = A("stg", [128, 2, 512], F32)
            PT = A("PT", [128, 3, 512], BF16)
            tmpS = A("tmpS", [128, 2, 512], F32)
            rs = A("rs", [128, 512], F32)
            yout = yB if is_na else yC
            ka = I["kaug_na"] if is_na else I["kaug_g"]
            ck = I["c_nakT"] if is_na else I["c_gkT"]
            cv = I["c_nav"] if is_na else I["c_gv"]
            DMA("pool", V_ctx[:], cv[l].rearrange("(j p) c -> p j c", p=128), writes=[V_ctx])

            def fill_aug(h0_, n_):
                DMA("pool", QT[64:80, :, :], I["qaug"].unsqueeze(1).broadcast_to([16, nq, 1024]), writes=[QT])
                DMA("pool", KT[64:80, :, :], ka.unsqueeze(1).broadcast_to([16, nkb, 1280]), writes=[KT])
                DMA("pool", KT[0:64, :, 1024:1280], ck[l, h0_:h0_ + n_].rearrange("h d k -> d h k"), writes=[KT])

            if is_na:
                ttb = [A("ttb%d" % i, [128, 30 * 64], BF16) for i in range(2)]
                wqn = load_w(I["w_in"][l][:, OFF["naq"]:OFF["naq"] + 512], 512)
                wkn = load_w(I["w_in"][l][:, OFF["nak"]:OFF["nak"] + 512], 512)
                wvn = load_w(I["w_in"][l][:, OFF["nav"]:OFF["nav"] + 512], 512)
                for t in range(NT):
                    for (wsrc, dram, keepbf) in ((wkn, O["o_nak"], False), (wvn, O["o_nav"], True)):
                        bk = sbank()
                        for kd in range(8):
                            mm(bk[:], hT[:, kd, t * 128:(t + 1) * 128], wsrc[:, kd, :], kd == 0, kd == 7, [hT, wsrc], [bk])
                        si_ = 0 if not keepbf else 1
                        cp("act", stg[:, si_, :], bk[:], [bk], [stg])
                        if keepbf:
                            cp("dve", V_tok[:, t, :], bk[:], [bk], [V_tok])
                        DMA("sp", dram[l, t * 128:(t + 1) * 128, :], stg[:, si_, :], reads=[stg], is_output=True)
                groups = [list(range(0, 4)), list(range(4, 8))]
            else:
                fill_aug(0, 2)
                wqg = load_w(I["w_in"][l][:, OFF["gq"]:OFF["gq"] + 512], 512)
                wkv = load_w(I["w_in"][l][:, OFF["gk"]:OFF["gk"] + 256], 256)
                gq = A("gq", [128, 512], F32)
                gsq = A("gsq", [128, 512], F32)
                gr = A("gr", [128, 512], F32)
                qr = A("qr", [128, 512], BF16)
                g8 = A("g8", [128, 4, 8], F32)
                qnb = A("qnb", [128, 64], F32)
                knb = A("knb", [128, 64], F32)
                DMA("sp", qnb[:], I["qn"][l], writes=[qnb])
                DMA("sp", knb[:], I["kn"][l], writes=[knb])

                def norm_rope(src_ap, src_bk, nh, gain, t, cache_dram):
                    n = nh * 64
                    v3 = lambda ap: ap.rearrange("p (h q) -> p h q", q=64)
                    cp("act", gq[:, 0:n], src_ap, [src_bk], [gq])
                    act(gsq[:, 0:n], src_ap, AF.Square, [src_bk], [gsq])
                    OP("dve", lambda e: e.tensor_reduce(out=g8[:, 0, 0:nh], in_=v3(gsq[:, 0:n]), axis=AX.X, op=ALU.add),
                       [gsq], [g8])
                    act(g8[:, 1, 0:nh], g8[:, 0, 0:nh], AF.Ln, [g8], [g8], bias=EPS, scale=1.0 / 64)
                    act(g8[:, 2, 0:nh], g8[:, 1, 0:nh], AF.Exp, [g8], [g8], scale=-0.5)
                    tt_("dve", v3(gq[:, 0:n]), v3(gq[:, 0:n]), g8[:, 2, 0:nh].unsqueeze(2).broadcast_to([128, nh, 64]),
                        ALU.mult, [gq, g8], [gq])
                    tt_("dve", v3(gq[:, 0:n]), v3(gq[:, 0:n]), gain[:].unsqueeze(1).broadcast_to([128, nh, 64]),
                        ALU.mult, [gq, gain], [gq])
                    if cache_dram is not None:
                        DMA("sp", cache_dram, gq[:, 0:n], reads=[gq], is_output=True)
                    v5 = lambda ap: ap.rearrange("p (h a b c) -> p h a b c", a=2, b=2, c=16)
                    sn = sinT[:, t, :].rearrange("p (a b c) -> p a b c", a=2, b=2)
                    for b_ in range(2):
                        tt_("dve", v5(gr[:, 0:n])[:, :, :, b_, :], v5(gq[:, 0:n])[:, :, :, 1 - b_, :],
                            sn[:, :, b_, :].unsqueeze(1).broadcast_to([128, nh, 2, 16]), ALU.mult, [gq, sinT], [gr])
                    tt_("dve", v3(gsq[:, 0:n]), v3(gq[:, 0:n]), cosT[:, t, :].unsqueeze(1).broadcast_to([128, nh, 64]),
                        ALU.mult, [gq, cosT], [gsq])
                    tt_("dve", qr[:, 0:n], gsq[:, 0:n], gr[:, 0:n], ALU.add, [gsq, gr], [qr])

                for t in range(NT):
                    ts = slice(t * 128, (t + 1) * 128)
                    bk = sbank()
                    for kd in range(8):
                        mm(bk[:], hT[:, kd, ts], wqg[:, kd, :], kd == 0, kd == 7, [hT, wqg], [bk])
                    norm_rope(bk[:, 0:512], bk, 8, qnb, t, None)
                    bk2 = sbank()
                    b2 = bk2[:].bitcast(BF16)
                    for h in range(8):
                        tp(b2[0:64, h * 128:(h + 1) * 128], qr[:, h * 64:(h + 1) * 64], ident[:], [qr, ident], [bk2])
                    cp("act", QT[0:64, :, ts], b2[0:64, :].rearrange("p (h t) -> p h t", t=128), [bk2], [QT])
                    bk = sbank()
                    for kd in range(8):
                        mm(bk[:, 0:256], hT[:, kd, ts], wkv[:, kd, 0:256], kd == 0, kd == 7, [hT, wkv], [bk])
                    cp("act", stg[:, 1, 0:128], bk[:, 128:256], [bk], [stg])
                    cp("dve", V_tok[:, t, :], bk[:, 128:256], [bk], [V_tok])
                    DMA("sp", O["o_gv"][l, ts, :], stg[:, 1, 0:128], reads=[stg], is_output=True)
                    norm_rope(bk[:, 0:128], bk, 2, knb, t, O["o_gk"][l, ts, :])
                    bk2 = sbank()
                    b2 = bk2[:].bitcast(BF16)
                    for h in range(2):
                        tp(b2[0:64, h * 128:(h + 1) * 128], qr[:, h * 64:(h + 1) * 64], ident[:], [qr, ident], [bk2])
                    cp("act", KT[0:64, 0:2, ts], b2[0:64, 0:256].rearrange("p (h t) -> p h t", t=128), [bk2], [KT])
                groups = [list(range(8))]

            if is_na:
                chk(61)
            else:
                chk(71)
            pcnt = 0
            for heads in groups:
                if is_na:
                    fill_aug(heads[0], 4)
                    for (wsrc, dst) in ((wqn, QT), (wkn, KT)):
                        for hi, h in enumerate(heads):
                            for half in range(2):
                                bk = sbank()
                                for kd in range(8):
                                    mm(bk[0:64, :], wsrc[:, kd, h * 64:(h + 1) * 64], hT[:, kd, half * 512:(half + 1) * 512],
                                       kd == 0, kd == 7, [wsrc, hT], [bk])
                                cp("act" if half == 0 else "dve", dst[0:64, hi, half * 512:(half + 1) * 512], bk[0:64, :],
                                   [bk], [dst])
                steps = [(hi, h, qc, kt) for hi, h in enumerate(heads) for qc in range(2) for kt in range(10)]
                sbk_of = {}
                acc_of = {}

                def issue_S(i):
                    hi, h, qc, kt = steps[i]
                    kvb = hi if is_na else h // 4
                    if is_na and qc == 0 and kt == 0:
                        tb = ttb[h % 2]
                        DMA("pool", tb[:], I["tt"][l, h], writes=[tb])
                    sbk = sbank()
                    mm(sbk[:], KT[0:80, kvb, kt * 128:(kt + 1) * 128], QT[0:80, hi, qc * 512:(qc + 1) * 512], True, True,
                       [KT, QT], [sbk])
                    sbk_of[i] = sbk

                for i in range(min(2, len(steps))):
                    issue_S(i)
                for i, (hi, h, qc, kt) in enumerate(steps):
                    if kt == 0:
                        pump(2)
                    if i + 2 < len(steps):
                        issue_S(i + 2)
                    kv = h if is_na else h // 4
                    ps_ = slice((h % 2) * 64, (h % 2) * 64 + 64)
                    qs = slice(qc * 512, (qc + 1) * 512)
                    if kt == 0:
                        acc_of[(h, qc)] = (hbank(), hbank())
                    ob, sb_ = acc_of[(h, qc)]
                    sbk = sbk_of.pop(i)
                    p2 = i % 3
                    if is_na and kt < 8:
                        tb = ttb[h % 2]
                        e0 = qc * 8 - 2 * kt + 14
                        stt(tmpS[:, i % 2, :], sbk[:], 0.125, tb[:, e0 * 64:(e0 + 8) * 64], ALU.mult, ALU.add,
                            [sbk, tb], [tmpS])
                        act(PT[:, p2, :], tmpS[:, i % 2, :], AF.Exp, [tmpS], [PT])
                    else:
                        act(PT[:, p2, :], sbk[:], AF.Exp, [sbk], [PT], scale=0.125)
                    if kt < 8:
                        vop = V_tok[:, kt, kv * 64:(kv + 1) * 64]
                        vb = V_tok
                    else:
                        vop = V_ctx[:, kt - 8, kv * 64:(kv + 1) * 64]
                        vb = V_ctx
                    mm(ob[ps_, :], vop, PT[:, p2, :], kt == 0, kt == 9, [vb, PT], [ob])
                    mm(sb_[ps_, :], onesb[:, 0:64], PT[:, p2, :], kt == 0, kt == 9, [onesb, PT], [sb_])
                    if kt == 9:
                        act(rs[ps_, :], sb_[ps_, :], AF.Ln, [sb_], [rs])
                        act(rs[ps_, :], rs[ps_, :], AF.Exp, [rs], [rs], scale=-1.0)
                        tt_("dve", yout[ps_, h // 2, qs], ob[ps_, :], rs[ps_, :], ALU.mult, [ob, rs], [yout])

        chk(8)
        P.barrier()
        ar["off"] = mix_off
        mT = A("mT", [128, 8, 1024], BF16)
        Wg = [[A("Wg%d_%d" % (i, j), [128, 8, 128], BF16) for j in range(3)] for i in range(2)]
        Wb = [[A("Wb%d_%d" % (i, j), [128, 4, 128], BF16) for j in range(3)] for i in range(2)]
        sg = A("sg", [128, 3, 512], F32)
        mtmp = A("mtmp", [128, 2, 512], F32)
        ys = (yA, yB, yC)
        for dc in range(8):
            pump(3)
            wg_, wb_ = Wg[dc % 2], Wb[dc % 2]
            for i in range(3):
                c0 = OFF["gate"] + i * 1024 + dc * 128
                DMA("pool", wg_[i][:], I["w_in"][l][:, c0:c0 + 128].rearrange("(k p) c -> p k c", p=128),
                    writes=[wg_[i]])
                DMA("pool", wb_[i][:], I["w_branch"][l, i][:, dc * 128:(dc + 1) * 128].rearrange("(k p) c -> p k c", p=128),
                    writes=[wb_[i]])
            for half in range(2):
                hs = slice(half * 512, (half + 1) * 512)
                for i in range(3):
                    gb = sbank()
                    for kd in range(8):
                        mm(gb[:], wg_[i][:, kd, :], hT[:, kd, hs], kd == 0, kd == 7, [wg_[i], hT], [gb])
                    act(sg[:, i, :], gb[:], AF.Sigmoid, [gb], [sg])
                    pb = sbank()
                    for e4 in range(4):
                        mm(pb[:], wb_[i][:, e4, :], ys[i][:, e4, hs], e4 == 0, e4 == 3, [wb_[i], ys[i]], [pb])
                    if i == 0:
                        tt_("dve", mtmp[:, 0, :], pb[:], sg[:, i, :], ALU.mult, [pb, sg], [mtmp])
                    else:
                        tt_("dve", mtmp[:, 1, :], pb[:], sg[:, i, :], ALU.mult, [pb, sg], [mtmp])
                        if i == 1:
                            tt_("dve", mtmp[:, 0, :], mtmp[:, 0, :], mtmp[:, 1, :], ALU.add, [mtmp], [mtmp])
                        else:
                            tt_("dve", mT[:, dc, hs], mtmp[:, 0, :], mtmp[:, 1, :], ALU.add, [mtmp], [mT])
        wo = [load_w(I["w_out"][l][:, hf * 512:(hf + 1) * 512], 512) for hf in range(2)]
        DMA("sp", lnp[:, 0, :], I["lnp"][l, 0], writes=[lnp])
        DMA("sp", lnp[:, 1, :], I["lnp"][l, 1], writes=[lnp])
        pre = A("pre", [128, D], F32)
        for t in range(NT):
            ts = slice(t * 128, (t + 1) * 128)
            for hf in range(2):
                bk = sbank()
                for dc in range(8):
                    mm(bk[:], mT[:, dc, ts], wo[hf][:, dc, :], dc == 0, dc == 7, [mT, wo[hf]], [bk])
                tt_("dve", pre[:, hf * 512:(hf + 1) * 512], bk[:], mrep[:, 2048 + hf * 512:2048 + (hf + 1) * 512], ALU.mult,
                    [bk, mrep], [pre])
            stt(pre[:], x[:, t, :], DN_ALPHA, pre[:], ALU.mult, ALU.add, [x, pre], [pre])
            mean, rstd = ln_stats(pre[:], pre)
            ts_("dve", pre[:], pre[:], mean, rstd, ALU.subtract, ALU.mult, [pre, small], [pre])
            tt_("dve", pre[:], pre[:], lnp[:, 0, :], ALU.mult, [pre, lnp], [pre])
            tt_("dve", x[:, t, :], pre[:], lnp[:, 1, :], ALU.add, [pre, lnp], [x])
        if l == 0:
            tap("x1", x, x[:], [128, NT, D])

        chk(9)
        stage_begin()
        h2tok = A("h2tok", [128, NT, D], BF16)
        e_idx = A("e_idx", [128, NT, 128], I32)
        g_all = A("g_all", [128, NT, 128], F32)
        peer_off = ar["off"]
        tmpA = A("tmpA2", [128, D], F32)
        ln_modulate(3072, 4096, tmpA, None, h2tok)
        P.barrier()
        ar["off"] = peer_off
        keysT = A("keysT", [128, 16, 128], BF16)
        qTh = A("qTh", [128, 16, 512], BF16)
        sv = A("sv", [128, 16, 16], F32)
        si = A("si", [128, 16, 16], U32)
        sif = A("sif", [128, 16, 16], F32)
        wk16 = A("wk16", [128, 16, 128], F32)
        wk8 = A("wk8", [128, 8, 256], F32)
        oh = Buf(wk16.t.rearrange("p a b -> p (a b)").rearrange("p (h i j) -> p h i j", h=8, i=16), wk16.r)
        cand = A("cand", [128, 8, 16, 16], F32)
        cvv = A("cvv", [128, 8, 16], F32)
        ci = A("ci", [128, 8, 16], U32)
        cij = A("cij", [128, 2, 8, 16], U32)
        cijf = A("cijf", [128, 2, 8, 16], F32)
        k01 = A("k01", [128, 2, 8, 16], F32)
        g8p = A("g8p", [128, 2, 8], F32)
        DMA("pool", keysT[:], I["keysT"][l].rearrange("h d k -> d h k"), writes=[keysT])
        for half in range(2):
            hs = slice(half * 512, (half + 1) * 512)
            for qb in range(4):
                wqb = load_w(I["wq"][l][:, qb * 512:(qb + 1) * 512], 512)
                for j in range(4):
                    bk = sbank()
                    for kd in range(8):
                        mm(bk[:], wqb[:, kd, j * 128:(j + 1) * 128], hT[:, kd, hs], kd == 0, kd == 7, [wqb, hT], [bk])
                    cp("act", qTh[:, qb * 4 + j, :], bk[:], [bk], [qTh])
            for tl in range(4):
                t = half * 4 + tl
                for grp in range(4):
                    bk = sbank()
                    for j in range(4):
                        hp = grp * 4 + j
                        mm(bk[:, j * 128:(j + 1) * 128], qTh[:, hp, tl * 128:(tl + 1) * 128], keysT[:, hp, :], True, True,
                           [qTh, keysT], [bk])
                    srcs = [(grp * 4 + j, bk[:, j * 128:(j + 1) * 128]) for j in range(4)]
                    for hp, src in srcs:
                        OP("dve", lambda e, src=src, hp=hp: e.max(out=sv[:, hp, 0:8], in_=src), [bk], [sv])
                    for hp, src in srcs:
                        OP("dve", lambda e, src=src, hp=hp: e.max_index(out=si[:, hp, 0:8], in_max=sv[:, hp, 0:8], in_values=src),
                           [bk, sv], [si])
                    for hp, src in srcs:
                        OP("dve", lambda e, src=src, hp=hp: e.match_replace(out=wk16[:, hp, :], in_to_replace=sv[:, hp, 0:8],
                                                                            in_values=src, imm_value=-1e30), [bk, sv], [wk16])
                    for hp, src in srcs:
                        OP("dve", lambda e, hp=hp: e.max(out=sv[:, hp, 8:16], in_=wk16[:, hp, :]), [wk16], [sv])
                    for hp, src in srcs:
                        OP("dve", lambda e, hp=hp: e.max_index(out=si[:, hp, 8:16], in_max=sv[:, hp, 8:16], in_values=wk16[:, hp, :]),
                           [wk16, sv], [si])
                cp("dve", sif[:], si[:], [si], [sif])
                sv4 = sv[:].rearrange("p (h q) k -> p h q k", q=2)
                sif4 = sif[:].rearrange("p (h q) k -> p h q k", q=2)
                tt_("dve", cand[:], sv4[:, :, 0, :].unsqueeze(3).broadcast_to([128, 8, 16, 16]),
                    sv4[:, :, 1, :].unsqueeze(2).broadcast_to([128, 8, 16, 16]), ALU.add, [sv], [cand])
                c2s = [cand[:, h, :, :].rearrange("p a b -> p (a b)") for h in range(8)]
                for h in range(8):
                    OP("dve", lambda e, c2=c2s[h], h=h: e.max(out=cvv[:, h, 0:8], in_=c2), [cand], [cvv])
                for h in range(8):
                    OP("dve", lambda e, c2=c2s[h], h=h: e.max_index(out=ci[:, h, 0:8], in_max=cvv[:, h, 0:8], in_values=c2),
                       [cand, cvv], [ci])
                for h in range(8):
                    OP("dve", lambda e, c2=c2s[h], h=h: e.match_replace(out=wk8[:, h, :], in_to_replace=cvv[:, h, 0:8], in_values=c2,
                                                                         imm_value=-1e30), [cand, cvv], [wk8])
                for h in range(8):
                    OP("dve", lambda e, h=h: e.max(out=cvv[:, h, 8:16], in_=wk8[:, h, :]), [wk8], [cvv])
                for h in range(8):
                    OP("dve", lambda e, h=h: e.max_index(out=ci[:, h, 8:16], in_max=cvv[:, h, 8:16], in_values=wk8[:, h, :]),
                       [wk8, cvv], [ci])
                OP("dve", lambda e: e.tensor_single_scalar(out=cij[:, 0, :, :], in_=ci[:], scalar=4, op=ALU.logical_shift_right),
                   [ci], [cij])
                OP("dve", lambda e: e.tensor_single_scalar(out=cij[:, 1, :, :], in_=ci[:], scalar=15, op=ALU.bitwise_and),
                   [ci], [cij])
                cp("dve", cijf[:], cij[:], [cij], [cijf])
                for q in range(2):
                    tt_("dve", oh[:], cijf[:, q, :, :].unsqueeze(3).broadcast_to([128, 8, 16, 16]),
                        iota16[:].unsqueeze(1).unsqueeze(1).broadcast_to([128, 8, 16, 16]), ALU.is_equal, [cijf, iota16], [oh])
                    tt_("dve", oh[:], oh[:], sif4[:, :, q, :].unsqueeze(2).broadcast_to([128, 8, 16, 16]), ALU.mult,
                        [oh, sif], [oh])
                    OP("dve", lambda e, q=q: e.tensor_reduce(out=k01[:, q, :, :], in_=oh[:], axis=AX.X, op=ALU.add), [oh], [k01])
                stt(k01[:, 0, :, :], k01[:, 0, :, :], 128.0, k01[:, 1, :, :], ALU.mult, ALU.add, [k01], [k01])
                cp("dve", e_idx[:, t, :].rearrange("p (h k) -> p h k", k=16), k01[:, 0, :, :], [k01], [e_idx])
                tt_("dve", cvv[:], cvv[:], cvv[:, :, 0:1].broadcast_to([128, 8, 16]), ALU.subtract, [cvv], [cvv])
                act(cvv[:], cvv[:], AF.Exp, [cvv], [cvv])
                OP("dve", lambda e: e.tensor_reduce(out=g8p[:, 0, :], in_=cvv[:], axis=AX.X, op=ALU.add), [cvv], [g8p])
                OP("dve", lambda e: e.reciprocal(out=g8p[:, 1, :], in_=g8p[:, 0, :]), [g8p], [g8p])
                tt_("dve", g_all[:, t, :].rearrange("p (h k) -> p h k", k=16), cvv[:],
                    g8p[:, 1, :].unsqueeze(2).broadcast_to([128, 8, 16]), ALU.mult, [cvv, g8p], [g_all])
        if l == 0:
            tap("e_idx", e_idx, e_idx[:], [128, NT, 128], I32)
            tap("g_all", g_all, g_all[:], [128, NT, 128])

        chk(11)
        P.barrier()
        ar["off"] = peer_off
        pump(128)
        NG = 11
        gb_ = [A("gbuf%d" % i, [128, 2 * D], BF16) for i in range(NG)]
        junks = [A("junk%d" % i, [128, D], BF16) for i in range(2)]
        dgs = [A("dg%d" % i, [128, 2, 128], BF16) for i in range(4)]
        asg = [A("asg%d" % i, [128, 2], F32) for i in range(8)]
        awg = [A("awg%d" % i, [128, 2], F32) for i in range(8)]
        pre = A("pre2", [128, D], F32)
        DMA("sp", lnp[:, 0, :], I["lnp"][l, 2], writes=[lnp])
        DMA("sp", lnp[:, 1, :], I["lnp"][l, 3], writes=[lnp])

        def finish_v(t, a0, a1):
            for hf, a_ in ((0, a0), (1, a1)):
                tt_("dve", pre[:, hf * 512:(hf + 1) * 512], a_[:], mrep[:, 5120 + hf * 512:5120 + (hf + 1) * 512], ALU.mult,
                    [a_, mrep], [pre])
            stt(pre[:], x[:, t, :], DN_ALPHA, pre[:], ALU.mult, ALU.add, [x, pre], [pre])
            mean, rstd = ln_stats(pre[:], pre)
            ts_("dve", pre[:], pre[:], mean, rstd, ALU.subtract, ALU.mult, [pre, small], [pre])
            tt_("dve", pre[:], pre[:], lnp[:, 0, :], ALU.mult, [pre, lnp], [pre])
            tt_("dve", x[:, t, :], pre[:], lnp[:, 1, :], ALU.add, [pre, lnp], [x])

        NSTEP = NT * 64
        accs = {}

        def st_A(i):
            t, q = divmod(i, 64)
            as_ = asg[i % 8]
            for j in range(2):
                s_ = q * 2 + j
                gb = gb_[(2 * i + j) % NG]
                P.dma("pool", lambda e, gb=gb, s_=s_, t=t: e.indirect_dma_start(
                    out=gb[:], out_offset=None, in_=uvb[l],
                    in_offset=bass.IndirectOffsetOnAxis(ap=e_idx[:, t, s_:s_ + 1], axis=0)),
                    [e_idx.r, uvbuf[l].r], [gb.r])
            for j in range(2):
                jk = junks[j]
                gb = gb_[(2 * i + j) % NG]
                stt(jk[:], gb[:, 0:D], 1.0, h2tok[:, t, :], ALU.mult, ALU.mult, [gb, h2tok], [jk, as_],
                    accum_out=as_[:, j:j + 1])

        def st_B(i):
            act(awg[i % 8][:], asg[i % 8][:], AF.Gelu, [asg[i % 8]], [awg[i % 8]])

        def st_CDE(i):
            t, q = divmod(i, 64)
            aw_, dg = awg[i % 8], dgs[i % 4]
            if q == 0:
                accs[t] = (hbank(), hbank())
            a0, a1 = accs[t]
            tt_("dve", aw_[:], aw_[:], g_all[:, t, q * 2:q * 2 + 2], ALU.mult, [aw_, g_all], [aw_])
            for j in range(2):
                act(dg[:, j, :], ident[:], AF.Identity, [ident, aw_], [dg], scale=aw_[:, j:j + 1])
            for j in range(2):
                s_ = q * 2 + j
                gb = gb_[(2 * i + j) % NG]
                mm(a0[:], dg[:, j, :], gb[:, D:D + 512], s_ == 0, s_ == 127, [dg, gb], [a0])
                mm(a1[:], dg[:, j, :], gb[:, D + 512:2 * D], s_ == 0, s_ == 127, [dg, gb], [a1])
            if q == 63:
                finish_v(t, a0, a1)

        for i in range(NSTEP + 2):
            if i < NSTEP:
                st_A(i)
            if 0 <= i - 1 < NSTEP:
                st_B(i - 1)
            if 0 <= i - 2 < NSTEP:
                st_CDE(i - 2)
        if l == 0:
            tap("x2", x, x[:], [128, NT, D])

    try:
        for l in range(NL):
            layer_body(l)
    except _Stop:
        pass

    P.barrier()
    for t in range(NT):
        DMA("sp", O["y"][t * 128:(t + 1) * 128, :], x[:, t, :], reads=[x], is_output=True)
    P.finish()
    return tap_out


def _na_tables():
    rows, kr = 16, 8
    r = np.arange(rows)
    start = np.clip(r - kr // 2, 0, rows - kr)
    rowvalid = np.zeros((rows, rows), bool)
    for i in range(rows):
        rowvalid[i, start[i]:start[i] + kr] = True
    qc = np.arange(64)
    qstart = np.clip(qc - 8, 0, 48)
    kc = np.arange(64)
    colvalid = (kc[None, :] >= qstart[:, None]) & (kc[None, :] < qstart[:, None] + 16)
    return rowvalid, colvalid


def host_inputs(inp, NL=DEPTH):
    f = np.float32
    rowvalid, colvalid = _na_tables()
    com = {}
    com["w_mod"] = np.ascontiguousarray(inp["w_mod"][:NL])
    com["b_mod"] = np.ascontiguousarray(inp["b_mod"][:NL].reshape(NL, 1, 6144))
    com["w_in"] = np.ascontiguousarray(inp["w_in"][:NL])
    cw = inp["conv_w"][:NL]
    com["cw"] = np.ascontiguousarray(cw.reshape(NL, 5, 6, 128).transpose(0, 3, 2, 1))
    com["cb"] = np.ascontiguousarray(inp["conv_b"][:NL].reshape(NL, 6, 128).transpose(0, 2, 1))
    com["alog"] = np.ascontiguousarray(np.broadcast_to(inp["ssd_a_log"][:NL].reshape(NL, 1, 16), (NL, 128, 16)))
    com["dtb"] = np.ascontiguousarray(np.broadcast_to(inp["ssd_dt_bias"][:NL].reshape(NL, 1, 16), (NL, 128, 16)))
    dsk = np.zeros((NL, 128, 4), f)
    for h in range(8):
        dsk[:, (h % 2) * 64:(h % 2) * 64 + 64, h // 2] = inp["ssd_d"][:NL, h][:, None]
    com["dsk"] = dsk
    com["ng"] = np.ascontiguousarray(inp["ssd_norm_g"][:NL].reshape(NL, 4, 128).transpose(0, 2, 1))
    com["qn"] = np.ascontiguousarray(np.broadcast_to(inp["gqa_q_norm"][:NL][:, None, :], (NL, 128, 64)))
    com["kn"] = np.ascontiguousarray(np.broadcast_to(inp["gqa_k_norm"][:NL][:, None, :], (NL, 128, 64)))
    lnp = np.stack([inp["ln1_g"][:NL], inp["ln1_b"][:NL], inp["ln2_g"][:NL], inp["ln2_b"][:NL]], 1)
    com["lnp"] = np.ascontiguousarray(np.broadcast_to(lnp[:, :, None, :], (NL, 4, 128, D)))
    com["w_branch"] = np.ascontiguousarray(inp["w_branch"][:NL])
    com["w_out"] = np.ascontiguousarray(inp["w_out"][:NL])
    com["wq"] = np.ascontiguousarray(inp["peer_wq"][:NL])
    com["keysT"] = np.ascontiguousarray(inp["peer_keys"][:NL].reshape(NL, 16, 128, 128).transpose(0, 1, 3, 2))
    for i in range(NL):
        com["uv%d" % i] = np.ascontiguousarray(np.concatenate([inp["peer_u"][i], inp["peer_v"][i]], axis=1))
    t = np.arange(1024)
    qaug = (t[None, :] // 64 == np.arange(16)[:, None]).astype(f)
    com["qaug"] = qaug
    rpb = inp["na_rpb"][:NL]
    tts = np.zeros((NL, 8, 2, 64, 30, 64), f)
    kc = np.arange(64)[:, None]
    qc = np.arange(64)[None, :]
    dc = np.clip(kc - qc + 15, 0, 30)
    band = colvalid.T
    for p2 in range(2):
        for e in range(30):
            dr = p2 - (e - 14)
            if -7 <= dr <= 7:
                tile = rpb[:, :, dr + 7, :][:, :, dc]
                tts[:, :, p2, :, e, :] = np.where(band[None, None], tile, f(NEG))
    tts = tts.reshape(NL, 8, 128, 30 * 64)
    ttp = np.zeros_like(tts)

    half = 32
    inv = 1.0 / (10000.0 ** (np.arange(0, half, 2, dtype=np.float32) / half))

    maps = []
    for core in range(8):
        m = dict(com)
        sample = core >= 4
        if sample:
            b = core - 4
            m["x0"] = np.ascontiguousarray(inp["x_sample"][b])
            cond = inp["c"][b]
            m["c_nakT"] = np.ascontiguousarray(inp["cache_na_k"][b, :NL].transpose(0, 2, 3, 1))
            m["c_nav"] = np.ascontiguousarray(inp["cache_na_v"][b, :NL].reshape(NL, 256, 512))
            m["c_gkT"] = np.ascontiguousarray(inp["cache_gqa_k"][b, :NL].transpose(0, 2, 3, 1))
            m["c_gv"] = np.ascontiguousarray(inp["cache_gqa_v"][b, :NL].reshape(NL, 256, 128))
            st = inp["state_ssd"][b, :NL]
            st = st.reshape(NL, 2, 2, 4, 64, 64).transpose(0, 1, 2, 5, 3, 4)
            m["h0"] = np.ascontiguousarray(st.reshape(NL, 2, 128, 256))
            m["tt"] = tts
            rm_na = np.where(rowvalid.T, 0.0, NEG * 8).astype(f)
            rm_g = np.zeros((16, 16), f)
            ctxv = 0.0
            pos_r = (t // 64).astype(f)
            pos_c = (t % 64).astype(f)
            cos = np.ones((1024, 64), f)
            sin = np.zeros((1024, 64), f)
            for a, pos in enumerate((pos_r, pos_c)):
                ang = pos[:, None] * inv[None, :]
                cos[:, a * 32:a * 32 + 16] = np.cos(ang)
                cos[:, a * 32 + 16:a * 32 + 32] = np.cos(ang)
                sin[:, a * 32:a * 32 + 16] = -np.sin(ang)
                sin[:, a * 32 + 16:a * 32 + 32] = np.sin(ang)
            keep = np.ones((128, 16), f)
            cm = np.ones((128, 1), f)
        else:
            m["x0"] = np.ascontiguousarray(inp["x_prompt"][core * 4:(core + 1) * 4].reshape(1024, D))
            cond = inp["c_ctx"]
            m["c_nakT"] = np.zeros((NL, 8, 64, 256), f)
            m["c_nav"] = np.zeros((NL, 256, 512), f)
            m["c_gkT"] = np.zeros((NL, 2, 64, 256), f)
            m["c_gv"] = np.zeros((NL, 256, 128), f)
            m["h0"] = np.zeros((NL, 2, 128, 256), f)
            m["tt"] = ttp
            seq = np.arange(16) // 4
            rm_na = np.where(seq[:, None] == seq[None, :], 0.0, NEG * 8).astype(f)
            rm_g = rm_na
            ctxv = NEG * 8
            cos = np.ones((1024, 64), f)
            sin = np.zeros((1024, 64), f)
            keep = np.ones((128, 16), f)
            keep[:, [0, 2, 4, 6]] = 0.0
            keep[:, [8 + 1, 8 + 3, 8 + 5, 8 + 7]] = 0.0
            cm = np.zeros((128, 1), f)
        m["cond_rep"] = np.ascontiguousarray(np.broadcast_to(cond.reshape(8, 128).T[:, :, None], (128, 8, 128)))
        for nm, rm in (("kaug_na", rm_na), ("kaug_g", rm_g)):
            ka = np.zeros((16, 1280), f)
            ka[:, :1024] = rm[t // 64, :].T
            ka[:, 1024:] = ctxv
            m[nm] = ka
        m["cosT"] = np.ascontiguousarray(cos.reshape(8, 128, 64).transpose(1, 0, 2))
        m["sinT"] = np.ascontiguousarray(sin.reshape(8, 128, 64).transpose(1, 0, 2))
        m["keep"] = keep
        m["cmask"] = cm
        maps.append({k: np.ascontiguousarray(v, dtype=np.float32) for k, v in m.items()})
    return maps


def assemble(results, NL=DEPTH):
    f = np.float32
    y_p = np.concatenate([results[c]["y"].reshape(4, 256, D) for c in range(4)], 0)
    y_s = np.stack([results[c]["y"] for c in range(4, 8)], 0)

    def cache(name, nh):
        parts = []
        for c in range(4):
            a = results[c][name].reshape(NL, 4, 256, nh, 64).transpose(1, 0, 2, 3, 4)
            parts.append(a)
        return np.ascontiguousarray(np.concatenate(parts, 0), dtype=f)

    nak, nav, gk, gv = cache("o_nak", 8), cache("o_nav", 8), cache("o_gk", 2), cache("o_gv", 2)
    sts = []
    for c in range(4):
        a = results[c]["o_st"].reshape(NL, 4, 2, 2, 64, 4, 64)
        a = a.transpose(1, 0, 2, 3, 5, 6, 4).reshape(4, NL, 2, 8, 64, 64)
        sts.append(a)
    st = np.ascontiguousarray(np.concatenate(sts, 0), dtype=f)
    return (y_p.astype(f), y_s.astype(f), nak, nav, gk, gv, st)


def kernel(**inputs):
    inp = {k: np.asarray(v) for k, v in inputs.items()}
    nc = bass.Bass("TRN2", target_bir_lowering=False)
    build(nc)
    maps = host_inputs(inp)
    res = run_bass_kernel_spmd(nc, maps, core_ids=list(range(8)))
    return assemble(res.results)
```

```python
import contextlib
import sys
import numpy as np
import concourse.bass as bass
import concourse.mybir as mybir
from concourse.alu_op_type import AluOpType as ALU
from concourse.bass_utils import run_bass_kernel_spmd

F32 = mybir.dt.float32
BF16 = mybir.dt.bfloat16
U32 = mybir.dt.uint32
I32 = mybir.dt.int32
AF = mybir.ActivationFunctionType
AX = mybir.AxisListType

D = 1024
NT = 8
DEPTH = 4
EPS = 1e-6
DN_ALPHA = (2 * DEPTH) ** 0.25
NEG = -30000.0
IN_COLS = 6672
OFF = dict(z=0, xs=512, B=1024, C=1152, dt=1280, naq=1296, nak=1808, nav=2320, gq=2832, gk=3344, gv=3472, gate=3600)


class Res:
    __slots__ = ("name", "last_w", "reads", "ch")

    def __init__(self, name):
        self.name = name
        self.last_w = None
        self.reads = {}
        self.ch = None


class Prog:
    ENG = ("pe", "act", "dve", "pool", "sp")

    def __init__(self, nc):
        self.nc = nc
        self.stack = contextlib.ExitStack()
        self.ops = {e: [] for e in self.ENG}
        self.cnt = {}
        self.sems = {}
        self.seen = {e: {} for e in self.ENG}
        self.pend = {e: {} for e in self.ENG}
        self.out_events = []
        self.nres = 0
        for e in self.ENG:
            self._sem("E_" + e)

    def _sem(self, key):
        if key not in self.sems:
            self.sems[key] = self.stack.enter_context(self.nc.semaphore("s" + key))
            self.cnt[key] = 0
        return self.sems[key]

    def sb(self, name, shape, dtype):
        return self.stack.enter_context(self.nc.sbuf_tensor("sb_" + name, list(shape), dtype))

    def ps(self, name, shape, dtype=F32):
        return self.stack.enter_context(self.nc.psum_tensor("ps_" + name, list(shape), dtype))

    def res(self, name=None):
        self.nres += 1
        return Res("%s_%d" % (name or "r", self.nres))

    def _deps(self, e, reads, writes):
        deps = dict(self.pend[e])
        self.pend[e] = {}

        def add(ev):
            if ev is None:
                return
            k, v = ev
            if deps.get(k, 0) < v:
                deps[k] = v

        for r in reads:
            add(r.last_w)
        for w in writes:
            add(w.last_w)
            for k, v in w.reads.items():
                add((k, v))
        waits = []
        seen = self.seen[e]
        own = "E_" + e
        for k, v in deps.items():
            if e == "pe" and k == own:
                continue
            if seen.get(k, 0) >= v:
                continue
            seen[k] = v
            waits.append((k, v))
        return waits

    def _commit(self, ev, reads, writes):
        k, v = ev
        for r in reads:
            if r.reads.get(k, 0) < v:
                r.reads[k] = v
        for w in writes:
            w.last_w = ev
            w.reads = {}

    def op(self, e, fn, reads=(), writes=()):
        waits = self._deps(e, reads, writes)
        key = "E_" + e
        self.cnt[key] += 1
        ev = (key, self.cnt[key])
        self.ops[e].append((waits, fn, key, 1, self._where()))
        self._commit(ev, reads, writes)
        return ev

    @staticmethod
    def _where():
        f = sys._getframe(2)
        out = []
        while f is not None and len(out) < 5:
            out.append(f.f_lineno)
            f = f.f_back
        return out

    def dma(self, q, fn, reads=(), writes=(), is_output=False):
        tgt = writes[0] if writes else reads[0]
        if tgt.ch is None:
            tgt.ch = "D_" + tgt.name.rsplit("_", 1)[0]
            self._sem(tgt.ch)
        key = tgt.ch
        waits = self._deps(q, reads, writes)
        self.cnt[key] += 16
        ev = (key, self.cnt[key])
        self.ops[q].append((waits, fn, key, 16, self._where()))
        self._commit(ev, reads, writes)
        if is_output:
            self.out_events.append(ev)
        return ev

    def barrier(self):
        snap = dict(self.cnt)
        for e in self.ENG:
            for k, v in snap.items():
                if v > 0 and self.pend[e].get(k, 0) < v:
                    self.pend[e][k] = v

    def finish(self):
        finals = {}
        for k, v in self.out_events:
            finals[k] = max(finals.get(k, 0), v)
        for e in self.ENG:
            if e != "sp" and self.cnt["E_" + e] > 0:
                finals["E_" + e] = self.cnt["E_" + e]
        final_waits = list(finals.items())
        nc, sems, ops = self.nc, self.sems, self.ops

        needed = {}
        for e in self.ENG:
            for waits, fn, key, inc, where in ops[e]:
                for k, v in waits:
                    needed.setdefault(k, set()).add(v)
        for k, v in final_waits:
            needed.setdefault(k, set()).add(v)
        remap = {}
        for e in self.ENG:
            key = "E_" + e
            need = needed.get(key, set())
            m = {}
            new_c = 0
            old_c = 0
            lst = []
            for waits, fn, k2, inc, where in ops[e]:
                if k2 == key:
                    old_c += 1
                    if old_c in need:
                        new_c += 1
                        m[old_c] = new_c
                        lst.append((waits, fn, k2, 1, where))
                    else:
                        lst.append((waits, fn, k2, 0, where))
                else:
                    lst.append((waits, fn, k2, inc, where))
            ops[e] = lst
            remap[key] = m

        def rv(k, v):
            return remap[k][v] if k in remap else v

        final_waits = [(k, rv(k, v)) for k, v in final_waits]

        def replay(eng, lst):
            for waits, fn, key, inc, where in lst:
                for k, v in waits:
                    eng.wait_ge(sems[k], rv(k, v))
                try:
                    inst = fn(eng)
                    if inc:
                        inst.then_inc(sems[key], inc)
                except Exception:
                    print("FAILED OP created at lines", where)
                    raise

        with nc.Block() as block:
            @block.tensor
            def _(eng):
                replay(eng, ops["pe"])

            @block.scalar
            def _(eng):
                replay(eng, ops["act"])

            @block.vector
            def _(eng):
                replay(eng, ops["dve"])

            @block.gpsimd
            def _(eng):
                replay(eng, ops["pool"])

            @block.sync
            def _(eng):
                replay(eng, ops["sp"])
                for k, v in final_waits:
                    eng.wait_ge(sems[k], v)
        self.stack.close()


class Buf:
    __slots__ = ("t", "r")

    def __init__(self, t, r):
        self.t = t
        self.r = r

    def __getitem__(self, idx):
        return self.t[idx]


class _Stop(Exception):
    pass


def build(nc, NL=DEPTH, taps=None, stop=None):
    P = Prog(nc)
    taps = taps or []
    tap_out = {}

    def din(name, shape, dt=F32):
        return nc.dram_tensor(name, list(shape), dt, kind="ExternalInput").ap()

    def dout(name, shape, dt=F32):
        return nc.dram_tensor(name, list(shape), dt, kind="ExternalOutput").ap()

    I = dict(
        x0=din("x0", [1024, D]), cond_rep=din("cond_rep", [128, 8, 128]),
        w_mod=din("w_mod", [NL, D, 6144]), b_mod=din("b_mod", [NL, 1, 6144]),
        w_in=din("w_in", [NL, D, IN_COLS]), cw=din("cw", [NL, 128, 6, 5]), cb=din("cb", [NL, 128, 6]),
        alog=din("alog", [NL, 128, 16]), dtb=din("dtb", [NL, 128, 16]),
        dsk=din("dsk", [NL, 128, 4]), ng=din("ng", [NL, 128, 4]),
        tt=din("tt", [NL, 8, 128, 30 * 64]), qn=din("qn", [NL, 128, 64]), kn=din("kn", [NL, 128, 64]),
        lnp=din("lnp", [NL, 4, 128, D]),
        w_branch=din("w_branch", [NL, 3, 512, D]), w_out=din("w_out", [NL, D, D]),
        wq=din("wq", [NL, D, 2048]), keysT=din("keysT", [NL, 16, 128, 128]),
        uv=[din("uv%d" % i, [16384, 2 * D]) for i in range(NL)],
        c_nakT=din("c_nakT", [NL, 8, 64, 256]), c_nav=din("c_nav", [NL, 256, 512]),
        c_gkT=din("c_gkT", [NL, 2, 64, 256]), c_gv=din("c_gv", [NL, 256, 128]),
        h0=din("h0", [NL, 2, 128, 256]),
        qaug=din("qaug", [16, 1024]), kaug_na=din("kaug_na", [16, 1280]), kaug_g=din("kaug_g", [16, 1280]),
        cosT=din("cosT", [128, 8, 64]), sinT=din("sinT", [128, 8, 64]),
        keep=din("keep", [128, 16]), cmask=din("cmask", [128, 1]),
    )
    O = dict(
        y=dout("y", [1024, D]), o_nak=dout("o_nak", [NL, 1024, 512]), o_nav=dout("o_nav", [NL, 1024, 512]),
        o_gk=dout("o_gk", [NL, 1024, 128]), o_gv=dout("o_gv", [NL, 1024, 128]),
        o_st=dout("o_st", [NL, 4, 2, 128, 256]),
    )

    def B(name, shape, dt):
        return Buf(P.sb(name, shape, dt), P.res(name))

    def OP(e, fn, reads=(), writes=()):
        ws = [b.r for b in writes] + [b.r for b in reads if b.r.name.startswith("bank")]
        P.op(e, fn, [b.r for b in reads], ws)

    def DMA(q, out, in_, reads=(), writes=(), is_output=False):
        P.dma(q, lambda e: e.dma_start(out=out, in_=in_), [b.r for b in reads], [b.r for b in writes], is_output)

    def mm(out, lhsT, rhs, start, stop, reads, writes):
        OP("pe", lambda e: e.matmul(out, lhsT=lhsT, rhs=rhs, start=start, stop=stop), reads, writes)

    def tp(out, in_, ident, reads, writes):
        OP("pe", lambda e: e.transpose(out=out, in_=in_, identity=ident), reads, writes)

    def act(out, in_, func, reads, writes, bias=None, scale=None, accum_out=None):
        kw = {}
        if bias is not None:
            kw["bias"] = bias
        if scale is not None:
            kw["scale"] = scale
        if accum_out is not None:
            kw["accum_out"] = accum_out
        OP("act", lambda e: e.activation(out=out, in_=in_, func=func, **kw), reads, writes)

    def tt_(eng, out, in0, in1, op, reads, writes):
        OP(eng, lambda e: e.tensor_tensor(out=out, in0=in0, in1=in1, op=op), reads, writes)

    def ts_(eng, out, in0, s1, s2, op0, op1, reads, writes):
        if op1 is None:
            OP(eng, lambda e: e.tensor_scalar(out=out, in0=in0, scalar1=s1, scalar2=None, op0=op0), reads, writes)
        else:
            OP(eng, lambda e: e.tensor_scalar(out=out, in0=in0, scalar1=s1, scalar2=s2, op0=op0, op1=op1), reads, writes)

    def stt(out, in0, scalar, in1, op0, op1, reads, writes, accum_out=None):
        if accum_out is None:
            OP("dve", lambda e: e.scalar_tensor_tensor(out=out, in0=in0, scalar=scalar, in1=in1, op0=op0, op1=op1),
               reads, writes)
        else:
            OP("dve", lambda e: e.scalar_tensor_tensor(out=out, in0=in0, scalar=scalar, in1=in1, op0=op0, op1=op1,
                                                       accum_out=accum_out), reads, writes)

    def cp(eng, out, in_, reads, writes):
        if eng == "act":
            OP("act", lambda e: e.copy(out=out, in_=in_), reads, writes)
        else:
            OP(eng, lambda e: e.tensor_copy(out=out, in_=in_), reads, writes)

    def tap(name, buf, ap, shape, dt=F32):
        if name in taps:
            d = dout("tap_" + name, shape, dt)
            tap_out[name] = d
            DMA("sp", d, ap, reads=[buf], is_output=True)

    banks = [Buf(P.ps("bank%d" % i, [128, 512], F32), P.res("bank%d" % i)) for i in range(8)]
    rot = [0, 0]

    def sbank():
        b = banks[rot[0] % 4]
        rot[0] += 1
        return b

    def hbank():
        b = banks[4 + rot[1] % 4]
        rot[1] += 1
        return b

    x = B("x", [128, NT, D], F32)
    mrep = B("mrep", [128, 6144], F32)
    hT = B("hT", [128, 8, 1024], BF16)
    lnp = B("lnp", [128, 2, D], F32)
    ident = B("ident", [128, 128], BF16)
    identf = B("identf", [128, 128], F32)
    TRIf = B("TRIf", [128, 128], F32)
    TRIb = B("TRIb", [128, 128], F32)
    maskf = B("maskf", [128, 128], F32)
    maskb = B("maskb", [128, 128], F32)
    onesb = B("onesb", [128, 128], BF16)
    onesf = B("onesf", [128, 128], F32)
    iota16 = B("iota16", [128, 16], F32)
    condS = B("condS", [128, 8, 128], BF16)
    smalls = [B("small%d" % i, [128, 16], F32) for i in range(NT)]
    keep = B("keep", [128, 16], F32)
    cmask = B("cmask", [128, 1], F32)
    cosT = B("cosT", [128, 8, 64], F32)
    sinT = B("sinT", [128, 8, 64], F32)
    NW = 3
    wbuf = [B("wbuf%d" % i, [128, 8, 512], BF16) for i in range(NW)]
    wrot = [0]

    uvb = [nc.dram_tensor("uvb%d" % i, [16384, 2 * D], BF16, kind="Internal").ap() for i in range(NL)]
    uvbuf = [Buf(None, P.res("uvb")) for i in range(NL)]
    NSTG = 3
    stgc = [B("stgc%d" % i, [128, 2 * D], BF16) for i in range(NSTG)]
    conv = {"l": 0, "i": 128}

    def start_conv(l_):
        conv["l"] = l_
        conv["i"] = 0

    def pump(n):
        while n > 0 and conv["i"] < 128:
            i_, l_ = conv["i"], conv["l"]
            st = stgc[i_ % NSTG]
            DMA("pool", st[:], I["uv"][l_][i_ * 128:(i_ + 1) * 128, :], writes=[st])
            DMA("sp", uvb[l_][i_ * 128:(i_ + 1) * 128, :], st[:], reads=[st], writes=[uvbuf[l_]])
            conv["i"] += 1
            n -= 1

    ARENA_W = 20480
    arena = P.sb("arena", [128, ARENA_W], F32)
    ar = {"off": 0, "n": 0}

    def stage_begin():
        P.barrier()
        ar["off"] = 0

    def A(name, shape, dt, parts=128):
        free = int(np.prod(shape[1:]))
        words = (free * (2 if dt == BF16 else 4) + 3) // 4
        words = (words + 7) // 8 * 8
        o = ar["off"]
        assert o + words <= ARENA_W, ("arena overflow", name, o, words)
        ar["off"] = o + words
        ar["n"] += 1
        v = arena[0:shape[0], o:o + words]
        if dt != F32:
            v = v.bitcast(dt)
        v = v[:, 0:free]
        if len(shape) == 3:
            v = v.rearrange("p (a b) -> p a b", b=shape[2])
        elif len(shape) == 4:
            v = v.rearrange("p (a b c) -> p a b c", b=shape[2], c=shape[3])
        elif len(shape) == 5:
            v = v.rearrange("p (a b c d) -> p a b c d", b=shape[2], c=shape[3], d=shape[4])
        return Buf(v, P.res(name))

    def load_w(src, ncols, npump=3):
        wb = wbuf[wrot[0] % NW]
        wrot[0] += 1
        DMA("pool", wb[:, :, 0:ncols], src.rearrange("(k p) c -> p k c", p=128), writes=[wb])
        pump(npump)
        return wb

    OP("pool", lambda e: e.memset(onesf[:], 1.0), writes=[onesf])
    OP("pool", lambda e: e.memset(onesb[:], 1.0), writes=[onesb])
    OP("pool", lambda e: e.memset(identf[:], 1.0), writes=[identf])
    OP("pool", lambda e: e.affine_select(out=identf[:], in_=identf[:], pattern=[[-1, 128]], compare_op=ALU.is_equal,
                                         fill=0.0, base=0, channel_multiplier=1), reads=[identf], writes=[identf])
    cp("pool", ident[:], identf[:], [identf], [ident])
    OP("pool", lambda e: e.affine_select(out=TRIf[:], in_=onesf[:], pattern=[[1, 128]], compare_op=ALU.is_ge,
                                         fill=0.0, base=0, channel_multiplier=-1), reads=[onesf], writes=[TRIf])
    OP("pool", lambda e: e.affine_select(out=TRIb[:], in_=onesf[:], pattern=[[-1, 128]], compare_op=ALU.is_ge,
                                         fill=0.0, base=0, channel_multiplier=1), reads=[onesf], writes=[TRIb])
    OP("pool", lambda e: e.memset(maskf[:], 0.0), writes=[maskf])
    OP("pool", lambda e: e.memset(maskb[:], 0.0), writes=[maskb])
    OP("pool", lambda e: e.affine_select(out=maskf[:], in_=maskf[:], pattern=[[1, 128]], compare_op=ALU.is_ge,
                                         fill=NEG, base=0, channel_multiplier=-1), reads=[maskf], writes=[maskf])
    OP("pool", lambda e: e.affine_select(out=maskb[:], in_=maskb[:], pattern=[[-1, 128]], compare_op=ALU.is_ge,
                                         fill=NEG, base=0, channel_multiplier=1), reads=[maskb], writes=[maskb])
    OP("pool", lambda e: e.iota(iota16[:], pattern=[[1, 16]], base=0, channel_multiplier=0,
                                allow_small_or_imprecise_dtypes=True), writes=[iota16])
    DMA("sp", keep[:], I["keep"], writes=[keep])
    DMA("sp", cmask[:], I["cmask"], writes=[cmask])
    DMA("sp", cosT[:], I["cosT"], writes=[cosT])
    DMA("sp", sinT[:], I["sinT"], writes=[sinT])
    condf = A("condf", [128, 8, 128], F32)
    DMA("sp", condf[:], I["cond_rep"], writes=[condf])
    act(condS[:], condf[:], AF.Silu, [condf], [condS])
    DMA("sp", x[:], I["x0"].rearrange("(t p) d -> p t d", p=128), writes=[x])

    def ln_stats_a(src_ap, srcbuf, t):
        small = smalls[t]
        OP("dve", lambda e: e.bn_stats(out=small[:, 0:6], in_=src_ap[:, 0:512]), [srcbuf], [small])
        OP("dve", lambda e: e.bn_stats(out=small[:, 6:12], in_=src_ap[:, 512:1024]), [srcbuf, small], [small])
        OP("dve", lambda e: e.bn_aggr(out=small[:, 12:14], in_=small[:, 0:12]), [small], [small])

    def ln_stats_b(t):
        small = smalls[t]
        act(small[:, 14:15], small[:, 13:14], AF.Ln, [small], [small], bias=EPS)
        act(small[:, 15:16], small[:, 14:15], AF.Exp, [small], [small], scale=-0.5)
        return small[:, 12:13], small[:, 15:16]

    def ln_stats(src_ap, srcbuf, t=0):
        ln_stats_a(src_ap, srcbuf, t)
        return ln_stats_b(t)

    def ln_modulate(shift_off, scale_off, tmpA, hb, h2tok=None):
        for t in range(NT):
            ln_stats_a(x[:, t, :], x, t)
        mr = [ln_stats_b(t) for t in range(NT)]
        for t in range(NT):
            mean, rstd = mr[t]
            small = smalls[t]
            ts_("dve", tmpA[:], x[:, t, :], mean, rstd, ALU.subtract, ALU.mult, [x, small], [tmpA])
            tt_("dve", tmpA[:], tmpA[:], mrep[:, scale_off:scale_off + D], ALU.mult, [tmpA, mrep], [tmpA])
            dst = hb if h2tok is None else h2tok
            dst_ap = hb[:] if h2tok is None else h2tok[:, t, :]
            tt_("dve", dst_ap, tmpA[:], mrep[:, shift_off:shift_off + D], ALU.add, [tmpA, mrep], [dst])
            bk = sbank()
            bkb = bk[:].bitcast(BF16)
            for k in range(8):
                tp(bkb[:, k * 128:(k + 1) * 128], dst_ap[:, k * 128:(k + 1) * 128], ident[:], [dst, ident], [bk])
            cp("act", hT[:, :, t * 128:(t + 1) * 128], bkb.rearrange("p (k t) -> p k t", t=128), [bk], [hT])

    def ln_affine(pre, gi, tmpbuf):
        pass

    def chk(k):
        if stop == k:
            raise _Stop()

    def layer_body(l):
        stage_begin()
        start_conv(l)
        bmod = A("bmod", [1, 6144], BF16)
        DMA("pool", bmod[:], I["b_mod"][l], writes=[bmod])
        for cc in range(12):
            wb = load_w(I["w_mod"][l][:, cc * 512:(cc + 1) * 512], 512, npump=0)
            bk = sbank()
            for k in range(8):
                mm(bk[:], condS[:, k, :], wb[:, k, :], k == 0, False, [condS, wb], [bk])
            mm(bk[:], onesb[0:1, :], bmod[0:1, cc * 512:(cc + 1) * 512], False, True, [onesb, bmod], [bk])
            if cc in (2, 3, 8, 9):
                act(mrep[:, cc * 512:(cc + 1) * 512], bk[:], AF.Identity, [bk], [mrep], bias=1.0)
            else:
                cp("act", mrep[:, cc * 512:(cc + 1) * 512], bk[:], [bk], [mrep])
        if l == 0:
            tap("mrep", mrep, mrep[:], [128, 6144])
        chk(1)

        tmpA = A("tmpA", [128, D], F32)
        hb = A("hb", [128, D], BF16)
        ln_modulate(0, 1024, tmpA, hb)
        if l == 0:
            tap("hT", hT, hT[:], [128, 8, 1024], BF16)
        chk(2)

        stage_begin()
        yA = A("yA", [128, 4, 1024], BF16)
        yB = A("yB", [128, 4, 1024], BF16)
        yC = A("yC", [128, 4, 1024], BF16)
        mix_off = ar["off"]
        xc = A("xc", [128, 6, 1024], BF16)
        x_tok = A("x_tok", [128, 8, 512], BF16)
        B_tok = A("B_tok", [128, 8, 128], BF16)
        Sin = A("Sin", [128, 8, 2, 256], BF16)
        a_all = A("a_all", [128, 8, 16], F32)
        lndt = A("lndt", [128, 8, 16], F32)
        csT = A("csT", [128, 8, 32], F32)
        w_all = A("w_all", [128, 8, 16], F32)
        decG = A("decG", [128, 8, 2, 4], F32)
        Arep = A("Arep", [128, 16], F32)
        dtbr = A("dtbr", [128, 16], F32)
        cwb = A("cwb", [128, 6, 5], F32)
        cbb = A("cbb", [128, 6], F32)
        dskb = A("dskb", [128, 4], F32)
        ngb = A("ngb", [128, 4], F32)
        S = A("S", [128, 2, 256], F32)
        sc16 = A("sc16", [128, 4, 16], F32)
        ssd_off = ar["off"]
        xp = A("xp", [128, 6, 4, 260], BF16)
        acc = A("acc", [128, 4, 256], F32)

        DMA("sp", Arep[:], I["alog"][l], writes=[Arep])
        DMA("sp", dtbr[:], I["dtb"][l], writes=[dtbr])
        DMA("sp", cwb[:], I["cw"][l], writes=[cwb])
        DMA("sp", cbb[:], I["cb"][l], writes=[cbb])
        DMA("sp", dskb[:], I["dsk"][l], writes=[dskb])
        DMA("sp", ngb[:], I["ng"][l], writes=[ngb])
        DMA("sp", S[:], I["h0"][l].rearrange("d p f -> p d f"), writes=[S])
        act(Arep[:], Arep[:], AF.Exp, [Arep], [Arep])
        ts_("dve", Arep[:], Arep[:], -1.0, None, ALU.mult, None, [Arep], [Arep])

        OP("pool", lambda e: e.memset(xp[:], 0.0), writes=[xp])
        w1 = load_w(I["w_in"][l][:, OFF["xs"]:OFF["xs"] + 512], 512)
        w2 = load_w(I["w_in"][l][:, OFF["B"]:OFF["B"] + 272], 272)
        for k in range(6):
            wsrc, c0 = (w1, k * 128) if k < 4 else (w2, (k - 4) * 128)
            for half in range(2):
                bk = sbank()
                for kd in range(8):
                    mm(bk[:], wsrc[:, kd, c0:c0 + 128], hT[:, kd, half * 512:(half + 1) * 512], kd == 0, kd == 7,
                       [wsrc, hT], [bk])
                cp("act", xp[:, k, 2 * half:2 * half + 2, 2:258], bk[:].rearrange("p (s t) -> p s t", t=256), [bk], [xp])
        chk(21)
        for t in range(NT):
            bk = sbank()
            for kd in range(8):
                mm(bk[:, 0:16], hT[:, kd, t * 128:(t + 1) * 128], w2[:, kd, 256:272], kd == 0, kd == 7, [hT, w2], [bk])
            tt_("dve", sc16[:, 0, :], bk[:, 0:16], dtbr[:], ALU.add, [bk, dtbr], [sc16])
            act(sc16[:, 1, :], sc16[:, 0, :], AF.Exp, [sc16], [sc16])
            act(sc16[:, 2, :], sc16[:, 1, :], AF.Ln, [sc16], [sc16], bias=1.0)
            act(lndt[:, t, :], sc16[:, 2, :], AF.Ln, [sc16], [lndt])
            tt_("dve", a_all[:, t, :], sc16[:, 2, :], Arep[:], ALU.mult, [sc16, Arep], [a_all])
        chk(22)
        ts_("dve", xp[:, :, 1:4, 0:2], xp[:, :, 0:3, 256:258], cmask[:, 0:1], None, ALU.mult, None, [xp, cmask], [xp])
        ts_("dve", xp[:, :, 0:3, 258:260], xp[:, :, 1:4, 2:4], cmask[:, 0:1], None, ALU.mult, None, [xp, cmask], [xp])
        for k in range(6):
            ts_("dve", acc[:], xp[:, k, :, 0:256], cwb[:, k, 0:1], None, ALU.mult, None, [xp, cwb], [acc])
            for j in range(1, 5):
                stt(acc[:], xp[:, k, :, j:j + 256], cwb[:, k, j:j + 1], acc[:], ALU.mult, ALU.add, [xp, cwb, acc], [acc])
            act(xc[:, k, :].rearrange("p (s t) -> p s t", t=256), acc[:], AF.Silu, [acc, cbb], [xc], bias=cbb[:, k:k + 1])
        chk(23)
        for t in range(NT):
            bk = sbank()
            bkb = bk[:].bitcast(BF16)
            for k in range(5):
                tp(bkb[:, k * 128:(k + 1) * 128], xc[:, k, t * 128:(t + 1) * 128], ident[:], [xc, ident], [bk])
            cp("act", x_tok[:, t, :], bkb[:, 0:512], [bk], [x_tok])
            chk(24)
            cp("act", B_tok[:, t, :], bkb[:, 512:640], [bk], [B_tok])

        chk(3)
        P.barrier()
        ar["off"] = ssd_off
        for c in range(8):
            bk = sbank()
            mm(bk[:, 0:8], TRIf[:], a_all[:, c, 0:8], True, True, [TRIf, a_all], [bk])
            mm(bk[:, 8:16], TRIb[:], a_all[:, c, 8:16], True, True, [TRIb, a_all], [bk])
            mm(bk[:, 16:32], onesf[:], a_all[:, c, :], True, True, [onesf, a_all], [bk])
            cp("act", csT[:, c, :], bk[:, 0:32], [bk], [csT])
            act(sc16[:, 0, :], bk[:, 16:32], AF.Exp, [bk], [sc16])
            d4 = sc16[:, 0, :].rearrange("p (d g h) -> p d g h", d=2, g=2)
            cp("dve", decG[0:64, c, :, :], d4[0:64, :, 0, :], [sc16], [decG])
            cp("dve", decG[64:128, c, :, :], d4[64:128, :, 1, :], [sc16], [decG])
            tt_("dve", sc16[:, 1, :], csT[:, c, 16:32], csT[:, c, 0:16], ALU.subtract, [csT], [sc16])
            tt_("dve", sc16[:, 1, :], sc16[:, 1, :], lndt[:, c, :], ALU.add, [sc16, lndt], [sc16])
            act(w_all[:, c, :], sc16[:, 1, :], AF.Exp, [sc16], [w_all])

        xw = A("xw", [128, 512], BF16)
        ssts = [A("sst%d" % i, [128, 256], F32) for i in range(2)]
        sst_n = 0
        for d_, order in ((0, range(8)), (1, range(7, -1, -1))):
            first = True
            for c in order:
                if not first:
                    ts_("dve", S[:, d_, :], S[:, d_, :], keep[:, d_ * 8 + c:d_ * 8 + c + 1], None, ALU.mult, None,
                        [S, keep], [S])
                first = False
                cp("act", Sin[:, c, d_, :], S[:, d_, :], [S], [Sin])
                tt_("dve", xw[:].rearrange("p (h q) -> p h q", q=64),
                    x_tok[:, c, :].rearrange("p (h q) -> p h q", q=64),
                    w_all[:, c, d_ * 8:d_ * 8 + 8].unsqueeze(2).broadcast_to([128, 8, 64]), ALU.mult,
                    [x_tok, w_all], [xw])
                bk = sbank()
                for g in range(2):
                    mm(bk[g * 64:(g + 1) * 64, 0:256], B_tok[:, c, g * 64:(g + 1) * 64], xw[:, g * 256:(g + 1) * 256],
                       True, True, [B_tok, xw], [bk])
                tt_("dve", S[:, d_, :].rearrange("p (h q) -> p h q", q=64),
                    S[:, d_, :].rearrange("p (h q) -> p h q", q=64),
                    decG[:, c, d_, :].unsqueeze(2).broadcast_to([128, 4, 64]), ALU.mult, [S, decG], [S])
                tt_("dve", S[:, d_, :], S[:, d_, :], bk[:, 0:256], ALU.add, [S, bk], [S])
                if (d_ == 0 and c % 2 == 1) or (d_ == 1 and c % 2 == 0):
                    sst = ssts[sst_n % 2]
                    sst_n += 1
                    cp("act", sst[:], S[:, d_, :], [S], [sst])
                    DMA("sp", O["o_st"][l, c // 2, d_], sst[:], reads=[sst], is_output=True)

        chk(4)
        arep = A("arep", [128, 16, 128], F32)
        cbT = A("cbT", [128, 2, 128], F32)
        Lw = A("Lw", [128, 4, 128], F32)
        MT = A("MT", [128, 2, 128], BF16)
        Ew = A("Ew", [128, 2, 128], F32)
        Cp = A("Cp", [128, 4, 128], BF16)
        yg32 = A("yg32", [128, 512], F32)
        ysq = A("ysq", [128, 512], F32)
        ygf = yA
        mcnt = 0
        for c in range(8):
            pump(4)
            cs_ = slice(c * 128, (c + 1) * 128)
            cp("dve", arep[:], a_all[:, c, :].unsqueeze(2).broadcast_to([128, 16, 128]), [a_all], [arep])
            reps = []
            for q in range(4):
                bk = hbank()
                for jj in range(4):
                    j = q * 4 + jj
                    mm(bk[:, jj * 128:(jj + 1) * 128], arep[:, j, :], (TRIf if j < 8 else TRIb)[:], True, True,
                       [arep, TRIf, TRIb], [bk])
                reps.append(bk)
            for g in range(2):
                bk = sbank()
                mm(bk[:, 0:128], xc[g * 64:(g + 1) * 64, 4, cs_], xc[g * 64:(g + 1) * 64, 5, cs_], True, True, [xc], [bk])
                cp("act", cbT[:, g, :], bk[:, 0:128], [bk], [cbT])
            for pr in range(4):
                ybk = sbank()
                for hh in range(2):
                    h = pr * 2 + hh
                    g, h4 = h // 4, h % 4
                    gs = slice(g * 64, (g + 1) * 64)
                    rf = reps[h // 4][:, (h % 4) * 128:(h % 4 + 1) * 128]
                    rb = reps[2 + h // 4][:, (h % 4) * 128:(h % 4 + 1) * 128]
                    m2 = mcnt % 2
                    mcnt += 1
                    stt(Lw[:, 0, :], rf, csT[:, c, h:h + 1], maskf[:], ALU.subtract, ALU.add,
                        [reps[h // 4], csT, maskf], [Lw])
                    act(Lw[:, 1, :], Lw[:, 0, :], AF.Exp, [Lw, lndt], [Lw], bias=lndt[:, c, h:h + 1])
                    stt(Lw[:, 2, :], rb, csT[:, c, 8 + h:9 + h], maskb[:], ALU.subtract, ALU.add,
                        [reps[2 + h // 4], csT, maskb], [Lw])
                    act(Lw[:, 3, :], Lw[:, 2, :], AF.Exp, [Lw, lndt], [Lw], bias=lndt[:, c, 8 + h:9 + h])
                    tt_("dve", Lw[:, 1, :], Lw[:, 1, :], Lw[:, 3, :], ALU.add, [Lw], [Lw])
                    tt_("dve", MT[:, m2, :], Lw[:, 1, :], cbT[:, g, :], ALU.mult, [Lw, cbT], [MT])
                    act(Ew[gs, 0, :], rf[gs, :], AF.Exp, [reps[h // 4]], [Ew])
                    act(Ew[gs, 1, :], rb[gs, :], AF.Exp, [reps[2 + h // 4]], [Ew])
                    tt_("dve", Cp[gs, m2 * 2, :], xc[gs, 5, cs_], Ew[gs, 0, :], ALU.mult, [xc, Ew], [Cp])
                    tt_("dve", Cp[gs, m2 * 2 + 1, :], xc[gs, 5, cs_], Ew[gs, 1, :], ALU.mult, [xc, Ew], [Cp])
                    yo = ybk[hh * 64:(hh + 1) * 64, 0:128]
                    mm(yo, x_tok[:, c, h * 64:(h + 1) * 64], MT[:, m2, :], True, False, [x_tok, MT], [ybk])
                    mm(yo, Sin[gs, c, 0, h4 * 64:(h4 + 1) * 64], Cp[gs, m2 * 2, :], False, False, [Sin, Cp], [ybk])
                    mm(yo, Sin[gs, c, 1, h4 * 64:(h4 + 1) * 64], Cp[gs, m2 * 2 + 1, :], False, True, [Sin, Cp], [ybk])
                stt(ygf[:, pr, cs_], xc[:, pr, cs_], dskb[:, pr:pr + 1], ybk[:, 0:128], ALU.mult, ALU.add,
                    [xc, dskb, ybk], [ygf])
        chk(5)
        wz = load_w(I["w_in"][l][:, OFF["z"]:OFF["z"] + 512], 512)
        for half in range(2):
            hs = slice(half * 512, (half + 1) * 512)
            sbk = hbank()
            for pr in range(4):
                bk = sbank()
                for kd in range(8):
                    mm(bk[:], wz[:, kd, pr * 128:(pr + 1) * 128], hT[:, kd, hs], kd == 0, kd == 7, [wz, hT], [bk])
                act(yg32[:], bk[:], AF.Silu, [bk], [yg32])
                tt_("dve", ygf[:, pr, hs], ygf[:, pr, hs], yg32[:], ALU.mult, [ygf, yg32], [ygf])
                tt_("dve", ysq[:], ygf[:, pr, hs], ygf[:, pr, hs], ALU.mult, [ygf], [ysq])
                mm(sbk[:], onesf[:], ysq[:], pr == 0, pr == 3, [onesf, ysq], [sbk])
            act(yg32[:], sbk[:], AF.Ln, [sbk], [yg32], bias=EPS, scale=1.0 / 512)
            act(yg32[:], yg32[:], AF.Exp, [yg32], [yg32], scale=-0.5)
            for pr in range(4):
                stt(yA[:, pr, hs], ygf[:, pr, hs], ngb[:, pr:pr + 1], yg32[:], ALU.mult, ALU.mult, [ygf, ngb, yg32], [ygf])

        chk(6)
        for br in range(2):
            P.barrier()
            ar["off"] = mix_off
            is_na = br == 0
            if br == 1:
                chk(7)
            nq = 4 if is_na else 8
            nkb = 4 if is_na else 2
            QT = A("QT", [80, nq, 1024], BF16)
            KT = A("KT", [80, nkb, 1280], BF16)
            vw = 512 if is_na else 128
            V_tok = A("V_tok", [128, 8, vw], BF16)
            V_ctx = A("V_ctx", [128, 2, vw], BF16)
            stgs = [A("stg%d" % i, [128, 512], F32) for i in range(4)]
            stg_n = [0]

            def next_stg():
                b_ = stgs[stg_n[0] % 4]
                stg_n[0] += 1
                return b_

            PTs = [A("PT%d" % i, [128, 512], BF16) for i in range(3)]
            tmpSs = [A("tmpS%d" % i, [128, 512], F32) for i in range(2)]
            rs = A("rs", [128, 512], F32)
            yout = yB if is_na else yC
            ka = I["kaug_na"] if is_na else I["kaug_g"]
            ck = I["c_nakT"] if is_na else I["c_gkT"]
            cv = I["c_nav"] if is_na else I["c_gv"]
            DMA("pool", V_ctx[:], cv[l].rearrange("(j p) c -> p j c", p=128), writes=[V_ctx])

            def fill_aug(h0_, n_):
                DMA("pool", QT[64:80, :, :], I["qaug"].unsqueeze(1).broadcast_to([16, nq, 1024]), writes=[QT])
                DMA("pool", KT[64:80, :, :], ka.unsqueeze(1).broadcast_to([16, nkb, 1280]), writes=[KT])
                DMA("pool", KT[0:64, :, 1024:1280], ck[l, h0_:h0_ + n_].rearrange("h d k -> d h k"), writes=[KT])

            if is_na:
                ttb = [A("ttb%d" % i, [128, 30 * 64], BF16) for i in range(2)]
                wqn = load_w(I["w_in"][l][:, OFF["naq"]:OFF["naq"] + 512], 512)
                wkn = load_w(I["w_in"][l][:, OFF["nak"]:OFF["nak"] + 512], 512)
                wvn = load_w(I["w_in"][l][:, OFF["nav"]:OFF["nav"] + 512], 512)
                for t in range(NT):
                    for (wsrc, dram, keepbf) in ((wkn, O["o_nak"], False), (wvn, O["o_nav"], True)):
                        bk = sbank()
                        for kd in range(8):
                            mm(bk[:], hT[:, kd, t * 128:(t + 1) * 128], wsrc[:, kd, :], kd == 0, kd == 7, [hT, wsrc], [bk])
                        stg = next_stg()
                        cp("act", stg[:], bk[:], [bk], [stg])
                        if keepbf:
                            cp("dve", V_tok[:, t, :], bk[:], [bk], [V_tok])
                        DMA("sp", dram[l, t * 128:(t + 1) * 128, :], stg[:], reads=[stg], is_output=True)
                groups = [list(range(0, 4)), list(range(4, 8))]
            else:
                fill_aug(0, 2)
                wqg = load_w(I["w_in"][l][:, OFF["gq"]:OFF["gq"] + 512], 512)
                wkv = load_w(I["w_in"][l][:, OFF["gk"]:OFF["gk"] + 256], 256)
                gq = A("gq", [128, 512], F32)
                gsq = A("gsq", [128, 512], F32)
                gr = A("gr", [128, 512], F32)
                qr = A("qr", [128, 512], BF16)
                g8 = A("g8", [128, 4, 8], F32)
                qnb = A("qnb", [128, 64], F32)
                knb = A("knb", [128, 64], F32)
                DMA("sp", qnb[:], I["qn"][l], writes=[qnb])
                DMA("sp", knb[:], I["kn"][l], writes=[knb])

                def norm_rope(src_ap, src_bk, nh, gain, t, cache_dram):
                    n = nh * 64
                    v3 = lambda ap: ap.rearrange("p (h q) -> p h q", q=64)
                    cp("act", gq[:, 0:n], src_ap, [src_bk], [gq])
                    act(gsq[:, 0:n], src_ap, AF.Square, [src_bk], [gsq])
                    OP("dve", lambda e: e.tensor_reduce(out=g8[:, 0, 0:nh], in_=v3(gsq[:, 0:n]), axis=AX.X, op=ALU.add),
                       [gsq], [g8])
                    act(g8[:, 1, 0:nh], g8[:, 0, 0:nh], AF.Ln, [g8], [g8], bias=EPS, scale=1.0 / 64)
                    act(g8[:, 2, 0:nh], g8[:, 1, 0:nh], AF.Exp, [g8], [g8], scale=-0.5)
                    tt_("dve", v3(gq[:, 0:n]), v3(gq[:, 0:n]), g8[:, 2, 0:nh].unsqueeze(2).broadcast_to([128, nh, 64]),
                        ALU.mult, [gq, g8], [gq])
                    tt_("dve", v3(gq[:, 0:n]), v3(gq[:, 0:n]), gain[:].unsqueeze(1).broadcast_to([128, nh, 64]),
                        ALU.mult, [gq, gain], [gq])
                    if cache_dram is not None:
                        stg = next_stg()
                        cp("act", stg[:, 0:n], gq[:, 0:n], [gq], [stg])
                        DMA("sp", cache_dram, stg[:, 0:n], reads=[stg], is_output=True)
                    v5 = lambda ap: ap.rearrange("p (h a b c) -> p h a b c", a=2, b=2, c=16)
                    sn = sinT[:, t, :].rearrange("p (a b c) -> p a b c", a=2, b=2)
                    for b_ in range(2):
                        tt_("dve", v5(gr[:, 0:n])[:, :, :, b_, :], v5(gq[:, 0:n])[:, :, :, 1 - b_, :],
                            sn[:, :, b_, :].unsqueeze(1).broadcast_to([128, nh, 2, 16]), ALU.mult, [gq, sinT], [gr])
                    tt_("dve", v3(gsq[:, 0:n]), v3(gq[:, 0:n]), cosT[:, t, :].unsqueeze(1).broadcast_to([128, nh, 64]),
                        ALU.mult, [gq, cosT], [gsq])
                    tt_("dve", qr[:, 0:n], gsq[:, 0:n], gr[:, 0:n], ALU.add, [gsq, gr], [qr])

                for t in range(NT):
                    ts = slice(t * 128, (t + 1) * 128)
                    bk = sbank()
                    for kd in range(8):
                        mm(bk[:], hT[:, kd, ts], wqg[:, kd, :], kd == 0, kd == 7, [hT, wqg], [bk])
                    norm_rope(bk[:, 0:512], bk, 8, qnb, t, None)
                    bk2 = sbank()
                    b2 = bk2[:].bitcast(BF16)
                    for h in range(8):
                        tp(b2[0:64, h * 128:(h + 1) * 128], qr[:, h * 64:(h + 1) * 64], ident[:], [qr, ident], [bk2])
                    cp("act", QT[0:64, :, ts], b2[0:64, :].rearrange("p (h t) -> p h t", t=128), [bk2], [QT])
                    bk = sbank()
                    for kd in range(8):
                        mm(bk[:, 0:256], hT[:, kd, ts], wkv[:, kd, 0:256], kd == 0, kd == 7, [hT, wkv], [bk])
                    stg = next_stg()
                    cp("act", stg[:, 0:128], bk[:, 128:256], [bk], [stg])
                    cp("dve", V_tok[:, t, :], bk[:, 128:256], [bk], [V_tok])
                    DMA("sp", O["o_gv"][l, ts, :], stg[:, 0:128], reads=[stg], is_output=True)
                    norm_rope(bk[:, 0:128], bk, 2, knb, t, O["o_gk"][l, ts, :])
                    bk2 = sbank()
                    b2 = bk2[:].bitcast(BF16)
                    for h in range(2):
                        tp(b2[0:64, h * 128:(h + 1) * 128], qr[:, h * 64:(h + 1) * 64], ident[:], [qr, ident], [bk2])
                    cp("act", KT[0:64, 0:2, ts], b2[0:64, 0:256].rearrange("p (h t) -> p h t", t=128), [bk2], [KT])
                groups = [list(range(8))]

            if is_na:
                chk(61)
            else:
                chk(71)
            pcnt = 0
            for heads in groups:
                if is_na:
                    fill_aug(heads[0], 4)
                    for (wsrc, dst) in ((wqn, QT), (wkn, KT)):
                        for hi, h in enumerate(heads):
                            for half in range(2):
                                bk = sbank()
                                for kd in range(8):
                                    mm(bk[0:64, :], wsrc[:, kd, h * 64:(h + 1) * 64], hT[:, kd, half * 512:(half + 1) * 512],
                                       kd == 0, kd == 7, [wsrc, hT], [bk])
                                cp("act" if half == 0 else "dve", dst[0:64, hi, half * 512:(half + 1) * 512], bk[0:64, :],
                                   [bk], [dst])
                steps = [(hi, h, qc, kt) for hi, h in enumerate(heads) for qc in range(2) for kt in range(10)]
                sbk_of = {}
                acc_of = {}

                def issue_S(i):
                    hi, h, qc, kt = steps[i]
                    kvb = hi if is_na else h // 4
                    if is_na and qc == 0 and kt == 0:
                        tb = ttb[h % 2]
                        DMA("pool", tb[:], I["tt"][l, h], writes=[tb])
                    sbk = sbank()
                    mm(sbk[:], KT[0:80, kvb, kt * 128:(kt + 1) * 128], QT[0:80, hi, qc * 512:(qc + 1) * 512], True, True,
                       [KT, QT], [sbk])
                    sbk_of[i] = sbk

                for i in range(min(2, len(steps))):
                    issue_S(i)
                for i, (hi, h, qc, kt) in enumerate(steps):
                    if kt == 0:
                        pump(2)
                    if i + 2 < len(steps):
                        issue_S(i + 2)
                    kv = h if is_na else h // 4
                    ps_ = slice((h % 2) * 64, (h % 2) * 64 + 64)
                    qs = slice(qc * 512, (qc + 1) * 512)
                    if kt == 0:
                        acc_of[(h, qc)] = (hbank(), hbank())
                    ob, sb_ = acc_of[(h, qc)]
                    sbk = sbk_of.pop(i)
                    PT = PTs[i % 3]
                    tmpS = tmpSs[i % 2]
                    if is_na and kt < 8:
                        tb = ttb[h % 2]
                        e0 = qc * 8 - 2 * kt + 14
                        stt(tmpS[:], sbk[:], 0.125, tb[:, e0 * 64:(e0 + 8) * 64], ALU.mult, ALU.add,
                            [sbk, tb], [tmpS])
                        act(PT[:], tmpS[:], AF.Exp, [tmpS], [PT])
                    else:
                        act(PT[:], sbk[:], AF.Exp, [sbk], [PT], scale=0.125)
                    if kt < 8:
                        vop = V_tok[:, kt, kv * 64:(kv + 1) * 64]
                        vb = V_tok
                    else:
                        vop = V_ctx[:, kt - 8, kv * 64:(kv + 1) * 64]
                        vb = V_ctx
                    mm(ob[ps_, :], vop, PT[:], kt == 0, kt == 9, [vb, PT], [ob])
                    mm(sb_[ps_, :], onesb[:, 0:64], PT[:], kt == 0, kt == 9, [onesb, PT], [sb_])
                    if kt == 9:
                        act(rs[ps_, :], sb_[ps_, :], AF.Ln, [sb_], [rs])
                        act(rs[ps_, :], rs[ps_, :], AF.Exp, [rs], [rs], scale=-1.0)
                        tt_("dve", yout[ps_, h // 2, qs], ob[ps_, :], rs[ps_, :], ALU.mult, [ob, rs], [yout])

        chk(8)
        P.barrier()
        ar["off"] = mix_off
        mT = A("mT", [128, 8, 1024], BF16)
        Wg = [[A("Wg%d_%d" % (i, j), [128, 8, 128], BF16) for j in range(3)] for i in range(2)]
        Wb = [[A("Wb%d_%d" % (i, j), [128, 4, 128], BF16) for j in range(3)] for i in range(2)]
        sgs = [A("sg%d" % i, [128, 512], F32) for i in range(3)]
        mtmp = A("mtmp", [128, 2, 512], F32)
        ys = (yA, yB, yC)
        for dc in range(8):
            pump(3)
            wg_, wb_ = Wg[dc % 2], Wb[dc % 2]
            for i in range(3):
                c0 = OFF["gate"] + i * 1024 + dc * 128
                DMA("pool", wg_[i][:], I["w_in"][l][:, c0:c0 + 128].rearrange("(k p) c -> p k c", p=128),
                    writes=[wg_[i]])
                DMA("pool", wb_[i][:], I["w_branch"][l, i][:, dc * 128:(dc + 1) * 128].rearrange("(k p) c -> p k c", p=128),
                    writes=[wb_[i]])
            for half in range(2):
                hs = slice(half * 512, (half + 1) * 512)
                for i in range(3):
                    gb = sbank()
                    for kd in range(8):
                        mm(gb[:], wg_[i][:, kd, :], hT[:, kd, hs], kd == 0, kd == 7, [wg_[i], hT], [gb])
                    sg = sgs[i]
                    act(sg[:], gb[:], AF.Sigmoid, [gb], [sg])
                    pb = sbank()
                    for e4 in range(4):
                        mm(pb[:], wb_[i][:, e4, :], ys[i][:, e4, hs], e4 == 0, e4 == 3, [wb_[i], ys[i]], [pb])
                    if i == 0:
                        tt_("dve", mtmp[:, 0, :], pb[:], sg[:], ALU.mult, [pb, sg], [mtmp])
                    else:
                        tt_("dve", mtmp[:, 1, :], pb[:], sg[:], ALU.mult, [pb, sg], [mtmp])
                        if i == 1:
                            tt_("dve", mtmp[:, 0, :], mtmp[:, 0, :], mtmp[:, 1, :], ALU.add, [mtmp], [mtmp])
                        else:
                            tt_("dve", mT[:, dc, hs], mtmp[:, 0, :], mtmp[:, 1, :], ALU.add, [mtmp], [mT])
        wo = [load_w(I["w_out"][l][:, hf * 512:(hf + 1) * 512], 512) for hf in range(2)]
        DMA("sp", lnp[:, 0, :], I["lnp"][l, 0], writes=[lnp])
        DMA("sp", lnp[:, 1, :], I["lnp"][l, 1], writes=[lnp])
        pre = A("pre", [128, D], F32)
        for t in range(NT):
            ts = slice(t * 128, (t + 1) * 128)
            for hf in range(2):
                bk = sbank()
                for dc in range(8):
                    mm(bk[:], mT[:, dc, ts], wo[hf][:, dc, :], dc == 0, dc == 7, [mT, wo[hf]], [bk])
                tt_("dve", pre[:, hf * 512:(hf + 1) * 512], bk[:], mrep[:, 2048 + hf * 512:2048 + (hf + 1) * 512], ALU.mult,
                    [bk, mrep], [pre])
            stt(pre[:], x[:, t, :], DN_ALPHA, pre[:], ALU.mult, ALU.add, [x, pre], [pre])
            mean, rstd = ln_stats(pre[:], pre, t)
            ts_("dve", pre[:], pre[:], mean, rstd, ALU.subtract, ALU.mult, [pre, smalls[t]], [pre])
            tt_("dve", pre[:], pre[:], lnp[:, 0, :], ALU.mult, [pre, lnp], [pre])
            tt_("dve", x[:, t, :], pre[:], lnp[:, 1, :], ALU.add, [pre, lnp], [x])
        if l == 0:
            tap("x1", x, x[:], [128, NT, D])

        chk(9)
        stage_begin()
        h2tok = A("h2tok", [128, NT, D], BF16)
        e_idx = A("e_idx", [128, NT, 128], I32)
        g_all = A("g_all", [128, NT, 128], F32)
        peer_off = ar["off"]
        tmpA = A("tmpA2", [128, D], F32)
        ln_modulate(3072, 4096, tmpA, None, h2tok)
        P.barrier()
        ar["off"] = peer_off
        keysT = A("keysT", [128, 16, 128], BF16)
        qTh = A("qTh", [128, 16, 512], BF16)
        sv = A("sv", [128, 16, 16], F32)
        si = A("si", [128, 16, 16], U32)
        sif = A("sif", [128, 16, 16], F32)
        wk16 = A("wk16", [128, 16, 128], F32)
        wk8 = A("wk8", [128, 8, 256], F32)
        oh = Buf(wk16.t.rearrange("p a b -> p (a b)").rearrange("p (h i j) -> p h i j", h=8, i=16), wk16.r)
        cand = A("cand", [128, 8, 16, 16], F32)
        cvv = A("cvv", [128, 8, 16], F32)
        ci = A("ci", [128, 8, 16], U32)
        cij = A("cij", [128, 2, 8, 16], U32)
        cijf = A("cijf", [128, 2, 8, 16], F32)
        k01 = A("k01", [128, 2, 8, 16], F32)
        g8p = A("g8p", [128, 2, 8], F32)
        DMA("pool", keysT[:], I["keysT"][l].rearrange("h d k -> d h k"), writes=[keysT])
        for half in range(2):
            hs = slice(half * 512, (half + 1) * 512)
            for qb in range(4):
                wqb = load_w(I["wq"][l][:, qb * 512:(qb + 1) * 512], 512)
                for j in range(4):
                    bk = sbank()
                    for kd in range(8):
                        mm(bk[:], wqb[:, kd, j * 128:(j + 1) * 128], hT[:, kd, hs], kd == 0, kd == 7, [wqb, hT], [bk])
                    cp("act", qTh[:, qb * 4 + j, :], bk[:], [bk], [qTh])
            for tl in range(4):
                t = half * 4 + tl
                for grp in range(4):
                    bk = sbank()
                    for j in range(4):
                        hp = grp * 4 + j
                        mm(bk[:, j * 128:(j + 1) * 128], qTh[:, hp, tl * 128:(tl + 1) * 128], keysT[:, hp, :], True, True,
                           [qTh, keysT], [bk])
                    srcs = [(grp * 4 + j, bk[:, j * 128:(j + 1) * 128]) for j in range(4)]
                    for hp, src in srcs:
                        OP("dve", lambda e, src=src, hp=hp: e.max(out=sv[:, hp, 0:8], in_=src), [bk], [sv])
                    for hp, src in srcs:
                        OP("dve", lambda e, src=src, hp=hp: e.max_index(out=si[:, hp, 0:8], in_max=sv[:, hp, 0:8], in_values=src),
                           [bk, sv], [si])
                    for hp, src in srcs:
                        OP("dve", lambda e, src=src, hp=hp: e.match_replace(out=wk16[:, hp, :], in_to_replace=sv[:, hp, 0:8],
                                                                            in_values=src, imm_value=-1e30), [bk, sv], [wk16])
                    for hp, src in srcs:
                        OP("dve", lambda e, hp=hp: e.max(out=sv[:, hp, 8:16], in_=wk16[:, hp, :]), [wk16], [sv])
                    for hp, src in srcs:
                        OP("dve", lambda e, hp=hp: e.max_index(out=si[:, hp, 8:16], in_max=sv[:, hp, 8:16], in_values=wk16[:, hp, :]),
                           [wk16, sv], [si])
                cp("dve", sif[:], si[:], [si], [sif])
                sv4 = sv[:].rearrange("p (h q) k -> p h q k", q=2)
                sif4 = sif[:].rearrange("p (h q) k -> p h q k", q=2)
                tt_("dve", cand[:], sv4[:, :, 0, :].unsqueeze(3).broadcast_to([128, 8, 16, 16]),
                    sv4[:, :, 1, :].unsqueeze(2).broadcast_to([128, 8, 16, 16]), ALU.add, [sv], [cand])
                c2s = [cand[:, h, :, :].rearrange("p a b -> p (a b)") for h in range(8)]
                for h in range(8):
                    OP("dve", lambda e, c2=c2s[h], h=h: e.max(out=cvv[:, h, 0:8], in_=c2), [cand], [cvv])
                for h in range(8):
                    OP("dve", lambda e, c2=c2s[h], h=h: e.max_index(out=ci[:, h, 0:8], in_max=cvv[:, h, 0:8], in_values=c2),
                       [cand, cvv], [ci])
                for h in range(8):
                    OP("dve", lambda e, c2=c2s[h], h=h: e.match_replace(out=wk8[:, h, :], in_to_replace=cvv[:, h, 0:8], in_values=c2,
                                                                         imm_value=-1e30), [cand, cvv], [wk8])
                for h in range(8):
                    OP("dve", lambda e, h=h: e.max(out=cvv[:, h, 8:16], in_=wk8[:, h, :]), [wk8], [cvv])
                for h in range(8):
                    OP("dve", lambda e, h=h: e.max_index(out=ci[:, h, 8:16], in_max=cvv[:, h, 8:16], in_values=wk8[:, h, :]),
                       [wk8, cvv], [ci])
                OP("dve", lambda e: e.tensor_single_scalar(out=cij[:, 0, :, :], in_=ci[:], scalar=4, op=ALU.logical_shift_right),
                   [ci], [cij])
                OP("dve", lambda e: e.tensor_single_scalar(out=cij[:, 1, :, :], in_=ci[:], scalar=15, op=ALU.bitwise_and),
                   [ci], [cij])
                cp("dve", cijf[:], cij[:], [cij], [cijf])
                for q in range(2):
                    tt_("dve", oh[:], cijf[:, q, :, :].unsqueeze(3).broadcast_to([128, 8, 16, 16]),
                        iota16[:].unsqueeze(1).unsqueeze(1).broadcast_to([128, 8, 16, 16]), ALU.is_equal, [cijf, iota16], [oh])
                    tt_("dve", oh[:], oh[:], sif4[:, :, q, :].unsqueeze(2).broadcast_to([128, 8, 16, 16]), ALU.mult,
                        [oh, sif], [oh])
                    OP("dve", lambda e, q=q: e.tensor_reduce(out=k01[:, q, :, :], in_=oh[:], axis=AX.X, op=ALU.add), [oh], [k01])
                stt(k01[:, 0, :, :], k01[:, 0, :, :], 128.0, k01[:, 1, :, :], ALU.mult, ALU.add, [k01], [k01])
                cp("dve", e_idx[:, t, :].rearrange("p (h k) -> p h k", k=16), k01[:, 0, :, :], [k01], [e_idx])
                tt_("dve", cvv[:], cvv[:], cvv[:, :, 0:1].broadcast_to([128, 8, 16]), ALU.subtract, [cvv], [cvv])
                act(cvv[:], cvv[:], AF.Exp, [cvv], [cvv])
                OP("dve", lambda e: e.tensor_reduce(out=g8p[:, 0, :], in_=cvv[:], axis=AX.X, op=ALU.add), [cvv], [g8p])
                OP("dve", lambda e: e.reciprocal(out=g8p[:, 1, :], in_=g8p[:, 0, :]), [g8p], [g8p])
                tt_("dve", g_all[:, t, :].rearrange("p (h k) -> p h k", k=16), cvv[:],
                    g8p[:, 1, :].unsqueeze(2).broadcast_to([128, 8, 16]), ALU.mult, [cvv, g8p], [g_all])
        if l == 0:
            tap("e_idx", e_idx, e_idx[:], [128, NT, 128], I32)
            tap("g_all", g_all, g_all[:], [128, NT, 128])

        chk(11)
        P.barrier()
        ar["off"] = peer_off
        pump(128)
        NG = 11
        gb_ = [A("gbuf%d" % i, [128, 2 * D], BF16) for i in range(NG)]
        junks = [A("junk%d" % i, [128, D], BF16) for i in range(2)]
        dgs = [A("dg%d" % i, [128, 2, 128], BF16) for i in range(4)]
        asg = [A("asg%d" % i, [128, 2], F32) for i in range(8)]
        awg = [A("awg%d" % i, [128, 2], F32) for i in range(8)]
        pre = A("pre2", [128, D], F32)
        DMA("sp", lnp[:, 0, :], I["lnp"][l, 2], writes=[lnp])
        DMA("sp", lnp[:, 1, :], I["lnp"][l, 3], writes=[lnp])

        def finish_v(t, a0, a1):
            for hf, a_ in ((0, a0), (1, a1)):
                tt_("dve", pre[:, hf * 512:(hf + 1) * 512], a_[:], mrep[:, 5120 + hf * 512:5120 + (hf + 1) * 512], ALU.mult,
                    [a_, mrep], [pre])
            stt(pre[:], x[:, t, :], DN_ALPHA, pre[:], ALU.mult, ALU.add, [x, pre], [pre])
            mean, rstd = ln_stats(pre[:], pre, t)
            ts_("dve", pre[:], pre[:], mean, rstd, ALU.subtract, ALU.mult, [pre, smalls[t]], [pre])
            tt_("dve", pre[:], pre[:], lnp[:, 0, :], ALU.mult, [pre, lnp], [pre])
            tt_("dve", x[:, t, :], pre[:], lnp[:, 1, :], ALU.add, [pre, lnp], [x])

        NSTEP = NT * 64
        accs = {}

        def st_A(i):
            t, q = divmod(i, 64)
            as_ = asg[i % 8]
            for j in range(2):
                s_ = q * 2 + j
                gb = gb_[(2 * i + j) % NG]
                P.dma("pool", lambda e, gb=gb, s_=s_, t=t: e.indirect_dma_start(
                    out=gb[:], out_offset=None, in_=uvb[l],
                    in_offset=bass.IndirectOffsetOnAxis(ap=e_idx[:, t, s_:s_ + 1], axis=0)),
                    [e_idx.r, uvbuf[l].r], [gb.r])
            for j in range(2):
                jk = junks[j]
                gb = gb_[(2 * i + j) % NG]
                stt(jk[:], gb[:, 0:D], 1.0, h2tok[:, t, :], ALU.mult, ALU.mult, [gb, h2tok], [jk, as_],
                    accum_out=as_[:, j:j + 1])

        def st_B(i):
            act(awg[i % 8][:], asg[i % 8][:], AF.Gelu, [asg[i % 8]], [awg[i % 8]])

        def st_CDE(i):
            t, q = divmod(i, 64)
            aw_, dg = awg[i % 8], dgs[i % 4]
            if q == 0:
                accs[t] = (hbank(), hbank())
            a0, a1 = accs[t]
            tt_("dve", aw_[:], aw_[:], g_all[:, t, q * 2:q * 2 + 2], ALU.mult, [aw_, g_all], [aw_])
            for j in range(2):
                act(dg[:, j, :], ident[:], AF.Identity, [ident, aw_], [dg], scale=aw_[:, j:j + 1])
            for j in range(2):
                s_ = q * 2 + j
                gb = gb_[(2 * i + j) % NG]
                mm(a0[:], dg[:, j, :], gb[:, D:D + 512], s_ == 0, s_ == 127, [dg, gb], [a0])
                mm(a1[:], dg[:, j, :], gb[:, D + 512:2 * D], s_ == 0, s_ == 127, [dg, gb], [a1])
            if q == 63:
                finish_v(t, a0, a1)

        for i in range(NSTEP + 2):
            if i < NSTEP:
                st_A(i)
            if 0 <= i - 1 < NSTEP:
                st_B(i - 1)
            if 0 <= i - 2 < NSTEP:
                st_CDE(i - 2)
        if l == 0:
            tap("x2", x, x[:], [128, NT, D])

    try:
        for l in range(NL):
            layer_body(l)
    except _Stop:
        pass

    P.barrier()
    for t in range(NT):
        DMA("sp", O["y"][t * 128:(t + 1) * 128, :], x[:, t, :], reads=[x], is_output=True)
    P.finish()
    return tap_out


def _na_tables():
    rows, kr = 16, 8
    r = np.arange(rows)
    start = np.clip(r - kr // 2, 0, rows - kr)
    rowvalid = np.zeros((rows, rows), bool)
    for i in range(rows):
        rowvalid[i, start[i]:start[i] + kr] = True
    qc = np.arange(64)
    qstart = np.clip(qc - 8, 0, 48)
    kc = np.arange(64)
    colvalid = (kc[None, :] >= qstart[:, None]) & (kc[None, :] < qstart[:, None] + 16)
    return rowvalid, colvalid


def host_inputs(inp, NL=DEPTH):
    f = np.float32
    rowvalid, colvalid = _na_tables()
    com = {}
    com["w_mod"] = np.ascontiguousarray(inp["w_mod"][:NL])
    com["b_mod"] = np.ascontiguousarray(inp["b_mod"][:NL].reshape(NL, 1, 6144))
    com["w_in"] = np.ascontiguousarray(inp["w_in"][:NL])
    cw = inp["conv_w"][:NL]
    com["cw"] = np.ascontiguousarray(cw.reshape(NL, 5, 6, 128).transpose(0, 3, 2, 1))
    com["cb"] = np.ascontiguousarray(inp["conv_b"][:NL].reshape(NL, 6, 128).transpose(0, 2, 1))
    com["alog"] = np.ascontiguousarray(np.broadcast_to(inp["ssd_a_log"][:NL].reshape(NL, 1, 16), (NL, 128, 16)))
    com["dtb"] = np.ascontiguousarray(np.broadcast_to(inp["ssd_dt_bias"][:NL].reshape(NL, 1, 16), (NL, 128, 16)))
    dsk = np.zeros((NL, 128, 4), f)
    for h in range(8):
        dsk[:, (h % 2) * 64:(h % 2) * 64 + 64, h // 2] = inp["ssd_d"][:NL, h][:, None]
    com["dsk"] = dsk
    com["ng"] = np.ascontiguousarray(inp["ssd_norm_g"][:NL].reshape(NL, 4, 128).transpose(0, 2, 1))
    com["qn"] = np.ascontiguousarray(np.broadcast_to(inp["gqa_q_norm"][:NL][:, None, :], (NL, 128, 64)))
    com["kn"] = np.ascontiguousarray(np.broadcast_to(inp["gqa_k_norm"][:NL][:, None, :], (NL, 128, 64)))
    lnp = np.stack([inp["ln1_g"][:NL], inp["ln1_b"][:NL], inp["ln2_g"][:NL], inp["ln2_b"][:NL]], 1)
    com["lnp"] = np.ascontiguousarray(np.broadcast_to(lnp[:, :, None, :], (NL, 4, 128, D)))
    com["w_branch"] = np.ascontiguousarray(inp["w_branch"][:NL])
    com["w_out"] = np.ascontiguousarray(inp["w_out"][:NL])
    com["wq"] = np.ascontiguousarray(inp["peer_wq"][:NL])
    com["keysT"] = np.ascontiguousarray(inp["peer_keys"][:NL].reshape(NL, 16, 128, 128).transpose(0, 1, 3, 2))
    for i in range(NL):
        com["uv%d" % i] = np.ascontiguousarray(np.concatenate([inp["peer_u"][i], inp["peer_v"][i]], axis=1))
    t = np.arange(1024)
    qaug = (t[None, :] // 64 == np.arange(16)[:, None]).astype(f)
    com["qaug"] = qaug
    rpb = inp["na_rpb"][:NL]
    tts = np.zeros((NL, 8, 2, 64, 30, 64), f)
    kc = np.arange(64)[:, None]
    qc = np.arange(64)[None, :]
    dc = np.clip(kc - qc + 15, 0, 30)
    band = colvalid.T
    for p2 in range(2):
        for e in range(30):
            dr = p2 - (e - 14)
            if -7 <= dr <= 7:
                tile = rpb[:, :, dr + 7, :][:, :, dc]
                tts[:, :, p2, :, e, :] = np.where(band[None, None], tile, f(NEG))
    tts = tts.reshape(NL, 8, 128, 30 * 64)
    ttp = np.zeros_like(tts)

    half = 32
    inv = 1.0 / (10000.0 ** (np.arange(0, half, 2, dtype=np.float32) / half))

    maps = []
    for core in range(8):
        m = dict(com)
        sample = core >= 4
        if sample:
            b = core - 4
            m["x0"] = np.ascontiguousarray(inp["x_sample"][b])
            cond = inp["c"][b]
            m["c_nakT"] = np.ascontiguousarray(inp["cache_na_k"][b, :NL].transpose(0, 2, 3, 1))
            m["c_nav"] = np.ascontiguousarray(inp["cache_na_v"][b, :NL].reshape(NL, 256, 512))
            m["c_gkT"] = np.ascontiguousarray(inp["cache_gqa_k"][b, :NL].transpose(0, 2, 3, 1))
            m["c_gv"] = np.ascontiguousarray(inp["cache_gqa_v"][b, :NL].reshape(NL, 256, 128))
            st = inp["state_ssd"][b, :NL]
            st = st.reshape(NL, 2, 2, 4, 64, 64).transpose(0, 1, 2, 5, 3, 4)
            m["h0"] = np.ascontiguousarray(st.reshape(NL, 2, 128, 256))
            m["tt"] = tts
            rm_na = np.where(rowvalid.T, 0.0, NEG * 8).astype(f)
            rm_g = np.zeros((16, 16), f)
            ctxv = 0.0
            pos_r = (t // 64).astype(f)
            pos_c = (t % 64).astype(f)
            cos = np.ones((1024, 64), f)
            sin = np.zeros((1024, 64), f)
            for a, pos in enumerate((pos_r, pos_c)):
                ang = pos[:, None] * inv[None, :]
                cos[:, a * 32:a * 32 + 16] = np.cos(ang)
                cos[:, a * 32 + 16:a * 32 + 32] = np.cos(ang)
                sin[:, a * 32:a * 32 + 16] = -np.sin(ang)
                sin[:, a * 32 + 16:a * 32 + 32] = np.sin(ang)
            keep = np.ones((128, 16), f)
            cm = np.ones((128, 1), f)
        else:
            m["x0"] = np.ascontiguousarray(inp["x_prompt"][core * 4:(core + 1) * 4].reshape(1024, D))
            cond = inp["c_ctx"]
            m["c_nakT"] = np.zeros((NL, 8, 64, 256), f)
            m["c_nav"] = np.zeros((NL, 256, 512), f)
            m["c_gkT"] = np.zeros((NL, 2, 64, 256), f)
            m["c_gv"] = np.zeros((NL, 256, 128), f)
            m["h0"] = np.zeros((NL, 2, 128, 256), f)
            m["tt"] = ttp
            seq = np.arange(16) // 4
            rm_na = np.where(seq[:, None] == seq[None, :], 0.0, NEG * 8).astype(f)
            rm_g = rm_na
            ctxv = NEG * 8
            cos = np.ones((1024, 64), f)
            sin = np.zeros((1024, 64), f)
            keep = np.ones((128, 16), f)
            keep[:, [0, 2, 4, 6]] = 0.0
            keep[:, [8 + 1, 8 + 3, 8 + 5, 8 + 7]] = 0.0
            cm = np.zeros((128, 1), f)
        m["cond_rep"] = np.ascontiguousarray(np.broadcast_to(cond.reshape(8, 128).T[:, :, None], (128, 8, 128)))
        for nm, rm in (("kaug_na", rm_na), ("kaug_g", rm_g)):
            ka = np.zeros((16, 1280), f)
            ka[:, :1024] = rm[t // 64, :].T
            ka[:, 1024:] = ctxv
            m[nm] = ka
        m["cosT"] = np.ascontiguousarray(cos.reshape(8, 128, 64).transpose(1, 0, 2))
        m["sinT"] = np.ascontiguousarray(sin.reshape(8, 128, 64).transpose(1, 0, 2))
        m["keep"] = keep
        m["cmask"] = cm
        maps.append({k: np.ascontiguousarray(v, dtype=np.float32) for k, v in m.items()})
    return maps


def assemble(results, NL=DEPTH):
    f = np.float32
    y_p = np.concatenate([results[c]["y"].reshape(4, 256, D) for c in range(4)], 0)
    y_s = np.stack([results[c]["y"] for c in range(4, 8)], 0)

    def cache(name, nh):
        parts = []
        for c in range(4):
            a = results[c][name].reshape(NL, 4, 256, nh, 64).transpose(1, 0, 2, 3, 4)
            parts.append(a)
        return np.ascontiguousarray(np.concatenate(parts, 0), dtype=f)

    nak, nav, gk, gv = cache("o_nak", 8), cache("o_nav", 8), cache("o_gk", 2), cache("o_gv", 2)
    sts = []
    for c in range(4):
        a = results[c]["o_st"].reshape(NL, 4, 2, 2, 64, 4, 64)
        a = a.transpose(1, 0, 2, 3, 5, 6, 4).reshape(4, NL, 2, 8, 64, 64)
        sts.append(a)
    st = np.ascontiguousarray(np.concatenate(sts, 0), dtype=f)
    return (y_p.astype(f), y_s.astype(f), nak, nav, gk, gv, st)


def kernel(**inputs):
    inp = {k: np.asarray(v) for k, v in inputs.items()}
    nc = bass.Bass("TRN2", target_bir_lowering=False)
    build(nc)
    maps = host_inputs(inp)
    res = run_bass_kernel_spmd(nc, maps, core_ids=list(range(8)))
    return assemble(res.results)
```
